# Optimizing a Trainium2 kernel written in Bass

```python
import math
import jax, jax.numpy as jnp
from jax import lax
import numpy as np

D_MODEL = 1024
BATCH = 16
SEQ = 2048
DEPTH = 4

MIX_WIDTH = D_MODEL
GROUP_WIDTH = MIX_WIDTH // 4
MIX_HEAD_DIM = 64
MIX_HEADS = MIX_WIDTH // MIX_HEAD_DIM

SC_WIDTH = GROUP_WIDTH
SC_KERNEL = 3

MLA_HEADS = 4
MLA_NOPE = 64
MLA_ROPE = 32
MLA_V = GROUP_WIDTH // MLA_HEADS
MLA_Q_RANK = 256
MLA_KV_RANK = 128
ROPE_THETA = 10000.0
Q_BLOCK = 128

CF_WIDTH = GROUP_WIDTH
CF_KERNEL = 31

SWA_Q_HEADS = 4
SWA_KV_HEADS = 2
SWA_HEAD_DIM = GROUP_WIDTH // SWA_Q_HEADS
WINDOW = 128

PEER_HEADS = 8
PEER_NKEYS = 128
PEER_EXPERTS = PEER_NKEYS * PEER_NKEYS
PEER_QDIM = 256
PEER_HALF = PEER_QDIM // 2
PEER_TOPK = 16
PEER_CHUNK = 128

ALPHA = (2 * DEPTH) ** 0.25
BETA = (8 * DEPTH) ** -0.25
NORM_EPS = 1e-5

SC_COLS = 3 * SC_WIDTH
MLA_COLS = MLA_Q_RANK + MLA_KV_RANK + MLA_ROPE
CF_COLS = 2 * CF_WIDTH
SWA_COLS = (SWA_Q_HEADS + 2 * SWA_KV_HEADS) * SWA_HEAD_DIM
IN_COLS = SC_COLS + MLA_COLS + CF_COLS + SWA_COLS

kernel_name = "hybrid_parallel_groups_peer_deepnorm"


def _split_points(sizes):
    pts, acc = [], 0
    for s in sizes[:-1]:
        acc += s
        pts.append(acc)
    return pts


def layer_norm(x, g, b):
    xf = x.astype(jnp.float32)
    mu = jnp.mean(xf, axis=-1, keepdims=True)
    var = jnp.mean(jnp.square(xf - mu), axis=-1, keepdims=True)
    y = (xf - mu) * lax.rsqrt(var + NORM_EPS)
    return (y * g.astype(jnp.float32) + b.astype(jnp.float32)).astype(x.dtype)


def rms_norm(x, g):
    xf = x.astype(jnp.float32)
    y = xf * lax.rsqrt(jnp.mean(xf * xf, axis=-1, keepdims=True) + NORM_EPS)
    return (y * g.astype(jnp.float32)).astype(x.dtype)


def causal_depthwise_conv(x, w):
    k = w.shape[0]
    xp = jnp.pad(x, ((0, 0), (k - 1, 0), (0, 0)))
    return lax.conv_general_dilated(
        xp, w[:, None, :], window_strides=(1,), padding='VALID',
        dimension_numbers=('NWC', 'WIO', 'NWC'), feature_group_count=x.shape[-1])


def rope(x, positions):
    d = x.shape[-1]
    half = d // 2
    inv = ROPE_THETA ** (-jnp.arange(half, dtype=jnp.float32) / half)
    ang = positions.astype(jnp.float32)[..., None] * inv
    ang = ang.reshape(ang.shape[:2] + (1,) * (x.ndim - 3) + (half,))
    cos, sin = jnp.cos(ang), jnp.sin(ang)
    xf = x.astype(jnp.float32)
    x1, x2 = xf[..., :half], xf[..., half:]
    return jnp.concatenate([x1 * cos - x2 * sin, x2 * cos + x1 * sin], axis=-1).astype(x.dtype)


def alibi_slopes(n):
    return 2.0 ** (-8.0 * jnp.arange(1, n + 1, dtype=jnp.float32) / n)


def short_conv_mixer(gb, gc, h, conv_w):
    return gb * causal_depthwise_conv(gc * h, conv_w)


def mla_attention(c_q, c_kv, k_rope_in, positions, q_norm_g, kv_norm_g, w_uq, w_uk, w_uv):
    b, s, _ = c_q.shape
    q = (rms_norm(c_q, q_norm_g) @ w_uq).reshape(b, s, MLA_HEADS, MLA_NOPE + MLA_ROPE)
    q_nope = q[..., :MLA_NOPE]
    q_pe = rope(q[..., MLA_NOPE:], positions)
    ckv = rms_norm(c_kv, kv_norm_g)
    k_nope = (ckv @ w_uk).reshape(b, s, MLA_HEADS, MLA_NOPE)
    v = (ckv @ w_uv).reshape(b, s, MLA_HEADS, MLA_V)
    k_pe = rope(k_rope_in, positions)
    nq = s // Q_BLOCK
    qn = q_nope.reshape(b, nq, Q_BLOCK, MLA_HEADS, MLA_NOPE).swapaxes(0, 1)
    qp = q_pe.reshape(b, nq, Q_BLOCK, MLA_HEADS, MLA_ROPE).swapaxes(0, 1)
    key_idx = jnp.arange(s)
    scale = (MLA_NOPE + MLA_ROPE) ** -0.5

    def one_block(args):
        blk, qn_b, qp_b = args
        sc = (jnp.einsum('bqhd,bkhd->bhqk', qn_b, k_nope)
              + jnp.einsum('bqhd,bkd->bhqk', qp_b, k_pe)).astype(jnp.float32) * scale
        q_idx = blk * Q_BLOCK + jnp.arange(Q_BLOCK)
        causal = key_idx[None, :] <= q_idx[:, None]
        sc = jnp.where(causal, sc, -jnp.inf)
        p = jax.nn.softmax(sc, axis=-1).astype(v.dtype)
        return jnp.einsum('bhqk,bkhd->bqhd', p, v)

    out = lax.map(one_block, (jnp.arange(nq), qn, qp))
    return out.swapaxes(0, 1).reshape(b, s, MLA_HEADS * MLA_V)


def conformer_conv(a, gate, dw_w, dw_b, ln_g, ln_b):
    u = a * jax.nn.sigmoid(gate)
    u = causal_depthwise_conv(u, dw_w) + dw_b
    u = layer_norm(u, ln_g, ln_b)
    return jax.nn.silu(u)


def swa_sink_attention(q, k, v, sinks, slopes):
    b, s, _ = q.shape
    g = SWA_Q_HEADS // SWA_KV_HEADS
    nb = s // WINDOW
    qb = q.reshape(b, nb, WINDOW, SWA_KV_HEADS, g, SWA_HEAD_DIM)
    kb = k.reshape(b, nb, WINDOW, SWA_KV_HEADS, SWA_HEAD_DIM)
    vb = v.reshape(b, nb, WINDOW, SWA_KV_HEADS, SWA_HEAD_DIM)

    def with_prev(t):
        prev = jnp.pad(t, ((0, 0), (1, 0), (0, 0), (0, 0), (0, 0)))[:, :-1]
        return jnp.concatenate([prev, t], axis=2)

    kk, vv = with_prev(kb), with_prev(vb)
    sc = jnp.einsum('bnqkgd,bnskd->bnkgqs', qb, kk).astype(jnp.float32) * (SWA_HEAD_DIM ** -0.5)
    qi = jnp.arange(WINDOW)[:, None]
    kj = jnp.arange(2 * WINDOW)[None, :]
    dist = qi + WINDOW - kj
    blk = jnp.arange(nb)[:, None, None]
    valid = ((dist >= 0) & (dist < WINDOW))[None] & ((blk > 0) | (kj[None] >= WINDOW))
    sc = sc - slopes.reshape(SWA_KV_HEADS, g, 1, 1) * dist.astype(jnp.float32)
    sc = jnp.where(valid[None, :, None, None], sc, -jnp.inf)
    sink = jnp.broadcast_to(sinks.astype(jnp.float32).reshape(SWA_KV_HEADS, g, 1, 1), sc.shape[:-1] + (1,))
    p = jax.nn.softmax(jnp.concatenate([sc, sink], axis=-1), axis=-1)[..., :-1].astype(v.dtype)
    o = jnp.einsum('bnkgqs,bnskd->bnqkgd', p, vv)
    return o.reshape(b, s, SWA_Q_HEADS * SWA_HEAD_DIM)


def peer_ffn(x, w_q, sub_keys, u_tab, v_tab):
    b, s, d = x.shape
    xt = x.reshape((b * s) // PEER_CHUNK, PEER_CHUNK, d)

    def chunk(xc):
        c = xc.shape[0]
        q = (xc @ w_q).reshape(c, PEER_HEADS, 2, PEER_HALF)
        sc = jnp.einsum('chpd,hpnd->chpn', q, sub_keys).astype(jnp.float32)
        top_s, top_i = lax.top_k(sc, PEER_TOPK)
        cand_s = top_s[:, :, 0, :, None] + top_s[:, :, 1, None, :]
        cand_i = top_i[:, :, 0, :, None] * PEER_NKEYS + top_i[:, :, 1, None, :]
        best_s, best_j = lax.top_k(cand_s.reshape(c, PEER_HEADS, PEER_TOPK * PEER_TOPK), PEER_TOPK)
        idx = jnp.take_along_axis(cand_i.reshape(c, PEER_HEADS, PEER_TOPK * PEER_TOPK), best_j, axis=-1)
        gate = jax.nn.softmax(best_s, axis=-1).astype(xc.dtype)
        u = u_tab[idx]
        act = jax.nn.gelu(jnp.einsum('chkd,cd->chk', u, xc), approximate=False)
        vsel = v_tab[idx]
        return jnp.einsum('chk,chkd->cd', gate * act, vsel)

    return lax.map(chunk, xt).reshape(b, s, d)


def setup_inputs(seed: int = 0) -> dict:
    key = jax.random.key(seed)
    ks = jax.random.split(key, 32)
    L = DEPTH

    def nrm(k, shape, scale):
        return jax.random.normal(k, shape, jnp.float32) * scale

    x = nrm(ks[0], (BATCH, SEQ, D_MODEL), 1.0)
    offs = jax.random.randint(ks[1], (BATCH, 1), 0, 4096, dtype=jnp.int32)
    positions = (offs + jnp.arange(SEQ, dtype=jnp.int32)[None, :]).astype(jnp.int32)
    return {
        "x": x,
        "positions": positions,
        "w_in": nrm(ks[2], (L, D_MODEL, IN_COLS), D_MODEL ** -0.5),
        "sc_conv_w": nrm(ks[3], (L, SC_KERNEL, SC_WIDTH), SC_KERNEL ** -0.5),
        "mla_q_norm": 1.0 + nrm(ks[4], (L, MLA_Q_RANK), 0.02),
        "mla_kv_norm": 1.0 + nrm(ks[5], (L, MLA_KV_RANK), 0.02),
        "mla_w_uq": nrm(ks[6], (L, MLA_Q_RANK, MLA_HEADS * (MLA_NOPE + MLA_ROPE)), MLA_Q_RANK ** -0.5),
        "mla_w_uk": nrm(ks[7], (L, MLA_KV_RANK, MLA_HEADS * MLA_NOPE), MLA_KV_RANK ** -0.5),
        "mla_w_uv": nrm(ks[8], (L, MLA_KV_RANK, MLA_HEADS * MLA_V), MLA_KV_RANK ** -0.5),
        "cf_dw_w": nrm(ks[9], (L, CF_KERNEL, CF_WIDTH), CF_KERNEL ** -0.5),
        "cf_dw_b": nrm(ks[10], (L, CF_WIDTH), 0.02),
        "cf_ln_g": 1.0 + nrm(ks[11], (L, CF_WIDTH), 0.02),
        "cf_ln_b": nrm(ks[12], (L, CF_WIDTH), 0.02),
        "swa_sinks": nrm(ks[13], (L, SWA_Q_HEADS), 0.5),
        "mix_norm_g": 1.0 + nrm(ks[14], (L, MIX_WIDTH), 0.02),
        "w_out": nrm(ks[15], (L, MIX_WIDTH, D_MODEL), BETA * MIX_WIDTH ** -0.5),
        "ln1_g": 1.0 + nrm(ks[16], (L, D_MODEL), 0.02),
        "ln1_b": nrm(ks[17], (L, D_MODEL), 0.02),
        "peer_w_q": nrm(ks[18], (L, D_MODEL, PEER_HEADS * PEER_QDIM), D_MODEL ** -0.5),
        "peer_sub_keys": nrm(ks[19], (L, PEER_HEADS, 2, PEER_NKEYS, PEER_HALF), PEER_HALF ** -0.5),
        "peer_u": nrm(ks[20], (L, PEER_EXPERTS, D_MODEL), D_MODEL ** -0.5),
        "peer_v": nrm(ks[21], (L, PEER_EXPERTS, D_MODEL), BETA * PEER_HEADS ** -0.5),
        "ln2_g": 1.0 + nrm(ks[22], (L, D_MODEL), 0.02),
        "ln2_b": nrm(ks[23], (L, D_MODEL), 0.02),
    }


def reference(x, positions, w_in, sc_conv_w, mla_q_norm, mla_kv_norm, mla_w_uq, mla_w_uk, mla_w_uv,
              cf_dw_w, cf_dw_b, cf_ln_g, cf_ln_b, swa_sinks, mix_norm_g, w_out, ln1_g, ln1_b,
              peer_w_q, peer_sub_keys, peer_u, peer_v, ln2_g, ln2_b):
    b, s, _ = x.shape
    slopes = alibi_slopes(SWA_Q_HEADS)
    split_sizes = [SC_WIDTH, SC_WIDTH, SC_WIDTH,
                   MLA_Q_RANK, MLA_KV_RANK, MLA_ROPE,
                   CF_WIDTH, CF_WIDTH,
                   SWA_Q_HEADS * SWA_HEAD_DIM, SWA_KV_HEADS * SWA_HEAD_DIM, SWA_KV_HEADS * SWA_HEAD_DIM]
    pts = _split_points(split_sizes)
    for l in range(DEPTH):
        proj = x @ w_in[l]
        (sc_b, sc_c, sc_h, m_cq, m_ckv, m_kr, cf_a, cf_g, sw_q, sw_k, sw_v) = jnp.split(proj, pts, axis=-1)
        y_sc = short_conv_mixer(sc_b, sc_c, sc_h, sc_conv_w[l])
        y_mla = mla_attention(m_cq, m_ckv, m_kr, positions, mla_q_norm[l], mla_kv_norm[l],
                              mla_w_uq[l], mla_w_uk[l], mla_w_uv[l])
        y_cf = conformer_conv(cf_a, cf_g, cf_dw_w[l], cf_dw_b[l], cf_ln_g[l], cf_ln_b[l])
        y_sw = swa_sink_attention(sw_q, sw_k, sw_v, swa_sinks[l], slopes)
        mix = jnp.concatenate([y_sc, y_mla, y_cf, y_sw], axis=-1)
        mix = rms_norm(mix.reshape(b, s, MIX_HEADS, MIX_HEAD_DIM),
                       mix_norm_g[l].reshape(MIX_HEADS, MIX_HEAD_DIM)).reshape(b, s, MIX_WIDTH)
        h = layer_norm(ALPHA * x + mix @ w_out[l], ln1_g[l], ln1_b[l])
        x = layer_norm(ALPHA * h + peer_ffn(h, peer_w_q[l], peer_sub_keys[l], peer_u[l], peer_v[l]),
                       ln2_g[l], ln2_b[l])
    return x
```

```python
import numpy as np
from contextlib import ExitStack
import concourse.bass as bass
import concourse.mybir as mybir
from concourse.bass_utils import run_bass_kernel_spmd

F32 = mybir.dt.float32
BF16 = mybir.dt.bfloat16
I32 = mybir.dt.int32
ALU = mybir.AluOpType
AF = mybir.ActivationFunctionType
AX = mybir.AxisListType

SEM_EPOCH = 1 << 30


class Buf:
    def __init__(self, name, handle, kind):
        self.name = name
        self.ap = handle
        self.kind = kind
        self.w = {}
        self.r = {}
        self.dsem = None
        self.dcnt = 0
        self.dlast = None


class Sched:
    def __init__(self, nc, es):
        self.nc = nc
        self.es = es
        self.es0 = es
        self.eng = {'pe': nc.tensor, 'act': nc.scalar, 'dve': nc.vector, 'pool': nc.gpsimd, 'sp': nc.sync}
        self.sem = {}
        self.cnt = {}
        self.known = {k: {} for k in self.eng}
        self.nsem = 0
        for k in ('pe', 'act', 'dve', 'pool'):
            self.sem[k] = self._newsem("e_" + k)
            self.cnt[k] = 0
        self.out_tickets = []
        self.dpool = []
        self.dbufs = []
        self.ninstr = {k: 0 for k in self.eng}

    def _newsem(self, name):
        self.nsem += 1
        return self.es0.enter_context(self.nc.semaphore("%s_%d" % (name, self.nsem)))

    def sb(self, name, shape, dtype):
        self.ntens = getattr(self, 'ntens', 0) + 1
        name = "%s_%d" % (name, self.ntens)
        h = self.es.enter_context(self.nc.sbuf_tensor(name, shape, dtype))
        return Buf(name, h, 'sb')

    def ps(self, name, shape, dtype):
        h = self.es.enter_context(self.nc.psum_tensor(name, shape, dtype))
        return Buf(name, h, 'ps')

    def dram(self, name):
        return Buf(name, None, 'dram')

    def _wait(self, ek, deps):
        e = self.eng[ek]
        kn = self.known[ek]
        for sem, val in deps.items():
            if kn.get(sem, 0) >= val:
                continue
            e.wait_ge(sem, val)
            self.ninstr[ek] += 1
            kn[sem] = val

    @staticmethod
    def _merge(d, sem, val):
        if d.get(sem, 0) < val:
            d[sem] = val

    def op(self, ek, fn, r=(), w=()):
        if ek != 'pe':
            w = list(w) + [b for b in r if b.kind == 'ps']
            r = [b for b in r if b.kind != 'ps']
        deps = {}
        own = self.sem[ek]
        for b in r:
            for s, v in b.w.items():
                self._merge(deps, s, v)
        for b in w:
            for s, v in b.w.items():
                self._merge(deps, s, v)
            for s, v in b.r.items():
                self._merge(deps, s, v)
        if ek == 'pe':
            deps.pop(own, None)
        self._wait(ek, deps)
        ins = fn(self.eng[ek])
        self.ninstr[ek] += 1
        self.cnt[ek] += 1
        ins.then_inc(own, 1)
        tk = (own, self.cnt[ek])
        for b in r:
            self._merge(b.r, tk[0], tk[1])
        for b in w:
            b.w = {tk[0]: tk[1]}
            b.r = {}
        if self.cnt[ek] >= SEM_EPOCH:
            self.sem[ek] = self._newsem("e_" + ek)
            self.cnt[ek] = 0
        return ins

    def dma(self, qk, sbuf, out_ap, in_ap, reads=(), writes=(), store=False, part=False, final=False, **kw):
        deps = {}
        if store:
            for s, v in sbuf.w.items():
                self._merge(deps, s, v)
        else:
            for s, v in sbuf.w.items():
                if part and s is sbuf.dsem:
                    continue
                self._merge(deps, s, v)
            for s, v in sbuf.r.items():
                self._merge(deps, s, v)
        for b in reads:
            for s, v in b.w.items():
                self._merge(deps, s, v)
        for b in writes:
            for s, v in b.w.items():
                self._merge(deps, s, v)
            for s, v in b.r.items():
                self._merge(deps, s, v)
        if sbuf.dsem is None:
            if self.dpool:
                sbuf.dsem, sbuf.dcnt = self.dpool.pop()
            else:
                sbuf.dsem = self._newsem("dma")
                sbuf.dcnt = 0
            sbuf.dlast = None
            self.dbufs.append(sbuf)
        elif not part and sbuf.dlast is not None:
            self._merge(deps, sbuf.dlast[0], sbuf.dlast[1])
        self._wait(qk, deps)
        ins = self.eng[qk].dma_start(out=out_ap, in_=in_ap, **kw)
        self.ninstr[qk] += 1
        sbuf.dcnt += 16
        ins.then_inc(sbuf.dsem, 16)
        tk = (sbuf.dsem, sbuf.dcnt)
        sbuf.dlast = tk
        if store:
            self._merge(sbuf.r, tk[0], tk[1])
        else:
            if part:
                self._merge(sbuf.w, tk[0], tk[1])
            else:
                sbuf.w = {tk[0]: tk[1]}
            sbuf.r = {}
        for b in reads:
            self._merge(b.r, tk[0], tk[1])
        for b in writes:
            b.w = {tk[0]: tk[1]}
            b.r = {}
        if final:
            self.out_tickets.append(tk)
        return ins

    def finish(self):
        deps = {}
        for s, v in self.out_tickets:
            self._merge(deps, s, v)
        self._wait('sp', deps)

    def barrier(self, bufs=()):
        deps = {}
        for k in ('pe', 'act', 'dve', 'pool'):
            if self.cnt[k] > 0:
                self._merge(deps, self.sem[k], self.cnt[k])
        for b in self.dbufs:
            if b.dlast is not None:
                self._merge(deps, b.dlast[0], b.dlast[1])
        for k in ('pe', 'act', 'dve', 'pool', 'sp'):
            self._wait(k, dict(deps))
        for b in self.dbufs:
            self.dpool.append((b.dsem, b.dcnt))
            b.dsem = None
            b.dlast = None
        self.dbufs = []


D_MODEL = 1024
SEQ = 2048
NSEQ = 2
NTOK = NSEQ * SEQ
NT = NTOK // 128
DEPTH = 4
ALPHA = (2 * DEPTH) ** 0.25
EPS = 1e-5
IN_COLS = 2208
TBT = 4
NE = 128


def bcast_row(ap1d, n):
    return ap1d.partition_broadcast(128)


class PS:
    def __init__(self, S):
        self.A = [S.ps("psA0", [128, 512], F32), S.ps("psA1", [128, 512], F32)]
        self.G = [S.ps("psG0", [128, 512], F32), S.ps("psG1", [128, 512], F32)]
        self.O = S.ps("psO", [128, 1024], F32)
        self.T = S.ps("psT", [128, 1024], BF16)
        self.X = S.ps("psX", [128, 512], F32)


def layer_norm_tile(S, y, stats, aggr, rstd, gt, bt, out):
    for c in range(2):
        S.op('dve', lambda e, c=c: e.bn_stats(stats.ap[:, c, :], y.ap[:, c * 512:(c + 1) * 512]), r=[y], w=[stats])
    S.op('dve', lambda e: e.bn_aggr(aggr.ap[:], stats.ap[:]), r=[stats], w=[aggr])
    S.op('act', lambda e: e.activation(rstd.ap[:], aggr.ap[:, 1:2], AF.Sqrt, bias=EPS_AP[0].ap[:, 0:1], scale=1.0), r=[aggr], w=[rstd])
    S.op('dve', lambda e: e.reciprocal(rstd.ap[:], rstd.ap[:]), r=[rstd], w=[rstd])
    S.op('dve', lambda e: e.tensor_scalar(y.ap[:], y.ap[:], aggr.ap[:, 0:1], rstd.ap[:, 0:1], ALU.subtract, ALU.mult), r=[y, aggr, rstd], w=[y])
    S.op('dve', lambda e: e.tensor_tensor(y.ap[:], y.ap[:], gt.ap[:], ALU.mult), r=[y, gt], w=[y])
    S.op('dve', lambda e: e.tensor_tensor(out.ap[:], y.ap[:], bt.ap[:], ALU.add), r=[y, bt], w=[out])


EPS_AP = [None]


def peer_phase(S, nc, P, C, l, h_dram, h_res, out_dram, out_res, D, final, ntiles=NT):
    identb = C['identb']
    scr = D['scr']
    hT_d, a_d, b_d, dg_d = scr['hT'], scr['a'], scr['b'], scr['dg']
    hT_r, a_r, b_r, dg_r = S.dram("hT_r"), S.dram("a_r"), S.dram("b_r"), S.dram("dg_r")
    with ExitStack() as es1:
        S.es = es1
        wq = S.sb("wq", [128, 8, 2048], BF16)
        wqst = [S.sb("wqst%d" % i, [128, 2048], F32) for i in range(2)]
        KT = S.sb("KT", [128, 16, 128], BF16)
        kst = [S.sb("kst%d" % i, [128, 128], F32) for i in range(2)]
        kb = [S.sb("kb%d" % i, [128, 128], BF16) for i in range(2)]
        hst = [S.sb("hst%d" % i, [128, 1024], F32) for i in range(2)]
        hb = S.sb("hb", [128, 1024], BF16)
        hTt = [S.sb("hTt%d" % i, [128, 8, 128], BF16) for i in range(2)]
        qT = S.sb("qT", [128, 16, 128], BF16)
        sc = S.sb("sc", [128, 16, 128], F32)
        mscr = S.sb("mscr", [128, 256], F32)
        top16 = S.sb("top16", [128, 16, 16], F32)
        cand = S.sb("cand", [128, 8, 256], F32)
        best = S.sb("best", [128, 8, 16], F32)
        sm = S.sb("sm", [128, 8, 8], F32)
        e16 = S.sb("e16", [128, 8, 16], F32)
        at = [S.sb("at%d" % i, [128, 8, 128], BF16) for i in range(2)]
        bt_ = [S.sb("bt%d" % i, [128, 8, 128], BF16) for i in range(2)]
        dgt = [S.sb("dgt%d" % i, [128, 8, 128], BF16) for i in range(2)]
        allb = wqst + kst + hst + hTt + at + bt_ + dgt
        for kc in range(8):
            st = wqst[kc % 2]
            S.dma('sp', st, st.ap[:], D['peer_w_q'][l, kc * 128:(kc + 1) * 128, :])
            S.op('pool' if kc % 2 else 'act', (lambda e, kc=kc, st=st: e.tensor_copy(wq.ap[:, kc, :], st.ap[:])) if kc % 2 else
                 (lambda e, kc=kc, st=st: e.activation(wq.ap[:, kc, :], st.ap[:], AF.Copy)), r=[st], w=[wq])
        for hp in range(16):
            st = kst[hp % 2]
            S.dma('sp', st, st.ap[:], D['peer_sub_keys'][l, hp // 2, hp % 2, :, :])
            S.op('dve', lambda e, st=st, hp=hp: e.tensor_copy(kb[hp % 2].ap[:], st.ap[:]), r=[st], w=[kb[hp % 2]])
            S.op('pe', lambda e, hp=hp: e.transpose(P.T.ap[:, 0:128], kb[hp % 2].ap[:], identb.ap[:]), r=[kb[hp % 2], identb], w=[P.T])
            S.op('dve', lambda e, hp=hp: e.tensor_copy(KT.ap[:, hp, :], P.T.ap[:, 0:128]), r=[P.T], w=[KT])
        for t in range(ntiles):
            hs = hst[t % 2]
            S.dma('sp', hs, hs.ap[:], h_dram[t * 128:(t + 1) * 128, :], reads=[h_res])
            S.op('act', lambda e, hs=hs: e.activation(hb.ap[:], hs.ap[:], AF.Copy), r=[hs], w=[hb])
            for kc in range(8):
                S.op('pe', lambda e, kc=kc: e.transpose(P.T.ap[:, kc * 128:(kc + 1) * 128], hb.ap[:, kc * 128:(kc + 1) * 128], identb.ap[:]),
                     r=[hb, identb], w=[P.T])
            ht = hTt[t % 2]
            S.op('dve', lambda e, ht=ht: e.tensor_copy(ht.ap[:].rearrange("p a b -> p (a b)"), P.T.ap[:]), r=[P.T], w=[ht])
            S.dma('sp', ht, hT_d[:, :, t * 128:(t + 1) * 128], ht.ap[:], store=True, writes=[hT_r])
            banks = [P.A[0], P.A[1], P.G[0], P.G[1]]
            for g in range(4):
                pb = banks[g]
                for j in range(4):
                    hp = g * 4 + j
                    for kc in range(8):
                        S.op('pe', lambda e, pb=pb, j=j, hp=hp, kc=kc: e.matmul(
                            pb.ap[:, j * 128:(j + 1) * 128], wq.ap[:, kc, hp * 128:(hp + 1) * 128], ht.ap[:, kc, :],
                            start=(kc == 0), stop=(kc == 7)), r=[wq, ht], w=[pb])
                S.op('act', lambda e, pb=pb, g=g: e.activation(qT.ap[:, g * 4:(g + 1) * 4, :].rearrange("p a b -> p (a b)"), pb.ap[:], AF.Copy),
                     r=[pb], w=[qT])
            for half in range(2):
                for j in range(8):
                    hp = half * 8 + j
                    S.op('pe', lambda e, j=j, hp=hp: e.matmul(P.O.ap[:, j * 128:(j + 1) * 128], qT.ap[:, hp, :], KT.ap[:, hp, :],
                                                              start=True, stop=True), r=[qT, KT], w=[P.O])
                S.op('dve', lambda e, half=half: e.tensor_copy(sc.ap[:, half * 8:(half + 1) * 8, :].rearrange("p a b -> p (a b)"), P.O.ap[:]),
                     r=[P.O], w=[sc])
            for hp in range(16):
                S.op('dve', lambda e, hp=hp: e.max(top16.ap[:, hp, 0:8], sc.ap[:, hp, :]), r=[sc], w=[top16])
                S.op('dve', lambda e, hp=hp: e.match_replace(mscr.ap[:, 0:128], top16.ap[:, hp, 0:8], sc.ap[:, hp, :], -1e30),
                     r=[sc, top16], w=[mscr])
                S.op('dve', lambda e, hp=hp: e.max(top16.ap[:, hp, 8:16], mscr.ap[:, 0:128]), r=[mscr], w=[top16])
            t4 = top16.ap[:].rearrange("p (h q) k -> p h q k", q=2)
            S.op('dve', lambda e: e.tensor_tensor(cand.ap[:].rearrange("p h (a b) -> p h a b", a=16),
                                                  t4[:, :, 0, :].unsqueeze(3).broadcast_to([128, 8, 16, 16]),
                                                  t4[:, :, 1, :].unsqueeze(2).broadcast_to([128, 8, 16, 16]), ALU.add),
                 r=[top16], w=[cand])
            for h in range(8):
                S.op('dve', lambda e, h=h: e.max(best.ap[:, h, 0:8], cand.ap[:, h, :]), r=[cand], w=[best])
                S.op('dve', lambda e, h=h: e.match_replace(mscr.ap[:], best.ap[:, h, 0:8], cand.ap[:, h, :], -1e30),
                     r=[cand, best], w=[mscr])
                S.op('dve', lambda e, h=h: e.max(best.ap[:, h, 8:16], mscr.ap[:]), r=[mscr], w=[best])
            S.op('dve', lambda e: e.tensor_tensor(e16.ap[:], best.ap[:], best.ap[:, :, 0:1].broadcast_to([128, 8, 16]), ALU.subtract),
                 r=[best], w=[e16])
            S.op('act', lambda e: e.activation(e16.ap[:], e16.ap[:], AF.Exp), r=[e16], w=[e16])
            S.op('dve', lambda e: e.reduce_sum(sm.ap[:, 1, :], e16.ap[:], axis=AX.X), r=[e16], w=[sm])
            S.op('dve', lambda e: e.reciprocal(sm.ap[:, 2, :], sm.ap[:, 1, :]), r=[sm], w=[sm])
            S.op('dve', lambda e: e.tensor_tensor(sm.ap[:, 3, :], e16.ap[:, :, 15], sm.ap[:, 2, :], ALU.mult), r=[e16, sm], w=[sm])
            S.op('dve', lambda e: e.tensor_tensor(sm.ap[:, 4, :], t4[:, :, 1, 0], best.ap[:, :, 15], ALU.subtract), r=[top16, best], w=[sm])
            S.op('dve', lambda e: e.tensor_scalar(sm.ap[:, 5, :], t4[:, :, 1, 0], -1.0, None, ALU.mult), r=[top16], w=[sm])
            a_t, b_t, d_t = at[t % 2], bt_[t % 2], dgt[t % 2]
            for h in range(8):
                S.op('act', lambda e, h=h: e.activation(a_t.ap[:, h, :], sc.ap[:, 2 * h, :], AF.Exp, bias=sm.ap[:, 4, h:h + 1], scale=1.0),
                     r=[sc, sm], w=[a_t])
                S.op('act', lambda e, h=h: e.activation(b_t.ap[:, h, :], sc.ap[:, 2 * h + 1, :], AF.Exp, bias=sm.ap[:, 5, h:h + 1], scale=1.0),
                     r=[sc, sm], w=[b_t])
                S.op('dve', lambda e, h=h: e.tensor_scalar(d_t.ap[:, h, :], identb.ap[:], sm.ap[:, 3, h:h + 1], None, ALU.mult),
                     r=[identb, sm], w=[d_t])
            S.dma('sp', a_t, a_d[t], a_t.ap[:], store=True, writes=[a_r])
            S.dma('sp', b_t, b_d[t], b_t.ap[:], store=True, writes=[b_r])
            S.dma('sp', d_t, dg_d[t], d_t.ap[:], store=True, writes=[dg_r])
        S.barrier(allb)
    with ExitStack() as es2:
        S.es = es2
        aB = S.sb("aB", [128, TBT, 8, 128], BF16)
        bB = S.sb("bB", [128, TBT, 8, 128], BF16)
        dB = S.sb("dB", [128, TBT, 8, 128], BF16)
        hTB = S.sb("hTB", [128, 8, TBT * 128], BF16)
        acc = S.sb("acc", [128, TBT, 1024], F32)
        Ust = [S.sb("Ust%d" % i, [128, 1024], F32) for i in range(3)]
        Vst = [S.sb("Vst%d" % i, [128, 1024], F32) for i in range(3)]
        Ub = [S.sb("Ub%d" % i, [128, 1024], BF16) for i in range(2)]
        UT = [S.sb("UT%d" % i, [128, 8, 128], BF16) for i in range(2)]
        Vb = [S.sb("Vb%d" % i, [128, 1024], BF16) for i in range(8)]
        WT = [S.sb("WT%d" % i, [128, 4, TBT * 128], BF16) for i in range(2)]
        Pm = [S.sb("Pm%d" % i, [128, 8, 128], BF16) for i in range(3)]
        Wh = [S.sb("Wh%d" % i, [128, 8, 128], BF16) for i in range(3)]
        GA = [S.sb("GA%d" % i, [128, TBT * 128], F32) for i in range(2)]
        hs2 = [S.sb("hs2_%d" % i, [128, 1024], F32) for i in range(2)]
        yb = [S.sb("yb%d" % i, [128, 1024], F32) for i in range(2)]
        gt = S.sb("gt", [128, 1024], F32)
        bt = S.sb("bt", [128, 1024], F32)
        stats = S.sb("stats", [128, 2, 6], F32)
        aggr = S.sb("aggr", [128, 2], F32)
        rstd = S.sb("rstd", [128, 1], F32)
        allb = [aB, bB, dB, hTB, gt, bt] + Ust + Vst + hs2 + yb
        S.dma('sp', gt, gt.ap[:], D['ln2_g'][l].partition_broadcast(128))
        S.dma('sp', bt, bt.ap[:], D['ln2_b'][l].partition_broadcast(128))
        nblk = ntiles // TBT
        cnt = 0
        for blk in range(nblk):
            t0 = blk * TBT
            for (bf, dd, rr) in ((aB, a_d, a_r), (bB, b_d, b_r), (dB, dg_d, dg_r)):
                S.dma('sp', bf, bf.ap[:], dd[t0:t0 + TBT].rearrange("t p h j -> p t h j"), reads=[rr])
            S.dma('sp', hTB, hTB.ap[:], hT_d[:, :, t0 * 128:(t0 + TBT) * 128], reads=[hT_r])
            for eg in range(NE // 4):
                wt = WT[eg % 2]
                for ei in range(4):
                    e_ = eg * 4 + ei
                    us, vs = Ust[cnt % 3], Vst[cnt % 3]
                    ub, ut = Ub[cnt % 2], UT[cnt % 2]
                    vb = Vb[(eg % 2) * 4 + ei]
                    pA, pG = P.A[cnt % 2], P.G[cnt % 2]
                    ga = GA[cnt % 2]
                    cnt += 1
                    S.dma('sp', us, us.ap[:], D['peer_u'][l, e_ * 128:(e_ + 1) * 128, :])
                    S.dma('sp', vs, vs.ap[:], D['peer_v'][l, e_ * 128:(e_ + 1) * 128, :])
                    S.op('act', lambda e, ub=ub, us=us: e.activation(ub.ap[:], us.ap[:], AF.Copy), r=[us], w=[ub])
                    S.op('act', lambda e, vb=vb, vs=vs: e.activation(vb.ap[:], vs.ap[:], AF.Copy), r=[vs], w=[vb])
                    for kc in range(8):
                        S.op('pe', lambda e, kc=kc, ub=ub: e.transpose(P.T.ap[:, kc * 128:(kc + 1) * 128], ub.ap[:, kc * 128:(kc + 1) * 128], identb.ap[:]),
                             r=[ub, identb], w=[P.T])
                    S.op('act', lambda e, ut=ut: e.activation(ut.ap[:].rearrange("p a b -> p (a b)"), P.T.ap[:], AF.Copy), r=[P.T], w=[ut])
                    for kc in range(8):
                        S.op('pe', lambda e, kc=kc, ut=ut, pA=pA: e.matmul(pA.ap[:], ut.ap[:, kc, :], hTB.ap[:, kc, :], start=(kc == 0), stop=(kc == 7)),
                             r=[ut, hTB], w=[pA])
                    for tt in range(TBT):
                        pm, wh = Pm[(cnt * TBT + tt) % 3], Wh[(cnt * TBT + tt) % 3]
                        S.op('pool', lambda e, pm=pm, tt=tt, e_=e_: e.tensor_tensor(
                            pm.ap[:], aB.ap[:, tt, :, e_:e_ + 1].broadcast_to([128, 8, 128]), bB.ap[:, tt, :, :], ALU.mult), r=[aB, bB], w=[pm])
                        S.op('dve', lambda e, pm=pm, wh=wh: e.scalar_tensor_tensor(wh.ap[:], pm.ap[:], 1.0, pm.ap[:], ALU.is_ge, ALU.mult), r=[pm], w=[wh])
                        for h in range(8):
                            S.op('pe', lambda e, h=h, tt=tt, wh=wh, pG=pG: e.matmul(pG.ap[:, tt * 128:(tt + 1) * 128], wh.ap[:, h, :], dB.ap[:, tt, h, :],
                                                                                    start=(h == 0), stop=(h == 7)), r=[wh, dB], w=[pG])
                    S.op('act', lambda e, ga=ga, pA=pA: e.activation(ga.ap[:], pA.ap[:], AF.Gelu), r=[pA], w=[ga])
                    S.op('dve', lambda e, ga=ga, pG=pG, wt=wt, ei=ei: e.tensor_tensor(wt.ap[:, ei, :], pG.ap[:], ga.ap[:], ALU.mult), r=[pG, ga], w=[wt])
                for tt in range(TBT):
                    for half in range(2):
                        for ei in range(4):
                            vb = Vb[(eg % 2) * 4 + ei]
                            S.op('pe', lambda e, tt=tt, half=half, ei=ei, vb=vb, wt=wt: e.matmul(
                                P.O.ap[:, half * 512:(half + 1) * 512], wt.ap[:, ei, tt * 128:(tt + 1) * 128], vb.ap[:, half * 512:(half + 1) * 512],
                                start=(ei == 0), stop=(ei == 3)), r=[wt, vb], w=[P.O])
                    if eg == 0:
                        S.op('dve', lambda e, tt=tt: e.tensor_copy(acc.ap[:, tt, :], P.O.ap[:]), r=[P.O], w=[acc])
                    else:
                        S.op('dve', lambda e, tt=tt: e.tensor_tensor(acc.ap[:, tt, :], acc.ap[:, tt, :], P.O.ap[:], ALU.add), r=[P.O, acc], w=[acc])
            for tt in range(TBT):
                t = t0 + tt
                hs, y = hs2[tt % 2], yb[tt % 2]
                S.dma('sp', hs, hs.ap[:], h_dram[t * 128:(t + 1) * 128, :], reads=[h_res])
                S.op('dve', lambda e, hs=hs, y=y, tt=tt: e.scalar_tensor_tensor(y.ap[:], hs.ap[:], float(ALPHA), acc.ap[:, tt, :], ALU.mult, ALU.add),
                     r=[hs, acc], w=[y])
                layer_norm_tile(S, y, stats, aggr, rstd, gt, bt, y)
                S.dma('sp', y, out_dram[t * 128:(t + 1) * 128, :], y.ap[:], store=True, writes=[out_res], final=final)
        S.barrier(allb)


def make_consts(S, nc, D):
    C = {}
    idf = S.sb("idf", [128, 128], F32)
    C['identb'] = S.sb("identb", [128, 128], BF16)
    S.dma('sp', idf, idf.ap[:], D['c_ident'][:, :])
    S.op('dve', lambda e: e.tensor_copy(C['identb'].ap[:], idf.ap[:]), r=[idf], w=[C['identb']])
    C['identf'] = idf
    C['iota'] = S.sb("iota", [128, 128], F32)
    S.dma('sp', C['iota'], C['iota'].ap[:], D['c_iota'][:, :])
    eps = S.sb("eps", [128, 1], F32)
    S.op('dve', lambda e: e.memset(eps.ap[:], EPS), w=[eps])
    EPS_AP[0] = eps
    return C


def declare_scratch(nc, ntok=NTOK):
    nt = ntok // 128
    scr = {}
    scr['hT'] = nc.dram_tensor("s_hT", [128, 8, ntok], BF16).ap()
    scr['a'] = nc.dram_tensor("s_a", [nt, 128, 8, 128], BF16).ap()
    scr['b'] = nc.dram_tensor("s_b", [nt, 128, 8, 128], BF16).ap()
    scr['dg'] = nc.dram_tensor("s_dg", [nt, 128, 8, 128], BF16).ap()
    scr['G'] = nc.dram_tensor("s_G", [128, 128, ntok], BF16).ap()
    scr['UT'] = nc.dram_tensor("s_UT", [128, 128, 8, 128], BF16).ap()
    scr['Vb'] = nc.dram_tensor("s_Vb", [128, 128, 1024], BF16).ap()
    return scr


WSHAPES = {
    "w_in": (4, 1024, 2208), "sc_conv_w": (4, 3, 256), "mla_q_norm": (4, 256), "mla_kv_norm": (4, 128),
    "mla_w_uq": (4, 256, 384), "mla_w_uk": (4, 128, 256), "mla_w_uv": (4, 128, 256), "cf_dw_w": (4, 31, 256),
    "cf_dw_b": (4, 256), "cf_ln_g": (4, 256), "cf_ln_b": (4, 256), "swa_sinks": (4, 4), "mix_norm_g": (4, 1024),
    "w_out": (4, 1024, 1024), "ln1_g": (4, 1024), "ln1_b": (4, 1024), "peer_w_q": (4, 1024, 2048),
    "peer_sub_keys": (4, 8, 2, 128, 128), "peer_u": (4, 16384, 1024), "peer_v": (4, 16384, 1024),
    "ln2_g": (4, 1024), "ln2_b": (4, 1024),
}


PEER_FN = [None]


def build_peer_test(ntiles=4):
    nc = bass.Bass("TRN2", target_bir_lowering=False)
    D = {}
    D['x'] = nc.dram_tensor("x", [ntiles * 128, 1024], F32, kind="ExternalInput").ap()
    D['c_ident'] = nc.dram_tensor("c_ident", [128, 128], F32, kind="ExternalInput").ap()
    D['c_iota'] = nc.dram_tensor("c_iota", [128, 128], F32, kind="ExternalInput").ap()
    for k in ("peer_w_q", "peer_sub_keys", "peer_u", "peer_v", "ln2_g", "ln2_b"):
        D[k] = nc.dram_tensor(k, list(WSHAPES[k]), F32, kind="ExternalInput").ap()
    out = nc.dram_tensor("out", [ntiles * 128, 1024], F32, kind="ExternalOutput").ap()
    D['scr'] = declare_scratch(nc, ntiles * 128)
    with ExitStack() as es:
        S = Sched(nc, es)
        P = PS(S)
        C = make_consts(S, nc, D)
        peer_prepass(S, nc, P, C, 0, D)
        PEER_FN[0](S, nc, P, C, 0, D['x'], S.dram("xr"), out, S.dram("outr"), D, True, ntiles=ntiles)
        S.finish()
        print("instr counts", S.ninstr, "sems", S.nsem)
    return nc


CH = 256
MIXDBG = [0]
MIXSKIP = set()
C_SCB, C_SCC, C_SCH, C_CQ, C_CKV, C_KR, C_CFA, C_CFG, C_SWQ, C_SWK, C_SWV = 0, 256, 512, 768, 1024, 1152, 1184, 1440, 1696, 1952, 2080
QSCALE = 96.0 ** -0.5
TWO_PI = 2.0 * np.pi
CW1 = 6.28125
CW2 = float(TWO_PI - CW1)


def mixer_phase(S, nc, P, C, l, x_dram, x_res, h_dram, h_res, D, nseq=NSEQ):
    identb, identf = C['identb'], C['identf']
    psr = [P.A[0], P.A[1], P.G[0], P.G[1], P.X]
    ctr = [0]

    def nextps():
        ctr[0] += 1
        return psr[ctr[0] % 5]

    with ExitStack() as esm:
        S.es = esm
        usc = S.sb("usc", [128, 2, 2 + SEQ], BF16)
        ucf = S.sb("ucf", [128, 2, 30 + SEQ], BF16)
        scb = S.sb("scb", [128, 16, 256], BF16)
        swq = S.sb("swq", [128, 2, SEQ], BF16)
        swk = S.sb("swk", [128, 2, SEQ], BF16)
        swv = S.sb("swv", [128, 16, 2, 66], BF16)
        mlav = S.sb("mlav", [128, 16, 4, 66], BF16)
        q96 = [S.sb("q96_%d" % h, [96, SEQ], BF16) for h in range(4)]
        k96 = [S.sb("k96_%d" % h, [96, SEQ], BF16) for h in range(4)]
        onesb = S.sb("onesb", [128, 128], BF16)
        S.op('dve', lambda e: e.memset(onesb.ap[:], 1.0), w=[onesb])
        for s in range(nseq):
            with ExitStack() as es1:
                S.es = es1
                win = S.sb("win", [128, 8, IN_COLS], BF16)
                wst = S.sb("wst", [128, IN_COLS], F32)
                wswk = S.sb("wswk", [128, 8, 256], BF16)
                wkr = S.sb("wkr", [128, 8, 96], BF16)
                wkrr = S.sb("wkrr", [128, 8, 96], BF16)
                wuq = S.sb("wuq", [128, 2, 384], BF16)
                wuqr = S.sb("wuqr", [128, 2, 384], BF16)
                wuk = S.sb("wuk", [128, 256], BF16)
                wuv = S.sb("wuv", [128, 256], BF16)
                sst = S.sb("sst", [128, 2, 384], F32)
                gq = S.sb("gq", [128, 2], F32)
                gkv = S.sb("gkv", [128, 1], F32)
                invf = S.sb("invf", [96, 1], F32)
                xs = S.sb("xs", [128, 1024], F32)
                xb = S.sb("xb", [128, 1024], BF16)
                xTc = S.sb("xTc", [128, 8, CH], BF16)
                tA = [S.sb("tA%d" % i, [128, CH], BF16) for i in range(2)]
                cqf = S.sb("cqf", [128, 2, CH], F32)
                cqs = S.sb("cqs", [128, 2, CH], BF16)
                rq = S.sb("rq", [128, CH], F32)
                cqn = S.sb("cqn", [128, 2, CH], BF16)
                ckvf = S.sb("ckvf", [128, CH], F32)
                ckvs = S.sb("ckvs", [128, CH], BF16)
                rkv = S.sb("rkv", [128, CH], F32)
                ckvn = S.sb("ckvn", [128, CH], BF16)
                posi = S.sb("posi", [96, CH], I32)
                ang = S.sb("ang", [96, CH], F32)
                ang2 = S.sb("ang2", [96, CH], F32)
                tm = S.sb("tm", [96, CH], F32)
                ki = S.sb("ki", [96, CH], I32)
                kf = S.sb("kf", [96, CH], F32)
                cosb = S.sb("cosb", [96, CH], F32)
                sinb = S.sb("sinb", [96, CH], F32)
                t1 = S.sb("t1", [96, CH], F32)
                t2 = S.sb("t2", [96, CH], F32)
                dmab = [wst, sst, gq, gkv, invf, xs, posi]
                R = slice(64, 96)
                for kc in range(8):
                    S.dma('sp', wst, wst.ap[:], D['w_in'][l, kc * 128:(kc + 1) * 128, :])
                    if kc % 2:
                        S.op('pool', lambda e, kc=kc: e.tensor_copy(win.ap[:, kc, :], wst.ap[:]), r=[wst], w=[win])
                    else:
                        S.op('act', lambda e, kc=kc: e.activation(win.ap[:, kc, :], wst.ap[:], AF.Copy), r=[wst], w=[win])
                for j in range(2):
                    S.op('dve', lambda e, j=j: e.tensor_copy(
                        wswk.ap[:, :, j * 128:(j + 1) * 128].rearrange("p k (a b) -> p k a b", a=2),
                        win.ap[:, :, C_SWK + j * 64:C_SWK + (j + 1) * 64].unsqueeze(2).broadcast_to([128, 8, 2, 64])), r=[win], w=[wswk])
                S.op('dve', lambda e: e.memset(wkr.ap[:], 0.0), w=[wkr])
                S.op('dve', lambda e: e.memset(wkrr.ap[:], 0.0), w=[wkrr])
                S.op('dve', lambda e: e.tensor_copy(wkr.ap[:, :, 64:96], win.ap[:, :, C_KR:C_KR + 32]), r=[win], w=[wkr])
                S.op('dve', lambda e: e.tensor_scalar(wkrr.ap[:, :, 64:80], win.ap[:, :, C_KR + 16:C_KR + 32], -1.0, None, ALU.mult), r=[win], w=[wkrr])
                S.op('dve', lambda e: e.tensor_copy(wkrr.ap[:, :, 80:96], win.ap[:, :, C_KR:C_KR + 16]), r=[win], w=[wkrr])
                S.dma('sp', sst, sst.ap[:], D['mla_w_uq'][l].rearrange("(c p) n -> p c n", p=128))
                S.op('dve', lambda e: e.tensor_copy(wuq.ap[:], sst.ap[:]), r=[sst], w=[wuq])
                S.op('dve', lambda e: e.memset(wuqr.ap[:], 0.0), w=[wuqr])
                wq4 = wuq.ap[:].rearrange("p c (h d) -> p c h d", h=4)
                wr4 = wuqr.ap[:].rearrange("p c (h d) -> p c h d", h=4)
                S.op('dve', lambda e: e.tensor_scalar(wr4[:, :, :, 64:80], wq4[:, :, :, 80:96], -1.0, None, ALU.mult), r=[wuq], w=[wuqr])
                S.op('dve', lambda e: e.tensor_copy(wr4[:, :, :, 80:96], wq4[:, :, :, 64:80]), r=[wuq], w=[wuqr])
                S.dma('sp', sst, sst.ap[:, 0, 0:256], D['mla_w_uk'][l])
                S.op('dve', lambda e: e.tensor_copy(wuk.ap[:], sst.ap[:, 0, 0:256]), r=[sst], w=[wuk])
                S.dma('sp', sst, sst.ap[:, 0, 0:256], D['mla_w_uv'][l])
                S.op('dve', lambda e: e.tensor_copy(wuv.ap[:], sst.ap[:, 0, 0:256]), r=[sst], w=[wuv])
                S.dma('sp', gq, gq.ap[:], D['mla_q_norm'][l].rearrange("(c p) -> p c", p=128), allow_slow_non_contiguous=True)
                S.dma('sp', gkv, gkv.ap[:], D['mla_kv_norm'][l].rearrange("(p o) -> p o", o=1))
                S.dma('sp', invf, invf.ap[:], D['c_invf'][:, :])
                S.op('dve', lambda e: e.memset(usc.ap[:, :, 0:2], 0.0), w=[usc])
                S.op('dve', lambda e: e.memset(ucf.ap[:, :, 0:30], 0.0), w=[ucf])
                S.op('dve', lambda e: e.memset(swv.ap[:, :, :, 64:65], 1.0), w=[swv])
                S.op('dve', lambda e: e.memset(mlav.ap[:, :, :, 64:65], 1.0), w=[mlav])

                def proj(lhs, M, nk, rhs, rbufs):
                    pb = nextps()
                    for kc in range(nk):
                        S.op('pe', lambda e, kc=kc: e.matmul(pb.ap[0:M, 0:CH], lhs(kc), rhs(kc), start=(kc == 0), stop=(kc == nk - 1)),
                             r=rbufs, w=[pb])
                    return pb

                def range_sin(src, dst):
                    S.op('dve', lambda e: e.tensor_scalar(tm.ap[R, :], src.ap[R, :], 1.0 / TWO_PI, None, ALU.mult), r=[src], w=[tm])
                    S.op('dve', lambda e: e.tensor_copy(ki.ap[R, :], tm.ap[R, :]), r=[tm], w=[ki])
                    S.op('dve', lambda e: e.tensor_copy(kf.ap[R, :], ki.ap[R, :]), r=[ki], w=[kf])
                    S.op('dve', lambda e: e.scalar_tensor_tensor(tm.ap[R, :], kf.ap[R, :], -CW1, src.ap[R, :], ALU.mult, ALU.add), r=[kf, src], w=[tm])
                    S.op('dve', lambda e: e.scalar_tensor_tensor(tm.ap[R, :], kf.ap[R, :], -CW2, tm.ap[R, :], ALU.mult, ALU.add), r=[kf, tm], w=[tm])
                    S.op('dve', lambda e: e.tensor_scalar(kf.ap[R, :], tm.ap[R, :], float(np.pi), float(TWO_PI), ALU.is_gt, ALU.mult), r=[tm], w=[kf])
                    S.op('dve', lambda e: e.tensor_tensor(tm.ap[R, :], tm.ap[R, :], kf.ap[R, :], ALU.subtract), r=[tm, kf], w=[tm])
                    S.op('dve', lambda e: e.tensor_scalar(kf.ap[R, :], tm.ap[R, :], float(-np.pi), float(TWO_PI), ALU.is_lt, ALU.mult), r=[tm], w=[kf])
                    S.op('dve', lambda e: e.tensor_tensor(tm.ap[R, :], tm.ap[R, :], kf.ap[R, :], ALU.add), r=[tm, kf], w=[tm])
                    S.op('act', lambda e: e.activation(dst.ap[R, :], tm.ap[R, :], AF.Sin), r=[tm], w=[dst])

                xrhs = lambda kc: xTc.ap[:, kc, :]
                for c in range(0 if MIXDBG[0] == 10 else SEQ // CH):
                    cs = slice(c * CH, (c + 1) * CH)
                    for j in range(CH // 128):
                        t = s * 16 + c * (CH // 128) + j
                        S.dma('sp', xs, xs.ap[:], x_dram[t * 128:(t + 1) * 128, :], reads=[x_res])
                        S.op('act', lambda e: e.activation(xb.ap[:], xs.ap[:], AF.Copy), r=[xs], w=[xb])
                        for kc in range(8):
                            S.op('pe', lambda e, kc=kc: e.transpose(P.T.ap[:, kc * 128:(kc + 1) * 128], xb.ap[:, kc * 128:(kc + 1) * 128], identb.ap[:]),
                                 r=[xb, identb], w=[P.T])
                        S.op('dve', lambda e, j=j: e.tensor_copy(xTc.ap[:, :, j * 128:(j + 1) * 128], P.T.ap[:].rearrange("p (a b) -> p a b", a=8)),
                             r=[P.T], w=[xTc])
                    S.dma('sp', posi, posi.ap[R, :], D['positions'][s, c * CH:(c + 1) * CH].partition_broadcast(32))
                    S.op('dve', lambda e: e.tensor_copy(ang.ap[R, :], posi.ap[R, :]), r=[posi], w=[ang])
                    S.op('dve', lambda e: e.tensor_scalar(ang.ap[R, :], ang.ap[R, :], invf.ap[R, 0:1], None, ALU.mult), r=[ang, invf], w=[ang])
                    S.op('dve', lambda e: e.tensor_scalar(ang2.ap[R, :], ang.ap[R, :], float(np.pi / 2), None, ALU.add), r=[ang], w=[ang2])
                    range_sin(ang, sinb)
                    range_sin(ang2, cosb)
                    if MIXDBG[0] == 11:
                        continue
                    for cc in range(2):
                        pb = proj(lambda kc, cc=cc: win.ap[:, kc, C_SCC + cc * 128:C_SCC + (cc + 1) * 128], 128, 8, xrhs, [win, xTc])
                        S.op('act', lambda e, pb=pb, cc=cc: e.activation(tA[cc].ap[:], pb.ap[:, 0:CH], AF.Copy), r=[pb], w=[tA[cc]])
                        pb2 = proj(lambda kc, cc=cc: win.ap[:, kc, C_SCH + cc * 128:C_SCH + (cc + 1) * 128], 128, 8, xrhs, [win, xTc])
                        S.op('dve', lambda e, pb2=pb2, cc=cc: e.tensor_tensor(usc.ap[:, cc, 2 + c * CH:2 + (c + 1) * CH], pb2.ap[:, 0:CH], tA[cc].ap[:], ALU.mult),
                             r=[pb2, tA[cc]], w=[usc])
                    for cc in range(2):
                        pb = proj(lambda kc, cc=cc: win.ap[:, kc, C_CFG + cc * 128:C_CFG + (cc + 1) * 128], 128, 8, xrhs, [win, xTc])
                        S.op('act', lambda e, pb=pb, cc=cc: e.activation(tA[cc].ap[:], pb.ap[:, 0:CH], AF.Sigmoid), r=[pb], w=[tA[cc]])
                        pb2 = proj(lambda kc, cc=cc: win.ap[:, kc, C_CFA + cc * 128:C_CFA + (cc + 1) * 128], 128, 8, xrhs, [win, xTc])
                        S.op('dve', lambda e, pb2=pb2, cc=cc: e.tensor_tensor(ucf.ap[:, cc, 30 + c * CH:30 + (c + 1) * CH], pb2.ap[:, 0:CH], tA[cc].ap[:], ALU.mult),
                             r=[pb2, tA[cc]], w=[ucf])
                    if MIXDBG[0] == 12:
                        continue
                    for j in range(2):
                        pb = proj(lambda kc, j=j: win.ap[:, kc, C_SWQ + j * 128:C_SWQ + (j + 1) * 128], 128, 8, xrhs, [win, xTc])
                        S.op('act', lambda e, pb=pb, j=j: e.activation(swq.ap[:, j, cs], pb.ap[:, 0:CH], AF.Copy, scale=0.125), r=[pb], w=[swq])
                        pb2 = proj(lambda kc, j=j: wswk.ap[:, kc, j * 128:(j + 1) * 128], 128, 8, xrhs, [wswk, xTc])
                        S.op('dve', lambda e, pb2=pb2, j=j: e.tensor_copy(swk.ap[:, j, cs], pb2.ap[:, 0:CH]), r=[pb2], w=[swk])
                    if MIXDBG[0] == 13:
                        continue
                    for cc in range(2):
                        pb = proj(lambda kc, cc=cc: win.ap[:, kc, C_CQ + cc * 128:C_CQ + (cc + 1) * 128], 128, 8, xrhs, [win, xTc])
                        S.op('act', lambda e, pb=pb, cc=cc: e.activation(cqf.ap[:, cc, :], pb.ap[:, 0:CH], AF.Copy), r=[pb], w=[cqf])
                        S.op('dve', lambda e, pb=pb, cc=cc: e.tensor_tensor(cqs.ap[:, cc, :], pb.ap[:, 0:CH], cqf.ap[:, cc, :], ALU.mult), r=[pb, cqf], w=[cqs])
                    pb = proj(lambda cc: onesb.ap[:], 128, 2, lambda cc: cqs.ap[:, cc, :], [onesb, cqs])
                    S.op('act', lambda e, pb=pb: e.activation(rq.ap[:], pb.ap[:, 0:CH], AF.Sqrt, bias=EPS_AP[0].ap[:, 0:1], scale=1.0 / 256), r=[pb], w=[rq])
                    S.op('dve', lambda e: e.reciprocal(rq.ap[:], rq.ap[:]), r=[rq], w=[rq])
                    for cc in range(2):
                        S.op('dve', lambda e, cc=cc: e.scalar_tensor_tensor(cqn.ap[:, cc, :], cqf.ap[:, cc, :], gq.ap[:, cc:cc + 1], rq.ap[:], ALU.mult, ALU.mult),
                             r=[cqf, gq, rq], w=[cqn])
                    pb = proj(lambda kc: win.ap[:, kc, C_CKV:C_CKV + 128], 128, 8, xrhs, [win, xTc])
                    S.op('act', lambda e, pb=pb: e.activation(ckvf.ap[:], pb.ap[:, 0:CH], AF.Copy), r=[pb], w=[ckvf])
                    S.op('dve', lambda e, pb=pb: e.tensor_tensor(ckvs.ap[:], pb.ap[:, 0:CH], ckvf.ap[:], ALU.mult), r=[pb, ckvf], w=[ckvs])
                    pb = proj(lambda cc: onesb.ap[:], 128, 1, lambda cc: ckvs.ap[:], [onesb, ckvs])
                    S.op('act', lambda e, pb=pb: e.activation(rkv.ap[:], pb.ap[:, 0:CH], AF.Sqrt, bias=EPS_AP[0].ap[:, 0:1], scale=1.0 / 128), r=[pb], w=[rkv])
                    S.op('dve', lambda e: e.reciprocal(rkv.ap[:], rkv.ap[:]), r=[rkv], w=[rkv])
                    S.op('dve', lambda e: e.scalar_tensor_tensor(ckvn.ap[:], ckvf.ap[:], gkv.ap[:, 0:1], rkv.ap[:], ALU.mult, ALU.mult), r=[ckvf, gkv, rkv], w=[ckvn])
                    if MIXDBG[0] == 14:
                        continue
                    for h in range(4):
                        pb = proj(lambda cc, h=h: wuq.ap[:, cc, h * 96:(h + 1) * 96], 96, 2, lambda cc: cqn.ap[:, cc, :], [wuq, cqn])
                        pbr = proj(lambda cc, h=h: wuqr.ap[:, cc, h * 96:(h + 1) * 96], 96, 2, lambda cc: cqn.ap[:, cc, :], [wuqr, cqn])
                        S.op('act', lambda e, pb=pb, h=h: e.activation(q96[h].ap[0:64, cs], pb.ap[0:64, 0:CH], AF.Copy, scale=QSCALE), r=[pb], w=[q96[h]])
                        S.op('dve', lambda e, pb=pb: e.scalar_tensor_tensor(t1.ap[R, :], pb.ap[R, 0:CH], QSCALE, cosb.ap[R, :], ALU.mult, ALU.mult), r=[pb, cosb], w=[t1])
                        S.op('dve', lambda e, pbr=pbr: e.scalar_tensor_tensor(t2.ap[R, :], pbr.ap[R, 0:CH], QSCALE, sinb.ap[R, :], ALU.mult, ALU.mult), r=[pbr, sinb], w=[t2])
                        S.op('dve', lambda e, h=h: e.tensor_tensor(q96[h].ap[R, cs], t1.ap[R, :], t2.ap[R, :], ALU.add), r=[t1, t2], w=[q96[h]])
                    if MIXDBG[0] == 15:
                        continue
                    for h in range(4):
                        pb = proj(lambda kc, h=h: wuk.ap[:, h * 64:(h + 1) * 64], 64, 1, lambda kc: ckvn.ap[:], [wuk, ckvn])
                        S.op('act', lambda e, pb=pb, h=h: e.activation(k96[h].ap[0:64, cs], pb.ap[0:64, 0:CH], AF.Copy), r=[pb], w=[k96[h]])
                    pb = proj(lambda kc: wkr.ap[:, kc, :], 96, 8, xrhs, [wkr, xTc])
                    pbr = proj(lambda kc: wkrr.ap[:, kc, :], 96, 8, xrhs, [wkrr, xTc])
                    S.op('dve', lambda e, pb=pb: e.tensor_tensor(t1.ap[R, :], pb.ap[R, 0:CH], cosb.ap[R, :], ALU.mult), r=[pb, cosb], w=[t1])
                    S.op('dve', lambda e, pbr=pbr: e.tensor_tensor(t2.ap[R, :], pbr.ap[R, 0:CH], sinb.ap[R, :], ALU.mult), r=[pbr, sinb], w=[t2])
                    S.op('dve', lambda e: e.tensor_tensor(t1.ap[R, :], t1.ap[R, :], t2.ap[R, :], ALU.add), r=[t1, t2], w=[t1])
                    for h in range(4):
                        S.op('pool' if h % 2 else 'dve', lambda e, h=h: e.tensor_copy(k96[h].ap[R, cs], t1.ap[R, :]), r=[t1], w=[k96[h]])
                    if MIXDBG[0] == 16:
                        continue
                    for j in range(CH // 128):
                        tq = c * (CH // 128) + j
                        pb = nextps()
                        for kc in range(0 if MIXDBG[0] == 18 else 8):
                            S.op('pe', lambda e, kc=kc, j=j, pb=pb: e.matmul(pb.ap[:, 0:256], xTc.ap[:, kc, j * 128:(j + 1) * 128], win.ap[:, kc, C_SCB:C_SCB + 256],
                                                                             start=(kc == 0), stop=(kc == 7)), r=[xTc, win], w=[pb])
                        for kc in range(0 if MIXDBG[0] == 18 else 8):
                            S.op('pe', lambda e, kc=kc, j=j, pb=pb: e.matmul(pb.ap[:, 256:384], xTc.ap[:, kc, j * 128:(j + 1) * 128], win.ap[:, kc, C_SWV:C_SWV + 128],
                                                                             start=(kc == 0), stop=(kc == 7)), r=[xTc, win], w=[pb])
                        S.op('act', lambda e, pb=pb, tq=tq: e.activation(scb.ap[:, tq, :], pb.ap[:, 0:256], AF.Copy), r=[pb], w=[scb])
                        S.op('dve', lambda e, pb=pb, tq=tq: e.tensor_copy(swv.ap[:, tq, :, 0:64], pb.ap[:, 256:384].rearrange("p (a b) -> p a b", a=2)), r=[pb], w=[swv])
                        pb2 = nextps()
                        if MIXDBG[0] == 17:
                            continue
                        S.op('pe', lambda e, j=j, pb2=pb2: e.matmul(pb2.ap[:, 0:256], ckvn.ap[:, j * 128:(j + 1) * 128], wuv.ap[:], start=True, stop=True),
                             r=[ckvn, wuv], w=[pb2])
                        S.op('dve', lambda e, pb2=pb2, tq=tq: e.tensor_copy(mlav.ap[:, tq, :, 0:64], pb2.ap[:, 0:256].rearrange("p (a b) -> p a b", a=4)), r=[pb2], w=[mlav])
                S.barrier(dmab)
            if MIXDBG[0] == 1:
                continue
            with ExitStack() as es2:
                S.es = es2
                wout = S.sb("wout", [128, 8, 1024], BF16)
                wst2 = S.sb("wst2", [128, 1024], F32)
                scw = S.sb("scw", [128, 2, 3], F32)
                cfw = S.sb("cfw", [128, 2, 31], F32)
                dsc = S.sb("dsc", [128, 2, 3, 128], BF16)
                dcf = S.sb("dcf", [128, 2, 31, 128], BF16)
                ln1g = S.sb("ln1g", [128, 1024], F32)
                ln1b = S.sb("ln1b", [128, 1024], F32)
                mixg = S.sb("mixg", [128, 1024], F32)
                cfb = S.sb("cfb", [128, 256], F32)
                cfg_ = S.sb("cfg_", [128, 256], F32)
                cfbe = S.sb("cfbe", [128, 256], F32)
                alibi = S.sb("alibi", [128, 4, 2, 128], F32)
                trif = S.sb("trif", [128, 128], F32)
                trib = S.sb("trib", [128, 128], BF16)
                esink = S.sb("esink", [128, 4], F32)
                xs2 = [S.sb("xs2_%d" % i, [128, 1024], F32) for i in range(2)]
                mix = S.sb("mix", [128, 1024], F32)
                sqt = S.sb("sqt", [128, 1024], F32)
                mixb = S.sb("mixb", [128, 1024], BF16)
                mixT = S.sb("mixT", [128, 8, 128], BF16)
                yb = [S.sb("y%d" % i, [128, 1024], F32) for i in range(2)]
                PT = [S.sb("PT%d" % i, [128, 512], BF16) for i in range(3)]
                tS = [S.sb("tS%d" % i, [128, 256], F32) for i in range(2)]
                cft = S.sb("cft", [128, 256], F32)
                stats = S.sb("stats", [128, 2, 6], F32)
                aggr = S.sb("aggr", [128, 2], F32)
                rstd = S.sb("rstd", [128, 1], F32)
                st1 = S.sb("st1", [128, 6], F32)
                ag1 = S.sb("ag1", [128, 2], F32)
                rs1 = S.sb("rs1", [128, 1], F32)
                ssq = S.sb("ssq", [128, 16], F32)
                rec4 = S.sb("rec4", [128, 4], F32)
                dmab = [wst2, scw, cfw, ln1g, ln1b, mixg, cfb, cfg_, cfbe, alibi, trif, esink] + xs2 + yb
                for kc in range(8):
                    S.dma('sp', wst2, wst2.ap[:], D['w_out'][l, kc * 128:(kc + 1) * 128, :])
                    S.op('act', lambda e, kc=kc: e.activation(wout.ap[:, kc, :], wst2.ap[:], AF.Copy), r=[wst2], w=[wout])
                for cc in range(2):
                    S.dma('sp', scw, scw.ap[:, cc, :], D['sc_conv_w'][l][:, cc * 128:(cc + 1) * 128].rearrange("k p -> p k"), part=True, allow_slow_non_contiguous=True)
                    S.dma('sp', cfw, cfw.ap[:, cc, :], D['cf_dw_w'][l][:, cc * 128:(cc + 1) * 128].rearrange("k p -> p k"), part=True, allow_slow_non_contiguous=True)
                for cc in range(2):
                    for k in range(3):
                        S.op('dve', lambda e, cc=cc, k=k: e.tensor_scalar(dsc.ap[:, cc, k, :], identf.ap[:], scw.ap[:, cc, k:k + 1], None, ALU.mult), r=[identf, scw], w=[dsc])
                    for k in range(31):
                        S.op('dve', lambda e, cc=cc, k=k: e.tensor_scalar(dcf.ap[:, cc, k, :], identf.ap[:], cfw.ap[:, cc, k:k + 1], None, ALU.mult), r=[identf, cfw], w=[dcf])
                for (bf, key) in ((ln1g, 'ln1_g'), (ln1b, 'ln1_b'), (mixg, 'mix_norm_g'), (cfb, 'cf_dw_b'), (cfg_, 'cf_ln_g'), (cfbe, 'cf_ln_b'), (esink, 'swa_sinks')):
                    S.dma('sp', bf, bf.ap[:], D[key][l].partition_broadcast(128))
                S.op('act', lambda e: e.activation(esink.ap[:], esink.ap[:], AF.Exp), r=[esink], w=[esink])
                S.dma('sp', alibi, alibi.ap[:], D['c_alibi'][:, :, :, :])
                S.dma('sp', trif, trif.ap[:], D['c_tri'][:, :])
                S.op('dve', lambda e: e.tensor_copy(trib.ap[:], trif.ap[:]), r=[trif], w=[trib])
                ptc = 0
                for qb in range(16):
                    t = s * 16 + qb
                    qs = slice(qb * 128, (qb + 1) * 128)
                    if 'conv' in MIXSKIP:
                        S.op('dve', lambda e: e.memset(mix.ap[:], 0.0), w=[mix])
                    for cc in range(0 if 'conv' in MIXSKIP else 2):
                        for k in range(3):
                            S.op('pe', lambda e, cc=cc, k=k: e.matmul(P.X.ap[:, cc * 128:(cc + 1) * 128], usc.ap[:, cc, qb * 128 + k:qb * 128 + k + 128], dsc.ap[:, cc, k, :],
                                                                      start=(k == 0), stop=(k == 2)), r=[usc, dsc], w=[P.X])
                    for cc in range(0 if 'conv' in MIXSKIP else 2):
                        for k in range(31):
                            S.op('pe', lambda e, cc=cc, k=k: e.matmul(P.X.ap[:, 256 + cc * 128:256 + (cc + 1) * 128], ucf.ap[:, cc, qb * 128 + k:qb * 128 + k + 128], dcf.ap[:, cc, k, :],
                                                                      start=(k == 0), stop=(k == 30)), r=[ucf, dcf], w=[P.X])
                    if 'conv' not in MIXSKIP:
                      S.op('dve', lambda e: e.tensor_tensor(mix.ap[:, 0:256], P.X.ap[:, 0:256], scb.ap[:, qb, :], ALU.mult), r=[P.X, scb], w=[mix])
                    S.op('dve', lambda e: e.tensor_tensor(cft.ap[:], P.X.ap[:, 256:512], cfb.ap[:], ALU.add), r=[P.X, cfb], w=[cft])
                    S.op('dve', lambda e: e.bn_stats(st1.ap[:], cft.ap[:]), r=[cft], w=[st1])
                    S.op('dve', lambda e: e.bn_aggr(ag1.ap[:], st1.ap[:]), r=[st1], w=[ag1])
                    S.op('act', lambda e: e.activation(rs1.ap[:], ag1.ap[:, 1:2], AF.Sqrt, bias=EPS_AP[0].ap[:, 0:1], scale=1.0), r=[ag1], w=[rs1])
                    S.op('dve', lambda e: e.reciprocal(rs1.ap[:], rs1.ap[:]), r=[rs1], w=[rs1])
                    S.op('dve', lambda e: e.tensor_scalar(cft.ap[:], cft.ap[:], ag1.ap[:, 0:1], rs1.ap[:, 0:1], ALU.subtract, ALU.mult), r=[cft, ag1, rs1], w=[cft])
                    S.op('dve', lambda e: e.tensor_tensor(cft.ap[:], cft.ap[:], cfg_.ap[:], ALU.mult), r=[cft, cfg_], w=[cft])
                    S.op('dve', lambda e: e.tensor_tensor(cft.ap[:], cft.ap[:], cfbe.ap[:], ALU.add), r=[cft, cfbe], w=[cft])
                    S.op('act', lambda e: e.activation(mix.ap[:, 512:768], cft.ap[:], AF.Silu), r=[cft], w=[mix])
                    oP = P.G[0]
                    nk = qb + 1
                    for h in range(0 if 'mla' in MIXSKIP else 4):
                        for g0 in range(0, nk, 4):
                            kts = list(range(g0, min(g0 + 4, nk)))
                            pb = P.A[ptc % 2]
                            pt = PT[ptc % 3]
                            ptc += 1
                            for gi, kt in enumerate(kts):
                                S.op('pe', lambda e, gi=gi, kt=kt, pb=pb, h=h: e.matmul(pb.ap[:, gi * 128:(gi + 1) * 128], k96[h].ap[:, kt * 128:(kt + 1) * 128], q96[h].ap[:, qs],
                                                                                       start=True, stop=True), r=[k96[h], q96[h]], w=[pb])
                            n = len(kts) * 128
                            S.op('act', lambda e, pb=pb, pt=pt, n=n: e.activation(pt.ap[:, 0:n], pb.ap[:, 0:n], AF.Exp), r=[pb], w=[pt])
                            if kts[-1] == qb:
                                gi = len(kts) - 1
                                S.op('pool', lambda e, pt=pt, gi=gi: e.tensor_tensor(pt.ap[:, gi * 128:(gi + 1) * 128], pt.ap[:, gi * 128:(gi + 1) * 128], trib.ap[:], ALU.mult),
                                     r=[pt, trib], w=[pt])
                            for gi, kt in enumerate(kts):
                                S.op('pe', lambda e, gi=gi, kt=kt, pt=pt, h=h: e.matmul(oP.ap[:, h * 65:(h + 1) * 65], pt.ap[:, gi * 128:(gi + 1) * 128], mlav.ap[:, kt, h, 0:65],
                                                                                       start=(kt == 0), stop=(kt == nk - 1)), r=[pt, mlav], w=[oP])
                    o4 = oP.ap[:, 0:260].rearrange("p (h d) -> p h d", h=4)
                    if 'mla' not in MIXSKIP:
                      S.op('dve', lambda e: e.reciprocal(rec4.ap[:], o4[:, :, 64]), r=[oP], w=[rec4])
                    if 'mla' not in MIXSKIP:
                      S.op('dve', lambda e: e.tensor_tensor(mix.ap[:, 256:512].rearrange("p (h d) -> p h d", h=4), o4[:, :, 0:64],
                                                          rec4.ap[:].unsqueeze(2).broadcast_to([128, 4, 64]), ALU.mult), r=[oP, rec4], w=[mix])
                    oS = P.G[1]
                    kts = [kt for kt in (qb - 1, qb) if kt >= 0]
                    for h in range(0 if 'swa' in MIXSKIP else 4):
                        j, s_ = h // 2, h % 2
                        pr = slice(64 * s_, 64 * s_ + 64)
                        pb = P.A[ptc % 2]
                        pt = PT[ptc % 3]
                        ts_ = tS[ptc % 2]
                        ptc += 1
                        for gi, kt in enumerate(kts):
                            S.op('pe', lambda e, gi=gi, kt=kt, pb=pb, j=j, pr=pr: e.matmul(pb.ap[:, gi * 128:(gi + 1) * 128], swk.ap[pr, j, kt * 128:(kt + 1) * 128], swq.ap[pr, j, qs],
                                                                                          start=True, stop=True), r=[swk, swq], w=[pb])
                        n = len(kts) * 128
                        if len(kts) == 2:
                            bias_ap = alibi.ap[:, h, :, :].rearrange("p a b -> p (a b)")
                        else:
                            bias_ap = alibi.ap[:, h, 1, :]
                        S.op('dve', lambda e, pb=pb, ts_=ts_, n=n, bias_ap=bias_ap: e.tensor_tensor(ts_.ap[:, 0:n], pb.ap[:, 0:n], bias_ap, ALU.add), r=[pb, alibi], w=[ts_])
                        S.op('act', lambda e, pt=pt, ts_=ts_, n=n: e.activation(pt.ap[:, 0:n], ts_.ap[:, 0:n], AF.Exp), r=[ts_], w=[pt])
                        for gi, kt in enumerate(kts):
                            S.op('pe', lambda e, gi=gi, kt=kt, pt=pt, h=h, j=j: e.matmul(oS.ap[:, h * 65:(h + 1) * 65], pt.ap[:, gi * 128:(gi + 1) * 128], swv.ap[:, kt, j, 0:65],
                                                                                        start=(gi == 0), stop=(gi == len(kts) - 1)), r=[pt, swv], w=[oS])
                    s4 = oS.ap[:, 0:260].rearrange("p (h d) -> p h d", h=4)
                    if 'swa' not in MIXSKIP:
                      S.op('dve', lambda e: e.tensor_tensor(rec4.ap[:], s4[:, :, 64], esink.ap[:], ALU.add), r=[oS, esink], w=[rec4])
                      S.op('dve', lambda e: e.reciprocal(rec4.ap[:], rec4.ap[:]), r=[rec4], w=[rec4])
                    if 'swa' not in MIXSKIP:
                      S.op('dve', lambda e: e.tensor_tensor(mix.ap[:, 768:1024].rearrange("p (h d) -> p h d", h=4), s4[:, :, 0:64],
                                                          rec4.ap[:].unsqueeze(2).broadcast_to([128, 4, 64]), ALU.mult), r=[oS, rec4], w=[mix])
                    S.op('act', lambda e: e.activation(sqt.ap[:], mix.ap[:], AF.Square), r=[mix], w=[sqt])
                    S.op('dve', lambda e: e.tensor_reduce(ssq.ap[:], sqt.ap[:].rearrange("p (a b) -> p a b", a=16), axis=AX.X, op=ALU.add), r=[sqt], w=[ssq])
                    S.op('act', lambda e: e.activation(ssq.ap[:], ssq.ap[:], AF.Sqrt, bias=EPS_AP[0].ap[:, 0:1], scale=1.0 / 64), r=[ssq], w=[ssq])
                    S.op('dve', lambda e: e.reciprocal(ssq.ap[:], ssq.ap[:]), r=[ssq], w=[ssq])
                    S.op('dve', lambda e: e.tensor_tensor(sqt.ap[:].rearrange("p (a b) -> p a b", a=16), mix.ap[:].rearrange("p (a b) -> p a b", a=16),
                                                          ssq.ap[:].unsqueeze(2).broadcast_to([128, 16, 64]), ALU.mult), r=[mix, ssq], w=[sqt])
                    S.op('dve', lambda e: e.tensor_tensor(mixb.ap[:], sqt.ap[:], mixg.ap[:], ALU.mult), r=[sqt, mixg], w=[mixb])
                    for kc in range(8):
                        S.op('pe', lambda e, kc=kc: e.transpose(P.T.ap[:, kc * 128:(kc + 1) * 128], mixb.ap[:, kc * 128:(kc + 1) * 128], identb.ap[:]),
                             r=[mixb, identb], w=[P.T])
                    S.op('act', lambda e: e.activation(mixT.ap[:].rearrange("p a b -> p (a b)"), P.T.ap[:], AF.Copy), r=[P.T], w=[mixT])
                    for half in range(2):
                        for kc in range(8):
                            S.op('pe', lambda e, half=half, kc=kc: e.matmul(P.O.ap[:, half * 512:(half + 1) * 512], mixT.ap[:, kc, :], wout.ap[:, kc, half * 512:(half + 1) * 512],
                                                                           start=(kc == 0), stop=(kc == 7)), r=[mixT, wout], w=[P.O])
                    xs_, y = xs2[qb % 2], yb[qb % 2]
                    S.dma('sp', xs_, xs_.ap[:], x_dram[t * 128:(t + 1) * 128, :], reads=[x_res])
                    S.op('dve', lambda e, xs_=xs_, y=y: e.scalar_tensor_tensor(y.ap[:], xs_.ap[:], float(ALPHA), P.O.ap[:], ALU.mult, ALU.add), r=[xs_, P.O], w=[y])
                    layer_norm_tile(S, y, stats, aggr, rstd, ln1g, ln1b, y)
                    S.dma('sp', y, h_dram[t * 128:(t + 1) * 128, :], y.ap[:], store=True, writes=[h_res])
                S.barrier(dmab)
        S.barrier([])


def host_consts():
    c = {}
    c['c_ident'] = np.eye(128, dtype=np.float32)
    c['c_iota'] = np.tile(np.arange(128, dtype=np.float32)[None, :], (128, 1))
    k = np.arange(128)[:, None]
    q = np.arange(128)[None, :]
    c['c_tri'] = (k <= q).astype(np.float32)
    slopes = 2.0 ** (-8.0 * np.arange(1, 5, dtype=np.float64) / 4)
    al = np.zeros((128, 4, 2, 128), dtype=np.float32)
    NEG = -30000.0
    for h in range(4):
        dprev = (q + 128 - k).astype(np.float64)
        al[:, h, 0, :] = np.where(dprev < 128, -slopes[h] * dprev, NEG)
        dcur = (q - k).astype(np.float64)
        al[:, h, 1, :] = np.where(dcur >= 0, -slopes[h] * dcur, NEG)
    c['c_alibi'] = al
    inv = (10000.0 ** (-np.arange(16, dtype=np.float32) / np.float32(16))).astype(np.float32)
    f = np.zeros((96, 1), dtype=np.float32)
    f[64:96, 0] = np.concatenate([inv, inv])
    c['c_invf'] = f
    return c


CONST_SHAPES = {'c_ident': [128, 128], 'c_iota': [128, 128], 'c_tri': [128, 128], 'c_alibi': [128, 4, 2, 128], 'c_invf': [96, 1]}


def build_mixer_test(nseq=1):
    nc = bass.Bass("TRN2", target_bir_lowering=False)
    D = {}
    ntok = nseq * SEQ
    D['x'] = nc.dram_tensor("x", [ntok, 1024], F32, kind="ExternalInput").ap()
    D['positions'] = nc.dram_tensor("positions", [nseq, SEQ], I32, kind="ExternalInput").ap()
    for k, shp in CONST_SHAPES.items():
        D[k] = nc.dram_tensor(k, shp, F32, kind="ExternalInput").ap()
    for k in WSHAPES:
        if k.startswith("peer") or k.startswith("ln2"):
            continue
        D[k] = nc.dram_tensor(k, list(WSHAPES[k]), F32, kind="ExternalInput").ap()
    out = nc.dram_tensor("out", [ntok, 1024], F32, kind="ExternalOutput").ap()
    with ExitStack() as es:
        S = Sched(nc, es)
        P = PS(S)
        C = make_consts(S, nc, D)
        mixer_phase(S, nc, P, C, 0, D['x'], S.dram("xr"), out, S.dram("outr"), D, nseq=nseq)
        S.barrier([])
        print("instr counts", S.ninstr, "sems", S.nsem)
    return nc


def build_full(depth=DEPTH):
    nc = bass.Bass("TRN2", target_bir_lowering=False)
    D = {}
    D['x'] = nc.dram_tensor("x", [NTOK, 1024], F32, kind="ExternalInput").ap()
    D['positions'] = nc.dram_tensor("positions", [NSEQ, SEQ], I32, kind="ExternalInput").ap()
    for k, shp in CONST_SHAPES.items():
        D[k] = nc.dram_tensor(k, shp, F32, kind="ExternalInput").ap()
    for k in WSHAPES:
        D[k] = nc.dram_tensor(k, list(WSHAPES[k]), F32, kind="ExternalInput").ap()
    out = nc.dram_tensor("out", [NTOK, 1024], F32, kind="ExternalOutput").ap()
    hbuf = nc.dram_tensor("s_h", [NTOK, 1024], F32).ap()
    xbuf = nc.dram_tensor("s_x", [NTOK, 1024], F32).ap()
    D['scr'] = declare_scratch(nc, NTOK)
    with ExitStack() as es:
        S = Sched(nc, es)
        P = PS(S)
        C = make_consts(S, nc, D)
        x_res, h_res, xb_res, o_res = S.dram("x_res"), S.dram("h_res"), S.dram("xb_res"), S.dram("o_res")
        for l in range(depth):
            xin, xin_r = (D['x'], x_res) if l == 0 else (xbuf, xb_res)
            last = (l == depth - 1)
            xo, xo_r = (out, o_res) if last else (xbuf, xb_res)
            mixer_phase(S, nc, P, C, l, xin, xin_r, hbuf, h_res, D)
            peer_prepass(S, nc, P, C, l, D)
            PEER_FN[0](S, nc, P, C, l, hbuf, h_res, xo, xo_r, D, last)
        S.finish()
        print("instr counts", S.ninstr, "sems", S.nsem, flush=True)
    return nc


_NC_CACHE = {}


def kernel(**inputs):
    n = 8
    x = np.ascontiguousarray(np.asarray(inputs['x'], dtype=np.float32))
    pos = np.ascontiguousarray(np.asarray(inputs['positions'], dtype=np.int32))
    consts = host_consts()
    wts = {k: np.ascontiguousarray(np.asarray(inputs[k], dtype=np.float32)) for k in WSHAPES}
    if 'nc' not in _NC_CACHE:
        _NC_CACHE['nc'] = build_full()
    nc = _NC_CACHE['nc']
    in_maps = []
    for c in range(n):
        m = {'x': x[c * NSEQ:(c + 1) * NSEQ].reshape(NTOK, 1024), 'positions': pos[c * NSEQ:(c + 1) * NSEQ]}
        m.update(consts)
        m.update(wts)
        in_maps.append(m)
    res = run_bass_kernel_spmd(nc, in_maps, core_ids=list(range(n)))
    out = np.concatenate([r['out'].reshape(NSEQ, SEQ, 1024) for r in res.results], axis=0)
    return out.astype(np.float32)


U32 = mybir.dt.uint32


def peer_phase2(S, nc, P, C, l, h_dram, h_res, out_dram, out_res, D, final, ntiles=NT):
    identb = C['identb']
    iota = C['iota']
    scr = D['scr']
    hT_d, G_d = scr['hT'], scr['G']
    hT_r, G_r = S.dram("hT_r"), S.dram("G_r")
    with ExitStack() as es1:
        S.es = es1
        wq = S.sb("wq", [128, 8, 2048], BF16)
        wqst = [S.sb("wqst%d" % i, [128, 2048], F32) for i in range(2)]
        KT = S.sb("KT", [128, 16, 128], BF16)
        kst = [S.sb("kst%d" % i, [128, 128], F32) for i in range(2)]
        kb = [S.sb("kb%d" % i, [128, 128], BF16) for i in range(2)]
        hst = [S.sb("hst%d" % i, [128, 1024], F32) for i in range(2)]
        hb = S.sb("hb", [128, 1024], BF16)
        hTt = [S.sb("hTt%d" % i, [128, 8, 128], BF16) for i in range(2)]
        qT = S.sb("qT", [128, 16, 128], BF16)
        sc = S.sb("sc", [128, 16, 128], F32)
        mscr = S.sb("mscr", [128, 256], F32)
        top16 = S.sb("top16", [128, 16, 16], F32)
        iu = S.sb("iu", [128, 16, 16], U32)
        idxf = S.sb("idxf", [128, 16, 16], F32)
        cand = S.sb("cand", [128, 8, 256], F32)
        best = S.sb("best", [128, 8, 16], F32)
        pu = S.sb("pu", [128, 8, 16], U32)
        posf = S.sb("posf", [128, 8, 16], F32)
        tmpf = S.sb("tmpf", [128, 8, 16], F32)
        ti = S.sb("ti", [128, 8, 16], I32)
        rif = S.sb("rif", [128, 8, 16], F32)
        rjf = S.sb("rjf", [128, 8, 16], F32)
        oh = S.sb("oh", [128, 8, 16, 16], F32)
        sm = S.sb("sm", [128, 8, 8], F32)
        e16 = S.sb("e16", [128, 8, 16], F32)
        K3 = S.sb("K3", [128, 3, 128], F32)
        K3b = S.sb("K3b", [128, 3, 128], BF16)
        T3 = S.sb("T3", [128, 3, 128], F32)
        E0 = [S.sb("E0_%d" % i, [128, 8, 128], BF16) for i in range(2)]
        E1 = [S.sb("E1_%d" % i, [128, 8, 128], BF16) for i in range(2)]
        E2 = [S.sb("E2_%d" % i, [128, 8, 128], BF16) for i in range(2)]
        Gs = [S.sb("Gs%d" % i, [128, 128, 128], BF16) for i in range(2)]
        for kc in range(8):
            st = wqst[kc % 2]
            S.dma('sp', st, st.ap[:], D['peer_w_q'][l, kc * 128:(kc + 1) * 128, :])
            if kc % 2:
                S.op('pool', lambda e, kc=kc, st=st: e.tensor_copy(wq.ap[:, kc, :], st.ap[:]), r=[st], w=[wq])
            else:
                S.op('act', lambda e, kc=kc, st=st: e.activation(wq.ap[:, kc, :], st.ap[:], AF.Copy), r=[st], w=[wq])
        for hp in range(16):
            st = kst[hp % 2]
            S.dma('sp', st, st.ap[:], D['peer_sub_keys'][l, hp // 2, hp % 2, :, :])
            S.op('dve', lambda e, st=st, hp=hp: e.tensor_copy(kb[hp % 2].ap[:], st.ap[:]), r=[st], w=[kb[hp % 2]])
            S.op('pe', lambda e, hp=hp: e.transpose(P.T.ap[:, 0:128], kb[hp % 2].ap[:], identb.ap[:]), r=[kb[hp % 2], identb], w=[P.T])
            S.op('dve', lambda e, hp=hp: e.tensor_copy(KT.ap[:, hp, :], P.T.ap[:, 0:128]), r=[P.T], w=[KT])
        ecnt = 0
        pcnt = 0
        for t in range(ntiles):
            hs = hst[t % 2]
            S.dma('sp', hs, hs.ap[:], h_dram[t * 128:(t + 1) * 128, :], reads=[h_res])
            S.op('act', lambda e, hs=hs: e.activation(hb.ap[:], hs.ap[:], AF.Copy), r=[hs], w=[hb])
            for kc in range(8):
                S.op('pe', lambda e, kc=kc: e.transpose(P.T.ap[:, kc * 128:(kc + 1) * 128], hb.ap[:, kc * 128:(kc + 1) * 128], identb.ap[:]),
                     r=[hb, identb], w=[P.T])
            ht = hTt[t % 2]
            S.op('dve', lambda e, ht=ht: e.tensor_copy(ht.ap[:].rearrange("p a b -> p (a b)"), P.T.ap[:]), r=[P.T], w=[ht])
            S.dma('sp', ht, hT_d[:, :, t * 128:(t + 1) * 128], ht.ap[:], store=True, writes=[hT_r])
            banks = [P.A[0], P.A[1], P.G[0], P.G[1]]
            for g in range(4):
                pb = banks[g]
                for j in range(4):
                    hp = g * 4 + j
                    for kc in range(8):
                        S.op('pe', lambda e, pb=pb, j=j, hp=hp, kc=kc: e.matmul(
                            pb.ap[:, j * 128:(j + 1) * 128], wq.ap[:, kc, hp * 128:(hp + 1) * 128], ht.ap[:, kc, :],
                            start=(kc == 0), stop=(kc == 7)), r=[wq, ht], w=[pb])
                S.op('act', lambda e, pb=pb, g=g: e.activation(qT.ap[:, g * 4:(g + 1) * 4, :].rearrange("p a b -> p (a b)"), pb.ap[:], AF.Copy),
                     r=[pb], w=[qT])
            for half in range(2):
                for j in range(8):
                    hp = half * 8 + j
                    S.op('pe', lambda e, j=j, hp=hp: e.matmul(P.O.ap[:, j * 128:(j + 1) * 128], qT.ap[:, hp, :], KT.ap[:, hp, :],
                                                              start=True, stop=True), r=[qT, KT], w=[P.O])
                S.op('act', lambda e, half=half: e.activation(sc.ap[:, half * 8:(half + 1) * 8, :].rearrange("p a b -> p (a b)"), P.O.ap[:], AF.Copy),
                     r=[P.O], w=[sc])
            for hp in range(16):
                S.op('dve', lambda e, hp=hp: e.max(top16.ap[:, hp, 0:8], sc.ap[:, hp, :]), r=[sc], w=[top16])
                S.op('dve', lambda e, hp=hp: e.max_index(iu.ap[:, hp, 0:8], top16.ap[:, hp, 0:8], sc.ap[:, hp, :]), r=[sc, top16], w=[iu])
                S.op('dve', lambda e, hp=hp: e.match_replace(mscr.ap[:, 0:128], top16.ap[:, hp, 0:8], sc.ap[:, hp, :], -1e30),
                     r=[sc, top16], w=[mscr])
                S.op('dve', lambda e, hp=hp: e.max(top16.ap[:, hp, 8:16], mscr.ap[:, 0:128]), r=[mscr], w=[top16])
                S.op('dve', lambda e, hp=hp: e.max_index(iu.ap[:, hp, 8:16], top16.ap[:, hp, 8:16], mscr.ap[:, 0:128]), r=[mscr, top16], w=[iu])
            S.op('dve', lambda e: e.tensor_copy(idxf.ap[:], iu.ap[:]), r=[iu], w=[idxf])
            t4 = top16.ap[:].rearrange("p (h q) k -> p h q k", q=2)
            i4 = idxf.ap[:].rearrange("p (h q) k -> p h q k", q=2)
            S.op('dve', lambda e: e.tensor_tensor(cand.ap[:].rearrange("p h (a b) -> p h a b", a=16),
                                                  t4[:, :, 0, :].unsqueeze(3).broadcast_to([128, 8, 16, 16]),
                                                  t4[:, :, 1, :].unsqueeze(2).broadcast_to([128, 8, 16, 16]), ALU.add),
                 r=[top16], w=[cand])
            for h in range(8):
                S.op('dve', lambda e, h=h: e.max(best.ap[:, h, 0:8], cand.ap[:, h, :]), r=[cand], w=[best])
                S.op('dve', lambda e, h=h: e.max_index(pu.ap[:, h, 0:8], best.ap[:, h, 0:8], cand.ap[:, h, :]), r=[cand, best], w=[pu])
                S.op('dve', lambda e, h=h: e.match_replace(mscr.ap[:], best.ap[:, h, 0:8], cand.ap[:, h, :], -1e30),
                     r=[cand, best], w=[mscr])
                S.op('dve', lambda e, h=h: e.max(best.ap[:, h, 8:16], mscr.ap[:]), r=[mscr], w=[best])
                S.op('dve', lambda e, h=h: e.max_index(pu.ap[:, h, 8:16], best.ap[:, h, 8:16], mscr.ap[:]), r=[mscr, best], w=[pu])
            S.op('dve', lambda e: e.tensor_copy(posf.ap[:], pu.ap[:]), r=[pu], w=[posf])
            S.op('dve', lambda e: e.tensor_scalar(tmpf.ap[:], posf.ap[:], -7.5, 1.0 / 16, ALU.add, ALU.mult), r=[posf], w=[tmpf])
            S.op('dve', lambda e: e.tensor_copy(ti.ap[:], tmpf.ap[:]), r=[tmpf], w=[ti])
            S.op('dve', lambda e: e.tensor_copy(rif.ap[:], ti.ap[:]), r=[ti], w=[rif])
            S.op('dve', lambda e: e.scalar_tensor_tensor(rjf.ap[:], rif.ap[:], -16.0, posf.ap[:], ALU.mult, ALU.add), r=[rif, posf], w=[rjf])
            S.op('dve', lambda e: e.tensor_tensor(e16.ap[:], best.ap[:], best.ap[:, :, 0:1].broadcast_to([128, 8, 16]), ALU.subtract),
                 r=[best], w=[e16])
            S.op('act', lambda e: e.activation(e16.ap[:], e16.ap[:], AF.Exp), r=[e16], w=[e16])
            S.op('dve', lambda e: e.reduce_sum(sm.ap[:, 1, :], e16.ap[:], axis=AX.X), r=[e16], w=[sm])
            S.op('dve', lambda e: e.reciprocal(sm.ap[:, 2, :], sm.ap[:, 1, :]), r=[sm], w=[sm])
            k3 = K3.ap[:].rearrange("p c (h k) -> p c h k", h=8)
            S.op('dve', lambda e: e.tensor_tensor(k3[:, 2, :, :], e16.ap[:], sm.ap[:, 2, :].unsqueeze(2).broadcast_to([128, 8, 16]), ALU.mult),
                 r=[e16, sm], w=[K3])
            io16 = iota.ap[:, 0:16].unsqueeze(1).unsqueeze(1).broadcast_to([128, 8, 16, 16])
            for q, rr in ((0, rif), (1, rjf)):
                S.op('dve', lambda e, rr=rr: e.tensor_tensor(oh.ap[:], rr.ap[:].unsqueeze(3).broadcast_to([128, 8, 16, 16]), io16, ALU.is_equal),
                     r=[rr, iota], w=[oh])
                S.op('dve', lambda e, q=q: e.tensor_tensor(oh.ap[:], oh.ap[:], i4[:, :, q, :].unsqueeze(2).broadcast_to([128, 8, 16, 16]), ALU.mult),
                     r=[oh, idxf], w=[oh])
                S.op('dve', lambda e, q=q: e.tensor_reduce(k3[:, q, :, :], oh.ap[:], axis=AX.X, op=ALU.add), r=[oh], w=[K3])
            S.op('dve', lambda e: e.tensor_copy(K3b.ap[:], K3.ap[:]), r=[K3], w=[K3b])
            for q in range(3):
                S.op('pe', lambda e, q=q: e.transpose(P.T.ap[:, q * 128:(q + 1) * 128], K3b.ap[:, q, :], identb.ap[:]), r=[K3b, identb], w=[P.T])
            S.op('dve', lambda e: e.tensor_copy(T3.ap[:].rearrange("p a b -> p (a b)"), P.T.ap[:, 0:384]), r=[P.T], w=[T3])
            gs = Gs[t % 2]
            NB = 8
            io_b = iota.ap[:].unsqueeze(1).broadcast_to([128, NB, 128])
            for t0 in range(0, 128, NB):
                e0, e1, e2 = E0[ecnt % 2], E1[ecnt % 2], E2[ecnt % 2]
                ecnt += 1
                S.op('dve', lambda e, e0=e0, t0=t0: e.tensor_tensor(e0.ap[:], io_b, T3.ap[:, 0, t0:t0 + NB].unsqueeze(2).broadcast_to([128, NB, 128]), ALU.is_equal),
                     r=[iota, T3], w=[e0])
                S.op('dve', lambda e, e1=e1, t0=t0: e.tensor_tensor(e1.ap[:], io_b, T3.ap[:, 1, t0:t0 + NB].unsqueeze(2).broadcast_to([128, NB, 128]), ALU.is_equal),
                     r=[iota, T3], w=[e1])
                S.op('pool', lambda e, e1=e1, e2=e2, t0=t0: e.tensor_tensor(e2.ap[:], e1.ap[:], T3.ap[:, 2, t0:t0 + NB].unsqueeze(2).broadcast_to([128, NB, 128]), ALU.mult),
                     r=[e1, T3], w=[e2])
                for q4 in range(NB // 4):
                    pb = P.A[pcnt % 2]
                    pcnt += 1
                    for k in range(4):
                        tk = q4 * 4 + k
                        S.op('pe', lambda e, e0=e0, e2=e2, pb=pb, tk=tk, k=k: e.matmul(pb.ap[:, k * 128:(k + 1) * 128], e2.ap[:, tk, :], e0.ap[:, tk, :], start=True, stop=True),
                             r=[e0, e2], w=[pb])
                    ta = t0 + q4 * 4
                    S.op('act', lambda e, pb=pb, gs=gs, ta=ta: e.activation(gs.ap[:, :, ta:ta + 4].rearrange("p i t -> p t i"),
                                                                           pb.ap[:].rearrange("p (t i) -> p t i", t=4), AF.Copy), r=[pb], w=[gs])
            S.dma('sp', gs, G_d[:, :, t * 128:(t + 1) * 128].rearrange("i j t -> j i t"), gs.ap[:], store=True, writes=[G_r])
        S.barrier()
    UT_d, Vb_d = scr['UT'], scr['Vb']
    UT_r, Vb_r = PRE['UT_r'], PRE['Vb_r']
    with ExitStack() as es2:
        S.es = es2
        TB = TBT * 128
        hTB = S.sb("hTB", [128, 8, TB], BF16)
        acc = S.sb("acc", [128, TBT, 1024], F32)
        Gb = [S.sb("Gb%d" % i, [128, TB], BF16) for i in range(4)]
        UT = [S.sb("UT%d" % i, [128, 8, 128], BF16) for i in range(4)]
        Vb = [S.sb("Vb%d" % i, [128, 1024], BF16) for i in range(12)]
        WT = [S.sb("WT%d" % i, [128, 4, TB], BF16) for i in range(3)]
        GA = [S.sb("GA%d" % i, [128, TB], BF16) for i in range(3)]
        hs2 = [S.sb("hs2_%d" % i, [128, 1024], F32) for i in range(2)]
        yb = [S.sb("yb%d" % i, [128, 1024], F32) for i in range(2)]
        gt = S.sb("gt", [128, 1024], F32)
        bt = S.sb("bt", [128, 1024], F32)
        stats = S.sb("stats", [128, 2, 6], F32)
        aggr = S.sb("aggr", [128, 2], F32)
        rstd = S.sb("rstd", [128, 1], F32)
        S.dma('sp', gt, gt.ap[:], D['ln2_g'][l].partition_broadcast(128))
        S.dma('sp', bt, bt.ap[:], D['ln2_b'][l].partition_broadcast(128))
        nblk = ntiles // TBT
        cnt = 0
        ocnt = 0
        NG = NE // 4

        def vstage(eg, first):
            nonlocal ocnt
            wt = WT[eg % 3]
            for tt in range(TBT):
                ob = ocnt % 2
                ocnt += 1
                for half in range(2):
                    po = (P.O if ob == 0 else P.G[half])
                    osl = slice(half * 512, (half + 1) * 512) if ob == 0 else slice(0, 512)
                    for ei in range(4):
                        vb = Vb[(eg % 3) * 4 + ei]
                        S.op('pe', lambda e, tt=tt, half=half, ei=ei, vb=vb, wt=wt, po=po, osl=osl: e.matmul(
                            po.ap[:, osl], wt.ap[:, ei, tt * 128:(tt + 1) * 128], vb.ap[:, half * 512:(half + 1) * 512],
                            start=(ei == 0), stop=(ei == 3)), r=[wt, vb], w=[po])
                if ob == 0:
                    if first:
                        S.op('dve', lambda e, tt=tt: e.tensor_copy(acc.ap[:, tt, :], P.O.ap[:]), r=[P.O], w=[acc])
                    else:
                        S.op('dve', lambda e, tt=tt: e.tensor_tensor(acc.ap[:, tt, :], acc.ap[:, tt, :], P.O.ap[:], ALU.add), r=[P.O, acc], w=[acc])
                else:
                    for half in range(2):
                        hsl = slice(half * 512, (half + 1) * 512)
                        if first:
                            S.op('dve', lambda e, tt=tt, half=half, hsl=hsl: e.tensor_copy(acc.ap[:, tt, hsl], P.G[half].ap[:]), r=[P.G[half]], w=[acc])
                        else:
                            S.op('dve', lambda e, tt=tt, half=half, hsl=hsl: e.tensor_tensor(acc.ap[:, tt, hsl], acc.ap[:, tt, hsl], P.G[half].ap[:], ALU.add),
                                 r=[P.G[half], acc], w=[acc])

        for blk in range(nblk):
            t0 = blk * TBT
            S.dma('sp', hTB, hTB.ap[:], hT_d[:, :, t0 * 128:(t0 + TBT) * 128], reads=[hT_r])
            for eg in range(NG):
                wt = WT[eg % 3]
                for ei in range(4):
                    e_ = eg * 4 + ei
                    gb = Gb[cnt % 4]
                    ut = UT[cnt % 4]
                    vb = Vb[(eg % 3) * 4 + ei]
                    pA = P.A[cnt % 2]
                    ga = GA[cnt % 3]
                    cnt += 1
                    S.dma('sp', ut, ut.ap[:], UT_d[e_], reads=[UT_r])
                    S.dma('sp', vb, vb.ap[:], Vb_d[e_], reads=[Vb_r])
                    S.dma('sp', gb, gb.ap[:], G_d[e_, :, t0 * 128:(t0 + TBT) * 128], reads=[G_r])
                    for kc in range(8):
                        S.op('pe', lambda e, kc=kc, ut=ut, pA=pA: e.matmul(pA.ap[:], ut.ap[:, kc, :], hTB.ap[:, kc, :], start=(kc == 0), stop=(kc == 7)),
                             r=[ut, hTB], w=[pA])
                    S.op('act', lambda e, ga=ga, pA=pA: e.activation(ga.ap[:], pA.ap[:], AF.Gelu), r=[pA], w=[ga])
                    S.op('dve', lambda e, ga=ga, gb=gb, wt=wt, ei=ei: e.tensor_tensor(wt.ap[:, ei, :], gb.ap[:], ga.ap[:], ALU.mult), r=[gb, ga], w=[wt])
                if eg >= 1:
                    vstage(eg - 1, eg - 1 == 0)
            vstage(NG - 1, False)
            for tt in range(TBT):
                t = t0 + tt
                hs, y = hs2[tt % 2], yb[tt % 2]
                S.dma('sp', hs, hs.ap[:], h_dram[t * 128:(t + 1) * 128, :], reads=[h_res])
                S.op('dve', lambda e, hs=hs, y=y, tt=tt: e.scalar_tensor_tensor(y.ap[:], hs.ap[:], float(ALPHA), acc.ap[:, tt, :], ALU.mult, ALU.add),
                     r=[hs, acc], w=[y])
                layer_norm_tile(S, y, stats, aggr, rstd, gt, bt, y)
                S.dma('sp', y, out_dram[t * 128:(t + 1) * 128, :], y.ap[:], store=True, writes=[out_res], final=final)
        S.barrier()


PRE = {}


def peer_prepass(S, nc, P, C, l, D):
    identb = C['identb']
    scr = D['scr']
    UT_d, Vb_d = scr['UT'], scr['Vb']
    PRE['UT_r'], PRE['Vb_r'] = S.dram("UT_r"), S.dram("Vb_r")
    with ExitStack() as es0:
        S.es = es0
        Ust = [S.sb("Ust%d" % i, [128, 1024], F32) for i in range(3)]
        Vst = [S.sb("Vst%d" % i, [128, 1024], F32) for i in range(3)]
        Ub = [S.sb("Ub%d" % i, [128, 1024], BF16) for i in range(2)]
        UTs = [S.sb("UTs%d" % i, [128, 8, 128], BF16) for i in range(3)]
        Vbs = [S.sb("Vbs%d" % i, [128, 1024], BF16) for i in range(3)]
        for e_ in range(NE):
            us, vs = Ust[e_ % 3], Vst[e_ % 3]
            ub, ut, vb = Ub[e_ % 2], UTs[e_ % 3], Vbs[e_ % 3]
            S.dma('sp', us, us.ap[:], D['peer_u'][l, e_ * 128:(e_ + 1) * 128, :])
            S.dma('sp', vs, vs.ap[:], D['peer_v'][l, e_ * 128:(e_ + 1) * 128, :])
            S.op('act', lambda e, ub=ub, us=us: e.activation(ub.ap[:], us.ap[:], AF.Copy), r=[us], w=[ub])
            S.op('dve', lambda e, vb=vb, vs=vs: e.tensor_copy(vb.ap[:], vs.ap[:]), r=[vs], w=[vb])
            S.dma('sp', vb, Vb_d[e_], vb.ap[:], store=True, writes=[PRE['Vb_r']])
            for kc in range(8):
                S.op('pe', lambda e, kc=kc, ub=ub: e.transpose(P.T.ap[:, kc * 128:(kc + 1) * 128], ub.ap[:, kc * 128:(kc + 1) * 128], identb.ap[:]),
                     r=[ub, identb], w=[P.T])
            S.op('dve' if e_ % 2 else 'act', (lambda e, ut=ut: e.tensor_copy(ut.ap[:].rearrange("p a b -> p (a b)"), P.T.ap[:])) if e_ % 2 else
                 (lambda e, ut=ut: e.activation(ut.ap[:].rearrange("p a b -> p (a b)"), P.T.ap[:], AF.Copy)), r=[P.T], w=[ut])
            S.dma('sp', ut, UT_d[e_], ut.ap[:], store=True, writes=[PRE['UT_r']])
        S.barrier()


PEER_FN[0] = peer_phase2
```

```python
import numpy as np
from contextlib import ExitStack
import concourse.bass as bass
import concourse.mybir as mybir
from concourse.bass_utils import run_bass_kernel_spmd

F32 = mybir.dt.float32
BF16 = mybir.dt.bfloat16
I32 = mybir.dt.int32
ALU = mybir.AluOpType
AF = mybir.ActivationFunctionType
AX = mybir.AxisListType

SEM_EPOCH = 1 << 30


class Buf:
    def __init__(self, name, handle, kind):
        self.name = name
        self.ap = handle
        self.kind = kind
        self.w = {}
        self.r = {}
        self.dsem = None
        self.dcnt = 0
        self.dlast = None


class Sched:
    def __init__(self, nc, es):
        self.nc = nc
        self.es = es
        self.es0 = es
        self.eng = {'pe': nc.tensor, 'act': nc.scalar, 'dve': nc.vector, 'pool': nc.gpsimd, 'sp': nc.sync}
        self.sem = {}
        self.cnt = {}
        self.known = {k: {} for k in self.eng}
        self.nsem = 0
        for k in ('pe', 'act', 'dve', 'pool'):
            self.sem[k] = self._newsem("e_" + k)
            self.cnt[k] = 0
        self.out_tickets = []
        self.dpool = []
        self.dbufs = []
        self.ninstr = {k: 0 for k in self.eng}

    def _newsem(self, name):
        self.nsem += 1
        return self.es0.enter_context(self.nc.semaphore("%s_%d" % (name, self.nsem)))

    def sb(self, name, shape, dtype):
        self.ntens = getattr(self, 'ntens', 0) + 1
        name = "%s_%d" % (name, self.ntens)
        h = self.es.enter_context(self.nc.sbuf_tensor(name, shape, dtype))
        return Buf(name, h, 'sb')

    def ps(self, name, shape, dtype):
        h = self.es.enter_context(self.nc.psum_tensor(name, shape, dtype))
        return Buf(name, h, 'ps')

    def dram(self, name):
        return Buf(name, None, 'dram')

    def _wait(self, ek, deps):
        e = self.eng[ek]
        kn = self.known[ek]
        for sem, val in deps.items():
            if kn.get(sem, 0) >= val:
                continue
            e.wait_ge(sem, val)
            self.ninstr[ek] += 1
            kn[sem] = val

    @staticmethod
    def _merge(d, sem, val):
        if d.get(sem, 0) < val:
            d[sem] = val

    @staticmethod
    def _bk(x):
        return x if isinstance(x, tuple) else (x, None)

    def _rdeps(self, deps, b, k):
        if k is None:
            for d in b.w.values():
                for s_, v in d.items():
                    self._merge(deps, s_, v)
        else:
            for kk in (k, None):
                for s_, v in b.w.get(kk, {}).items():
                    self._merge(deps, s_, v)

    def _wdeps(self, deps, b, k):
        self._rdeps(deps, b, k)
        if k is None:
            for d in b.r.values():
                for s_, v in d.items():
                    self._merge(deps, s_, v)
        else:
            for kk in (k, None):
                for s_, v in b.r.get(kk, {}).items():
                    self._merge(deps, s_, v)

    def _rmark(self, b, k, tk):
        self._merge(b.r.setdefault(k, {}), tk[0], tk[1])

    def _wmark(self, b, k, tk):
        if k is None:
            b.w = {None: {tk[0]: tk[1]}}
            b.r = {}
        else:
            b.w[k] = {tk[0]: tk[1]}
            b.r[k] = {}

    def op(self, ek, fn, r=(), w=()):
        r = [self._bk(x) for x in r]
        w = [self._bk(x) for x in w]
        if ek != 'pe':
            w = w + [x for x in r if x[0].kind == 'ps']
            r = [x for x in r if x[0].kind != 'ps']
        deps = {}
        own = self.sem[ek]
        for b, k in r:
            self._rdeps(deps, b, k)
        for b, k in w:
            self._wdeps(deps, b, k)
        if ek == 'pe':
            deps.pop(own, None)
        self._wait(ek, deps)
        ins = fn(self.eng[ek])
        self.ninstr[ek] += 1
        self.cnt[ek] += 1
        ins.then_inc(own, 1)
        tk = (own, self.cnt[ek])
        for b, k in r:
            self._rmark(b, k, tk)
        for b, k in w:
            self._wmark(b, k, tk)
        return ins

    def dma(self, qk, sbuf, out_ap, in_ap, reads=(), writes=(), store=False, part=False, final=False, **kw):
        deps = {}
        if store:
            self._rdeps(deps, sbuf, None)
        else:
            self._wdeps(deps, sbuf, None)
            if part and sbuf.dsem is not None:
                deps.pop(sbuf.dsem, None)
        for b in reads:
            self._rdeps(deps, b, None)
        for b in writes:
            self._wdeps(deps, b, None)
        if sbuf.dsem is None:
            if self.dpool:
                sbuf.dsem, sbuf.dcnt = self.dpool.pop()
            else:
                sbuf.dsem = self._newsem("dma")
                sbuf.dcnt = 0
            sbuf.dlast = None
            self.dbufs.append(sbuf)
        elif not part and sbuf.dlast is not None:
            self._merge(deps, sbuf.dlast[0], sbuf.dlast[1])
        self._wait(qk, deps)
        ins = self.eng[qk].dma_start(out=out_ap, in_=in_ap, **kw)
        self.ninstr[qk] += 1
        sbuf.dcnt += 16
        ins.then_inc(sbuf.dsem, 16)
        tk = (sbuf.dsem, sbuf.dcnt)
        sbuf.dlast = tk
        if store:
            self._rmark(sbuf, None, tk)
        else:
            if part:
                self._merge(sbuf.w.setdefault(None, {}), tk[0], tk[1])
                sbuf.r = {}
            else:
                self._wmark(sbuf, None, tk)
        for b in reads:
            self._rmark(b, None, tk)
        for b in writes:
            self._wmark(b, None, tk)
        if final:
            self.out_tickets.append(tk)
        return ins

    def finish(self):
        deps = {}
        for s, v in self.out_tickets:
            self._merge(deps, s, v)
        self._wait('sp', deps)

    def barrier(self, bufs=()):
        deps = {}
        for k in ('pe', 'act', 'dve', 'pool'):
            if self.cnt[k] > 0:
                self._merge(deps, self.sem[k], self.cnt[k])
        for b in self.dbufs:
            if b.dlast is not None:
                self._merge(deps, b.dlast[0], b.dlast[1])
        for k in ('pe', 'act', 'dve', 'pool', 'sp'):
            self._wait(k, dict(deps))
        for b in self.dbufs:
            self.dpool.append((b.dsem, b.dcnt))
            b.dsem = None
            b.dlast = None
        self.dbufs = []


D_MODEL = 1024
SEQ = 2048
NSEQ = 2
NTOK = NSEQ * SEQ
NT = NTOK // 128
DEPTH = 4
ALPHA = (2 * DEPTH) ** 0.25
EPS = 1e-5
IN_COLS = 2208
TBT = 4
NE = 128


def bcast_row(ap1d, n):
    return ap1d.partition_broadcast(128)


class HView:
    def __init__(self, h, lo, n):
        self.h, self.lo, self.n = h, lo, n

    def __getitem__(self, key):
        if not isinstance(key, tuple):
            key = (key, slice(None))
        pk, fk = key
        a = 0 if fk.start is None else fk.start
        b = self.n if fk.stop is None else fk.stop
        return self.h[pk, self.lo + a:self.lo + b]


class PS:
    def __init__(self, S):
        self.O2 = S.ps("psO2", [128, 1024], F32)
        self.A = [Buf("psA0", HView(self.O2.ap, 0, 512), 'ps'), Buf("psA1", HView(self.O2.ap, 512, 512), 'ps')]
        self.G = [S.ps("psG0", [128, 512], F32), S.ps("psG1", [128, 512], F32)]
        self.O = S.ps("psO", [128, 1024], F32)
        self.T = S.ps("psT", [128, 1024], BF16)
        self.X = S.ps("psX", [128, 512], F32)


def layer_norm_tile(S, y, stats, aggr, rstd, gt, bt, out):
    for c in range(2):
        S.op('dve', lambda e, c=c: e.bn_stats(stats.ap[:, c, :], y.ap[:, c * 512:(c + 1) * 512]), r=[y], w=[stats])
    S.op('dve', lambda e: e.bn_aggr(aggr.ap[:], stats.ap[:]), r=[stats], w=[aggr])
    S.op('act', lambda e: e.activation(rstd.ap[:], aggr.ap[:, 1:2], AF.Sqrt, bias=EPS_AP[0].ap[:, 0:1], scale=1.0), r=[aggr], w=[rstd])
    S.op('dve', lambda e: e.reciprocal(rstd.ap[:], rstd.ap[:]), r=[rstd], w=[rstd])
    S.op('dve', lambda e: e.tensor_scalar(y.ap[:], y.ap[:], aggr.ap[:, 0:1], rstd.ap[:, 0:1], ALU.subtract, ALU.mult), r=[y, aggr, rstd], w=[y])
    S.op('dve', lambda e: e.tensor_tensor(y.ap[:], y.ap[:], gt.ap[:], ALU.mult), r=[y, gt], w=[y])
    S.op('dve', lambda e: e.tensor_tensor(out.ap[:], y.ap[:], bt.ap[:], ALU.add), r=[y, bt], w=[out])


EPS_AP = [None]


def peer_phase(S, nc, P, C, l, h_dram, h_res, out_dram, out_res, D, final, ntiles=NT):
    identb = C['identb']
    scr = D['scr']
    hT_d, a_d, b_d, dg_d = scr['hT'], scr['a'], scr['b'], scr['dg']
    hT_r, a_r, b_r, dg_r = S.dram("hT_r"), S.dram("a_r"), S.dram("b_r"), S.dram("dg_r")
    with ExitStack() as es1:
        S.es = es1
        wq = S.sb("wq", [128, 8, 2048], BF16)
        wqst = [S.sb("wqst%d" % i, [128, 2048], F32) for i in range(2)]
        KT = S.sb("KT", [128, 16, 128], BF16)
        kst = [S.sb("kst%d" % i, [128, 128], F32) for i in range(2)]
        kb = [S.sb("kb%d" % i, [128, 128], BF16) for i in range(2)]
        hst = [S.sb("hst%d" % i, [128, 1024], F32) for i in range(2)]
        hb = S.sb("hb", [128, 1024], BF16)
        hTt = [S.sb("hTt%d" % i, [128, 8, 128], BF16) for i in range(2)]
        qT = S.sb("qT", [128, 16, 128], BF16)
        sc = S.sb("sc", [128, 16, 128], F32)
        mscr = S.sb("mscr", [128, 256], F32)
        top16 = S.sb("top16", [128, 16, 16], F32)
        cand = S.sb("cand", [128, 8, 256], F32)
        best = S.sb("best", [128, 8, 16], F32)
        sm = S.sb("sm", [128, 8, 8], F32)
        e16 = S.sb("e16", [128, 8, 16], F32)
        at = [S.sb("at%d" % i, [128, 8, 128], BF16) for i in range(2)]
        bt_ = [S.sb("bt%d" % i, [128, 8, 128], BF16) for i in range(2)]
        dgt = [S.sb("dgt%d" % i, [128, 8, 128], BF16) for i in range(2)]
        allb = wqst + kst + hst + hTt + at + bt_ + dgt
        for kc in range(8):
            st = wqst[kc % 2]
            S.dma('sp', st, st.ap[:], D['peer_w_q'][l, kc * 128:(kc + 1) * 128, :])
            S.op('pool' if kc % 2 else 'act', (lambda e, kc=kc, st=st: e.tensor_copy(wq.ap[:, kc, :], st.ap[:])) if kc % 2 else
                 (lambda e, kc=kc, st=st: e.activation(wq.ap[:, kc, :], st.ap[:], AF.Copy)), r=[st], w=[wq])
        for hp in range(16):
            st = kst[hp % 2]
            S.dma('sp', st, st.ap[:], D['peer_sub_keys'][l, hp // 2, hp % 2, :, :])
            S.op('dve', lambda e, st=st, hp=hp: e.tensor_copy(kb[hp % 2].ap[:], st.ap[:]), r=[st], w=[kb[hp % 2]])
            S.op('pe', lambda e, hp=hp: e.transpose(P.T.ap[:, 0:128], kb[hp % 2].ap[:], identb.ap[:]), r=[kb[hp % 2], identb], w=[P.T])
            S.op('dve', lambda e, hp=hp: e.tensor_copy(KT.ap[:, hp, :], P.T.ap[:, 0:128]), r=[P.T], w=[KT])
        for t in range(ntiles):
            hs = hst[t % 2]
            S.dma('sp', hs, hs.ap[:], h_dram[t * 128:(t + 1) * 128, :], reads=[h_res])
            S.op('act', lambda e, hs=hs: e.activation(hb.ap[:], hs.ap[:], AF.Copy), r=[hs], w=[hb])
            for kc in range(8):
                S.op('pe', lambda e, kc=kc: e.transpose(P.T.ap[:, kc * 128:(kc + 1) * 128], hb.ap[:, kc * 128:(kc + 1) * 128], identb.ap[:]),
                     r=[hb, identb], w=[P.T])
            ht = hTt[t % 2]
            S.op('dve', lambda e, ht=ht: e.tensor_copy(ht.ap[:].rearrange("p a b -> p (a b)"), P.T.ap[:]), r=[P.T], w=[ht])
            S.dma('sp', ht, hT_d[:, :, t * 128:(t + 1) * 128], ht.ap[:], store=True, writes=[hT_r])
            banks = [P.A[0], P.A[1], P.G[0], P.G[1]]
            for g in range(4):
                pb = banks[g]
                for j in range(4):
                    hp = g * 4 + j
                    for kc in range(8):
                        S.op('pe', lambda e, pb=pb, j=j, hp=hp, kc=kc: e.matmul(
                            pb.ap[:, j * 128:(j + 1) * 128], wq.ap[:, kc, hp * 128:(hp + 1) * 128], ht.ap[:, kc, :],
                            start=(kc == 0), stop=(kc == 7)), r=[wq, ht], w=[pb])
                S.op('act', lambda e, pb=pb, g=g: e.activation(qT.ap[:, g * 4:(g + 1) * 4, :].rearrange("p a b -> p (a b)"), pb.ap[:], AF.Copy),
                     r=[pb], w=[qT])
            for half in range(2):
                for j in range(8):
                    hp = half * 8 + j
                    S.op('pe', lambda e, j=j, hp=hp: e.matmul(P.O.ap[:, j * 128:(j + 1) * 128], qT.ap[:, hp, :], KT.ap[:, hp, :],
                                                              start=True, stop=True), r=[qT, KT], w=[P.O])
                S.op('dve', lambda e, half=half: e.tensor_copy(sc.ap[:, half * 8:(half + 1) * 8, :].rearrange("p a b -> p (a b)"), P.O.ap[:]),
                     r=[P.O], w=[sc])
            for hp in range(16):
                S.op('dve', lambda e, hp=hp: e.max(top16.ap[:, hp, 0:8], sc.ap[:, hp, :]), r=[sc], w=[top16])
                S.op('dve', lambda e, hp=hp: e.match_replace(mscr.ap[:, 0:128], top16.ap[:, hp, 0:8], sc.ap[:, hp, :], -1e30),
                     r=[sc, top16], w=[mscr])
                S.op('dve', lambda e, hp=hp: e.max(top16.ap[:, hp, 8:16], mscr.ap[:, 0:128]), r=[mscr], w=[top16])
            t4 = top16.ap[:].rearrange("p (h q) k -> p h q k", q=2)
            S.op('dve', lambda e: e.tensor_tensor(cand.ap[:].rearrange("p h (a b) -> p h a b", a=16),
                                                  t4[:, :, 0, :].unsqueeze(3).broadcast_to([128, 8, 16, 16]),
                                                  t4[:, :, 1, :].unsqueeze(2).broadcast_to([128, 8, 16, 16]), ALU.add),
                 r=[top16], w=[cand])
            for h in range(8):
                S.op('dve', lambda e, h=h: e.max(best.ap[:, h, 0:8], cand.ap[:, h, :]), r=[cand], w=[best])
                S.op('dve', lambda e, h=h: e.match_replace(mscr.ap[:], best.ap[:, h, 0:8], cand.ap[:, h, :], -1e30),
                     r=[cand, best], w=[mscr])
                S.op('dve', lambda e, h=h: e.max(best.ap[:, h, 8:16], mscr.ap[:]), r=[mscr], w=[best])
            S.op('dve', lambda e: e.tensor_tensor(e16.ap[:], best.ap[:], best.ap[:, :, 0:1].broadcast_to([128, 8, 16]), ALU.subtract),
                 r=[best], w=[e16])
            S.op('act', lambda e: e.activation(e16.ap[:], e16.ap[:], AF.Exp), r=[e16], w=[e16])
            S.op('dve', lambda e: e.reduce_sum(sm.ap[:, 1, :], e16.ap[:], axis=AX.X), r=[e16], w=[sm])
            S.op('dve', lambda e: e.reciprocal(sm.ap[:, 2, :], sm.ap[:, 1, :]), r=[sm], w=[sm])
            S.op('dve', lambda e: e.tensor_tensor(sm.ap[:, 3, :], e16.ap[:, :, 15], sm.ap[:, 2, :], ALU.mult), r=[e16, sm], w=[sm])
            S.op('dve', lambda e: e.tensor_tensor(sm.ap[:, 4, :], t4[:, :, 1, 0], best.ap[:, :, 15], ALU.subtract), r=[top16, best], w=[sm])
            S.op('dve', lambda e: e.tensor_scalar(sm.ap[:, 5, :], t4[:, :, 1, 0], -1.0, None, ALU.mult), r=[top16], w=[sm])
            a_t, b_t, d_t = at[t % 2], bt_[t % 2], dgt[t % 2]
            for h in range(8):
                S.op('act', lambda e, h=h: e.activation(a_t.ap[:, h, :], sc.ap[:, 2 * h, :], AF.Exp, bias=sm.ap[:, 4, h:h + 1], scale=1.0),
                     r=[sc, sm], w=[a_t])
                S.op('act', lambda e, h=h: e.activation(b_t.ap[:, h, :], sc.ap[:, 2 * h + 1, :], AF.Exp, bias=sm.ap[:, 5, h:h + 1], scale=1.0),
                     r=[sc, sm], w=[b_t])
                S.op('dve', lambda e, h=h: e.tensor_scalar(d_t.ap[:, h, :], identb.ap[:], sm.ap[:, 3, h:h + 1], None, ALU.mult),
                     r=[identb, sm], w=[d_t])
            S.dma('sp', a_t, a_d[t], a_t.ap[:], store=True, writes=[a_r])
            S.dma('sp', b_t, b_d[t], b_t.ap[:], store=True, writes=[b_r])
            S.dma('sp', d_t, dg_d[t], d_t.ap[:], store=True, writes=[dg_r])
        S.barrier(allb)
    with ExitStack() as es2:
        S.es = es2
        aB = S.sb("aB", [128, TBT, 8, 128], BF16)
        bB = S.sb("bB", [128, TBT, 8, 128], BF16)
        dB = S.sb("dB", [128, TBT, 8, 128], BF16)
        hTB = S.sb("hTB", [128, 8, TBT * 128], BF16)
        acc = S.sb("acc", [128, TBT, 1024], F32)
        Ust = [S.sb("Ust%d" % i, [128, 1024], F32) for i in range(3)]
        Vst = [S.sb("Vst%d" % i, [128, 1024], F32) for i in range(3)]
        Ub = [S.sb("Ub%d" % i, [128, 1024], BF16) for i in range(2)]
        UT = [S.sb("UT%d" % i, [128, 8, 128], BF16) for i in range(2)]
        Vb = [S.sb("Vb%d" % i, [128, 1024], BF16) for i in range(8)]
        WT = [S.sb("WT%d" % i, [128, 4, TBT * 128], BF16) for i in range(2)]
        Pm = [S.sb("Pm%d" % i, [128, 8, 128], BF16) for i in range(3)]
        Wh = [S.sb("Wh%d" % i, [128, 8, 128], BF16) for i in range(3)]
        GA = [S.sb("GA%d" % i, [128, TBT * 128], F32) for i in range(2)]
        hs2 = [S.sb("hs2_%d" % i, [128, 1024], F32) for i in range(2)]
        yb = [S.sb("yb%d" % i, [128, 1024], F32) for i in range(2)]
        gt = S.sb("gt", [128, 1024], F32)
        bt = S.sb("bt", [128, 1024], F32)
        stats = S.sb("stats", [128, 2, 6], F32)
        aggr = S.sb("aggr", [128, 2], F32)
        rstd = S.sb("rstd", [128, 1], F32)
        allb = [aB, bB, dB, hTB, gt, bt] + Ust + Vst + hs2 + yb
        S.dma('sp', gt, gt.ap[:], D['ln2_g'][l].partition_broadcast(128))
        S.dma('sp', bt, bt.ap[:], D['ln2_b'][l].partition_broadcast(128))
        nblk = ntiles // TBT
        cnt = 0
        for blk in range(nblk):
            t0 = blk * TBT
            for (bf, dd, rr) in ((aB, a_d, a_r), (bB, b_d, b_r), (dB, dg_d, dg_r)):
                S.dma('sp', bf, bf.ap[:], dd[t0:t0 + TBT].rearrange("t p h j -> p t h j"), reads=[rr])
            S.dma('sp', hTB, hTB.ap[:], hT_d[:, :, t0 * 128:(t0 + TBT) * 128], reads=[hT_r])
            for eg in range(NE // 4):
                wt = WT[eg % 2]
                for ei in range(4):
                    e_ = eg * 4 + ei
                    us, vs = Ust[cnt % 3], Vst[cnt % 3]
                    ub, ut = Ub[cnt % 2], UT[cnt % 2]
                    vb = Vb[(eg % 2) * 4 + ei]
                    pA, pG = P.A[cnt % 2], P.G[cnt % 2]
                    ga = GA[cnt % 2]
                    cnt += 1
                    S.dma('sp', us, us.ap[:], D['peer_u'][l, e_ * 128:(e_ + 1) * 128, :])
                    S.dma('sp', vs, vs.ap[:], D['peer_v'][l, e_ * 128:(e_ + 1) * 128, :])
                    S.op('act', lambda e, ub=ub, us=us: e.activation(ub.ap[:], us.ap[:], AF.Copy), r=[us], w=[ub])
                    S.op('act', lambda e, vb=vb, vs=vs: e.activation(vb.ap[:], vs.ap[:], AF.Copy), r=[vs], w=[vb])
                    for kc in range(8):
                        S.op('pe', lambda e, kc=kc, ub=ub: e.transpose(P.T.ap[:, kc * 128:(kc + 1) * 128], ub.ap[:, kc * 128:(kc + 1) * 128], identb.ap[:]),
                             r=[ub, identb], w=[P.T])
                    S.op('act', lambda e, ut=ut: e.activation(ut.ap[:].rearrange("p a b -> p (a b)"), P.T.ap[:], AF.Copy), r=[P.T], w=[ut])
                    for kc in range(8):
                        S.op('pe', lambda e, kc=kc, ut=ut, pA=pA: e.matmul(pA.ap[:], ut.ap[:, kc, :], hTB.ap[:, kc, :], start=(kc == 0), stop=(kc == 7)),
                             r=[ut, hTB], w=[pA])
                    for tt in range(TBT):
                        pm, wh = Pm[(cnt * TBT + tt) % 3], Wh[(cnt * TBT + tt) % 3]
                        S.op('pool', lambda e, pm=pm, tt=tt, e_=e_: e.tensor_tensor(
                            pm.ap[:], aB.ap[:, tt, :, e_:e_ + 1].broadcast_to([128, 8, 128]), bB.ap[:, tt, :, :], ALU.mult), r=[aB, bB], w=[pm])
                        S.op('dve', lambda e, pm=pm, wh=wh: e.scalar_tensor_tensor(wh.ap[:], pm.ap[:], 1.0, pm.ap[:], ALU.is_ge, ALU.mult), r=[pm], w=[wh])
                        for h in range(8):
                            S.op('pe', lambda e, h=h, tt=tt, wh=wh, pG=pG: e.matmul(pG.ap[:, tt * 128:(tt + 1) * 128], wh.ap[:, h, :], dB.ap[:, tt, h, :],
                                                                                    start=(h == 0), stop=(h == 7)), r=[wh, dB], w=[pG])
                    S.op('act', lambda e, ga=ga, pA=pA: e.activation(ga.ap[:], pA.ap[:], AF.Gelu), r=[pA], w=[ga])
                    S.op('dve', lambda e, ga=ga, pG=pG, wt=wt, ei=ei: e.tensor_tensor(wt.ap[:, ei, :], pG.ap[:], ga.ap[:], ALU.mult), r=[pG, ga], w=[wt])
                for tt in range(TBT):
                    for half in range(2):
                        for ei in range(4):
                            vb = Vb[(eg % 2) * 4 + ei]
                            S.op('pe', lambda e, tt=tt, half=half, ei=ei, vb=vb, wt=wt: e.matmul(
                                P.O.ap[:, half * 512:(half + 1) * 512], wt.ap[:, ei, tt * 128:(tt + 1) * 128], vb.ap[:, half * 512:(half + 1) * 512],
                                start=(ei == 0), stop=(ei == 3)), r=[wt, vb], w=[P.O])
                    if eg == 0:
                        S.op('dve', lambda e, tt=tt: e.tensor_copy(acc.ap[:, tt, :], P.O.ap[:]), r=[P.O], w=[acc])
                    else:
                        S.op('dve', lambda e, tt=tt: e.tensor_tensor(acc.ap[:, tt, :], acc.ap[:, tt, :], P.O.ap[:], ALU.add), r=[P.O, acc], w=[acc])
            for tt in range(TBT):
                t = t0 + tt
                hs, y = hs2[tt % 2], yb[tt % 2]
                S.dma('sp', hs, hs.ap[:], h_dram[t * 128:(t + 1) * 128, :], reads=[h_res])
                S.op('dve', lambda e, hs=hs, y=y, tt=tt: e.scalar_tensor_tensor(y.ap[:], hs.ap[:], float(ALPHA), acc.ap[:, tt, :], ALU.mult, ALU.add),
                     r=[hs, acc], w=[y])
                layer_norm_tile(S, y, stats, aggr, rstd, gt, bt, y)
                S.dma('sp', y, out_dram[t * 128:(t + 1) * 128, :], y.ap[:], store=True, writes=[out_res], final=final)
        S.barrier(allb)


def make_consts(S, nc, D):
    C = {}
    idf = S.sb("idf", [128, 128], F32)
    C['identb'] = S.sb("identb", [128, 128], BF16)
    S.dma('sp', idf, idf.ap[:], D['c_ident'][:, :])
    S.op('dve', lambda e: e.tensor_copy(C['identb'].ap[:], idf.ap[:]), r=[idf], w=[C['identb']])
    C['identf'] = idf
    C['iota'] = S.sb("iota", [128, 128], F32)
    S.dma('sp', C['iota'], C['iota'].ap[:], D['c_iota'][:, :])
    C['iotab'] = S.sb("iotab", [128, 128], BF16)
    S.op('dve', lambda e: e.tensor_copy(C['iotab'].ap[:], C['iota'].ap[:]), r=[C['iota']], w=[C['iotab']])
    eps = S.sb("eps", [128, 1], F32)
    S.op('dve', lambda e: e.memset(eps.ap[:], EPS), w=[eps])
    EPS_AP[0] = eps
    return C


def declare_scratch(nc, ntok=NTOK):
    nt = ntok // 128
    scr = {}
    scr['hT'] = nc.dram_tensor("s_hT", [128, 8, ntok], BF16).ap()
    scr['a'] = nc.dram_tensor("s_a", [nt, 128, 8, 128], BF16).ap()
    scr['b'] = nc.dram_tensor("s_b", [nt, 128, 8, 128], BF16).ap()
    scr['dg'] = nc.dram_tensor("s_dg", [nt, 128, 8, 128], BF16).ap()
    scr['G'] = nc.dram_tensor("s_G", [128, 128, ntok], BF16).ap()
    scr['UT'] = nc.dram_tensor("s_UT", [128, 128, 8, 128], BF16).ap()
    scr['Vb'] = nc.dram_tensor("s_Vb", [128, 128, 1024], BF16).ap()
    return scr


WSHAPES = {
    "w_in": (4, 1024, 2208), "sc_conv_w": (4, 3, 256), "mla_q_norm": (4, 256), "mla_kv_norm": (4, 128),
    "mla_w_uq": (4, 256, 384), "mla_w_uk": (4, 128, 256), "mla_w_uv": (4, 128, 256), "cf_dw_w": (4, 31, 256),
    "cf_dw_b": (4, 256), "cf_ln_g": (4, 256), "cf_ln_b": (4, 256), "swa_sinks": (4, 4), "mix_norm_g": (4, 1024),
    "w_out": (4, 1024, 1024), "ln1_g": (4, 1024), "ln1_b": (4, 1024), "peer_w_q": (4, 1024, 2048),
    "peer_sub_keys": (4, 8, 2, 128, 128), "peer_u": (4, 16384, 1024), "peer_v": (4, 16384, 1024),
    "ln2_g": (4, 1024), "ln2_b": (4, 1024),
}


PEER_FN = [None]


def build_peer_test(ntiles=4):
    nc = bass.Bass("TRN2", target_bir_lowering=False)
    D = {}
    D['x'] = nc.dram_tensor("x", [ntiles * 128, 1024], F32, kind="ExternalInput").ap()
    D['c_ident'] = nc.dram_tensor("c_ident", [128, 128], F32, kind="ExternalInput").ap()
    D['c_iota'] = nc.dram_tensor("c_iota", [128, 128], F32, kind="ExternalInput").ap()
    for k in ("peer_w_q", "peer_sub_keys", "peer_u", "peer_v", "ln2_g", "ln2_b"):
        D[k] = nc.dram_tensor(k, list(WSHAPES[k]), F32, kind="ExternalInput").ap()
    out = nc.dram_tensor("out", [ntiles * 128, 1024], F32, kind="ExternalOutput").ap()
    D['scr'] = declare_scratch(nc, ntiles * 128)
    with ExitStack() as es:
        S = Sched(nc, es)
        P = PS(S)
        C = make_consts(S, nc, D)
        peer_prepass(S, nc, P, C, 0, D)
        PEER_FN[0](S, nc, P, C, 0, D['x'], S.dram("xr"), out, S.dram("outr"), D, True, ntiles=ntiles)
        S.finish()
        print("instr counts", S.ninstr, "sems", S.nsem)
    return nc


CH = 256
MIXDBG = [0]
MIXSKIP = set()
C_SCB, C_SCC, C_SCH, C_CQ, C_CKV, C_KR, C_CFA, C_CFG, C_SWQ, C_SWK, C_SWV = 0, 256, 512, 768, 1024, 1152, 1184, 1440, 1696, 1952, 2080
QSCALE = 96.0 ** -0.5
TWO_PI = 2.0 * np.pi
CW1 = 6.28125
CW2 = float(TWO_PI - CW1)


def mixer_phase(S, nc, P, C, l, x_dram, x_res, h_dram, h_res, D, nseq=NSEQ):
    identb, identf = C['identb'], C['identf']
    psr = [P.A[0], P.A[1], P.G[0], P.G[1], P.X]
    ctr = [0]

    def nextps():
        ctr[0] += 1
        return psr[ctr[0] % 5]

    with ExitStack() as esm:
        S.es = esm
        usc = S.sb("usc", [128, 2, 2 + SEQ], BF16)
        ucf = S.sb("ucf", [128, 2, 30 + SEQ], BF16)
        scb = S.sb("scb", [128, 16, 256], BF16)
        swq = S.sb("swq", [128, 2, SEQ], BF16)
        swk = S.sb("swk", [128, 2, SEQ], BF16)
        swv = S.sb("swv", [128, 16, 2, 66], BF16)
        mlav = S.sb("mlav", [128, 16, 4, 66], BF16)
        q96 = [S.sb("q96_%d" % h, [96, SEQ], BF16) for h in range(4)]
        k96 = [S.sb("k96_%d" % h, [96, SEQ], BF16) for h in range(4)]
        onesb = S.sb("onesb", [128, 128], BF16)
        S.op('dve', lambda e: e.memset(onesb.ap[:], 1.0), w=[onesb])
        for s in range(nseq):
            with ExitStack() as es1:
                S.es = es1
                win = S.sb("win", [128, 8, IN_COLS], BF16)
                wst = S.sb("wst", [128, IN_COLS], F32)
                wswk = S.sb("wswk", [128, 8, 256], BF16)
                wkr = S.sb("wkr", [128, 8, 96], BF16)
                wkrr = S.sb("wkrr", [128, 8, 96], BF16)
                wuq = S.sb("wuq", [128, 2, 384], BF16)
                wuqr = S.sb("wuqr", [128, 2, 384], BF16)
                wuk = S.sb("wuk", [128, 256], BF16)
                wuv = S.sb("wuv", [128, 256], BF16)
                sst = S.sb("sst", [128, 2, 384], F32)
                gq = S.sb("gq", [128, 2], F32)
                gkv = S.sb("gkv", [128, 1], F32)
                invf = S.sb("invf", [96, 1], F32)
                xs = S.sb("xs", [128, 1024], F32)
                xb = S.sb("xb", [128, 1024], BF16)
                xTc = S.sb("xTc", [128, 8, CH], BF16)
                tA = [S.sb("tA%d" % i, [128, CH], BF16) for i in range(2)]
                cqf = S.sb("cqf", [128, 2, CH], F32)
                cqs = S.sb("cqs", [128, 2, CH], BF16)
                rq = S.sb("rq", [128, CH], F32)
                cqn = S.sb("cqn", [128, 2, CH], BF16)
                ckvf = S.sb("ckvf", [128, CH], F32)
                ckvs = S.sb("ckvs", [128, CH], BF16)
                rkv = S.sb("rkv", [128, CH], F32)
                ckvn = S.sb("ckvn", [128, CH], BF16)
                posi = S.sb("posi", [96, CH], I32)
                ang = S.sb("ang", [96, CH], F32)
                ang2 = S.sb("ang2", [96, CH], F32)
                tm = S.sb("tm", [96, CH], F32)
                ki = S.sb("ki", [96, CH], I32)
                kf = S.sb("kf", [96, CH], F32)
                cosb = S.sb("cosb", [96, CH], F32)
                sinb = S.sb("sinb", [96, CH], F32)
                t1 = S.sb("t1", [96, CH], F32)
                t2 = S.sb("t2", [96, CH], F32)
                dmab = [wst, sst, gq, gkv, invf, xs, posi]
                R = slice(64, 96)
                for kc in range(8):
                    S.dma('sp', wst, wst.ap[:], D['w_in'][l, kc * 128:(kc + 1) * 128, :])
                    if kc % 2:
                        S.op('pool', lambda e, kc=kc: e.tensor_copy(win.ap[:, kc, :], wst.ap[:]), r=[wst], w=[win])
                    else:
                        S.op('act', lambda e, kc=kc: e.activation(win.ap[:, kc, :], wst.ap[:], AF.Copy), r=[wst], w=[win])
                for j in range(2):
                    S.op('dve', lambda e, j=j: e.tensor_copy(
                        wswk.ap[:, :, j * 128:(j + 1) * 128].rearrange("p k (a b) -> p k a b", a=2),
                        win.ap[:, :, C_SWK + j * 64:C_SWK + (j + 1) * 64].unsqueeze(2).broadcast_to([128, 8, 2, 64])), r=[win], w=[wswk])
                S.op('dve', lambda e: e.memset(wkr.ap[:], 0.0), w=[wkr])
                S.op('dve', lambda e: e.memset(wkrr.ap[:], 0.0), w=[wkrr])
                S.op('dve', lambda e: e.tensor_copy(wkr.ap[:, :, 64:96], win.ap[:, :, C_KR:C_KR + 32]), r=[win], w=[wkr])
                S.op('dve', lambda e: e.tensor_scalar(wkrr.ap[:, :, 64:80], win.ap[:, :, C_KR + 16:C_KR + 32], -1.0, None, ALU.mult), r=[win], w=[wkrr])
                S.op('dve', lambda e: e.tensor_copy(wkrr.ap[:, :, 80:96], win.ap[:, :, C_KR:C_KR + 16]), r=[win], w=[wkrr])
                S.dma('sp', sst, sst.ap[:], D['mla_w_uq'][l].rearrange("(c p) n -> p c n", p=128))
                S.op('dve', lambda e: e.tensor_copy(wuq.ap[:], sst.ap[:]), r=[sst], w=[wuq])
                S.op('dve', lambda e: e.memset(wuqr.ap[:], 0.0), w=[wuqr])
                wq4 = wuq.ap[:].rearrange("p c (h d) -> p c h d", h=4)
                wr4 = wuqr.ap[:].rearrange("p c (h d) -> p c h d", h=4)
                S.op('dve', lambda e: e.tensor_scalar(wr4[:, :, :, 64:80], wq4[:, :, :, 80:96], -1.0, None, ALU.mult), r=[wuq], w=[wuqr])
                S.op('dve', lambda e: e.tensor_copy(wr4[:, :, :, 80:96], wq4[:, :, :, 64:80]), r=[wuq], w=[wuqr])
                S.dma('sp', sst, sst.ap[:, 0, 0:256], D['mla_w_uk'][l])
                S.op('dve', lambda e: e.tensor_copy(wuk.ap[:], sst.ap[:, 0, 0:256]), r=[sst], w=[wuk])
                S.dma('sp', sst, sst.ap[:, 0, 0:256], D['mla_w_uv'][l])
                S.op('dve', lambda e: e.tensor_copy(wuv.ap[:], sst.ap[:, 0, 0:256]), r=[sst], w=[wuv])
                S.dma('sp', gq, gq.ap[:], D['mla_q_norm'][l].rearrange("(c p) -> p c", p=128), allow_slow_non_contiguous=True)
                S.dma('sp', gkv, gkv.ap[:], D['mla_kv_norm'][l].rearrange("(p o) -> p o", o=1))
                S.dma('sp', invf, invf.ap[:], D['c_invf'][:, :])
                S.op('dve', lambda e: e.memset(usc.ap[:, :, 0:2], 0.0), w=[usc])
                S.op('dve', lambda e: e.memset(ucf.ap[:, :, 0:30], 0.0), w=[ucf])
                S.op('dve', lambda e: e.memset(swv.ap[:, :, :, 64:65], 1.0), w=[swv])
                S.op('dve', lambda e: e.memset(mlav.ap[:, :, :, 64:65], 1.0), w=[mlav])

                def proj(lhs, M, nk, rhs, rbufs):
                    pb = nextps()
                    for kc in range(nk):
                        S.op('pe', lambda e, kc=kc: e.matmul(pb.ap[0:M, 0:CH], lhs(kc), rhs(kc), start=(kc == 0), stop=(kc == nk - 1)),
                             r=rbufs, w=[pb])
                    return pb

                def range_sin(src, dst):
                    S.op('dve', lambda e: e.tensor_scalar(tm.ap[R, :], src.ap[R, :], 1.0 / TWO_PI, None, ALU.mult), r=[src], w=[tm])
                    S.op('dve', lambda e: e.tensor_copy(ki.ap[R, :], tm.ap[R, :]), r=[tm], w=[ki])
                    S.op('dve', lambda e: e.tensor_copy(kf.ap[R, :], ki.ap[R, :]), r=[ki], w=[kf])
                    S.op('dve', lambda e: e.scalar_tensor_tensor(tm.ap[R, :], kf.ap[R, :], -CW1, src.ap[R, :], ALU.mult, ALU.add), r=[kf, src], w=[tm])
                    S.op('dve', lambda e: e.scalar_tensor_tensor(tm.ap[R, :], kf.ap[R, :], -CW2, tm.ap[R, :], ALU.mult, ALU.add), r=[kf, tm], w=[tm])
                    S.op('dve', lambda e: e.tensor_scalar(kf.ap[R, :], tm.ap[R, :], float(np.pi), float(TWO_PI), ALU.is_gt, ALU.mult), r=[tm], w=[kf])
                    S.op('dve', lambda e: e.tensor_tensor(tm.ap[R, :], tm.ap[R, :], kf.ap[R, :], ALU.subtract), r=[tm, kf], w=[tm])
                    S.op('dve', lambda e: e.tensor_scalar(kf.ap[R, :], tm.ap[R, :], float(-np.pi), float(TWO_PI), ALU.is_lt, ALU.mult), r=[tm], w=[kf])
                    S.op('dve', lambda e: e.tensor_tensor(tm.ap[R, :], tm.ap[R, :], kf.ap[R, :], ALU.add), r=[tm, kf], w=[tm])
                    S.op('act', lambda e: e.activation(dst.ap[R, :], tm.ap[R, :], AF.Sin), r=[tm], w=[dst])

                xrhs = lambda kc: xTc.ap[:, kc, :]
                for c in range(0 if MIXDBG[0] == 10 else SEQ // CH):
                    cs = slice(c * CH, (c + 1) * CH)
                    for j in range(CH // 128):
                        t = s * 16 + c * (CH // 128) + j
                        S.dma('sp', xs, xs.ap[:], x_dram[t * 128:(t + 1) * 128, :], reads=[x_res])
                        S.op('act', lambda e: e.activation(xb.ap[:], xs.ap[:], AF.Copy), r=[xs], w=[xb])
                        for kc in range(8):
                            S.op('pe', lambda e, kc=kc: e.transpose(P.T.ap[:, kc * 128:(kc + 1) * 128], xb.ap[:, kc * 128:(kc + 1) * 128], identb.ap[:]),
                                 r=[xb, identb], w=[P.T])
                        S.op('dve', lambda e, j=j: e.tensor_copy(xTc.ap[:, :, j * 128:(j + 1) * 128], P.T.ap[:].rearrange("p (a b) -> p a b", a=8)),
                             r=[P.T], w=[xTc])
                    S.dma('sp', posi, posi.ap[R, :], D['positions'][s, c * CH:(c + 1) * CH].partition_broadcast(32))
                    S.op('dve', lambda e: e.tensor_copy(ang.ap[R, :], posi.ap[R, :]), r=[posi], w=[ang])
                    S.op('dve', lambda e: e.tensor_scalar(ang.ap[R, :], ang.ap[R, :], invf.ap[R, 0:1], None, ALU.mult), r=[ang, invf], w=[ang])
                    S.op('dve', lambda e: e.tensor_scalar(ang2.ap[R, :], ang.ap[R, :], float(np.pi / 2), None, ALU.add), r=[ang], w=[ang2])
                    range_sin(ang, sinb)
                    range_sin(ang2, cosb)
                    if MIXDBG[0] == 11:
                        continue
                    for cc in range(2):
                        pb = proj(lambda kc, cc=cc: win.ap[:, kc, C_SCC + cc * 128:C_SCC + (cc + 1) * 128], 128, 8, xrhs, [win, xTc])
                        S.op('act', lambda e, pb=pb, cc=cc: e.activation(tA[cc].ap[:], pb.ap[:, 0:CH], AF.Copy), r=[pb], w=[tA[cc]])
                        pb2 = proj(lambda kc, cc=cc: win.ap[:, kc, C_SCH + cc * 128:C_SCH + (cc + 1) * 128], 128, 8, xrhs, [win, xTc])
                        S.op('dve', lambda e, pb2=pb2, cc=cc: e.tensor_tensor(usc.ap[:, cc, 2 + c * CH:2 + (c + 1) * CH], pb2.ap[:, 0:CH], tA[cc].ap[:], ALU.mult),
                             r=[pb2, tA[cc]], w=[usc])
                    for cc in range(2):
                        pb = proj(lambda kc, cc=cc: win.ap[:, kc, C_CFG + cc * 128:C_CFG + (cc + 1) * 128], 128, 8, xrhs, [win, xTc])
                        S.op('act', lambda e, pb=pb, cc=cc: e.activation(tA[cc].ap[:], pb.ap[:, 0:CH], AF.Sigmoid), r=[pb], w=[tA[cc]])
                        pb2 = proj(lambda kc, cc=cc: win.ap[:, kc, C_CFA + cc * 128:C_CFA + (cc + 1) * 128], 128, 8, xrhs, [win, xTc])
                        S.op('dve', lambda e, pb2=pb2, cc=cc: e.tensor_tensor(ucf.ap[:, cc, 30 + c * CH:30 + (c + 1) * CH], pb2.ap[:, 0:CH], tA[cc].ap[:], ALU.mult),
                             r=[pb2, tA[cc]], w=[ucf])
                    if MIXDBG[0] == 12:
                        continue
                    for j in range(2):
                        pb = proj(lambda kc, j=j: win.ap[:, kc, C_SWQ + j * 128:C_SWQ + (j + 1) * 128], 128, 8, xrhs, [win, xTc])
                        S.op('act', lambda e, pb=pb, j=j: e.activation(swq.ap[:, j, cs], pb.ap[:, 0:CH], AF.Copy, scale=0.125), r=[pb], w=[swq])
                        pb2 = proj(lambda kc, j=j: wswk.ap[:, kc, j * 128:(j + 1) * 128], 128, 8, xrhs, [wswk, xTc])
                        S.op('dve', lambda e, pb2=pb2, j=j: e.tensor_copy(swk.ap[:, j, cs], pb2.ap[:, 0:CH]), r=[pb2], w=[swk])
                    if MIXDBG[0] == 13:
                        continue
                    for cc in range(2):
                        pb = proj(lambda kc, cc=cc: win.ap[:, kc, C_CQ + cc * 128:C_CQ + (cc + 1) * 128], 128, 8, xrhs, [win, xTc])
                        S.op('act', lambda e, pb=pb, cc=cc: e.activation(cqf.ap[:, cc, :], pb.ap[:, 0:CH], AF.Copy), r=[pb], w=[cqf])
                        S.op('dve', lambda e, pb=pb, cc=cc: e.tensor_tensor(cqs.ap[:, cc, :], pb.ap[:, 0:CH], cqf.ap[:, cc, :], ALU.mult), r=[pb, cqf], w=[cqs])
                    pb = proj(lambda cc: onesb.ap[:], 128, 2, lambda cc: cqs.ap[:, cc, :], [onesb, cqs])
                    S.op('act', lambda e, pb=pb: e.activation(rq.ap[:], pb.ap[:, 0:CH], AF.Sqrt, bias=EPS_AP[0].ap[:, 0:1], scale=1.0 / 256), r=[pb], w=[rq])
                    S.op('dve', lambda e: e.reciprocal(rq.ap[:], rq.ap[:]), r=[rq], w=[rq])
                    for cc in range(2):
                        S.op('dve', lambda e, cc=cc: e.scalar_tensor_tensor(cqn.ap[:, cc, :], cqf.ap[:, cc, :], gq.ap[:, cc:cc + 1], rq.ap[:], ALU.mult, ALU.mult),
                             r=[cqf, gq, rq], w=[cqn])
                    pb = proj(lambda kc: win.ap[:, kc, C_CKV:C_CKV + 128], 128, 8, xrhs, [win, xTc])
                    S.op('act', lambda e, pb=pb: e.activation(ckvf.ap[:], pb.ap[:, 0:CH], AF.Copy), r=[pb], w=[ckvf])
                    S.op('dve', lambda e, pb=pb: e.tensor_tensor(ckvs.ap[:], pb.ap[:, 0:CH], ckvf.ap[:], ALU.mult), r=[pb, ckvf], w=[ckvs])
                    pb = proj(lambda cc: onesb.ap[:], 128, 1, lambda cc: ckvs.ap[:], [onesb, ckvs])
                    S.op('act', lambda e, pb=pb: e.activation(rkv.ap[:], pb.ap[:, 0:CH], AF.Sqrt, bias=EPS_AP[0].ap[:, 0:1], scale=1.0 / 128), r=[pb], w=[rkv])
                    S.op('dve', lambda e: e.reciprocal(rkv.ap[:], rkv.ap[:]), r=[rkv], w=[rkv])
                    S.op('dve', lambda e: e.scalar_tensor_tensor(ckvn.ap[:], ckvf.ap[:], gkv.ap[:, 0:1], rkv.ap[:], ALU.mult, ALU.mult), r=[ckvf, gkv, rkv], w=[ckvn])
                    if MIXDBG[0] == 14:
                        continue
                    for h in range(4):
                        pb = proj(lambda cc, h=h: wuq.ap[:, cc, h * 96:(h + 1) * 96], 96, 2, lambda cc: cqn.ap[:, cc, :], [wuq, cqn])
                        pbr = proj(lambda cc, h=h: wuqr.ap[:, cc, h * 96:(h + 1) * 96], 96, 2, lambda cc: cqn.ap[:, cc, :], [wuqr, cqn])
                        S.op('act', lambda e, pb=pb, h=h: e.activation(q96[h].ap[0:64, cs], pb.ap[0:64, 0:CH], AF.Copy, scale=QSCALE), r=[pb], w=[q96[h]])
                        S.op('dve', lambda e, pb=pb: e.scalar_tensor_tensor(t1.ap[R, :], pb.ap[R, 0:CH], QSCALE, cosb.ap[R, :], ALU.mult, ALU.mult), r=[pb, cosb], w=[t1])
                        S.op('dve', lambda e, pbr=pbr: e.scalar_tensor_tensor(t2.ap[R, :], pbr.ap[R, 0:CH], QSCALE, sinb.ap[R, :], ALU.mult, ALU.mult), r=[pbr, sinb], w=[t2])
                        S.op('dve', lambda e, h=h: e.tensor_tensor(q96[h].ap[R, cs], t1.ap[R, :], t2.ap[R, :], ALU.add), r=[t1, t2], w=[q96[h]])
                    if MIXDBG[0] == 15:
                        continue
                    for h in range(4):
                        pb = proj(lambda kc, h=h: wuk.ap[:, h * 64:(h + 1) * 64], 64, 1, lambda kc: ckvn.ap[:], [wuk, ckvn])
                        S.op('act', lambda e, pb=pb, h=h: e.activation(k96[h].ap[0:64, cs], pb.ap[0:64, 0:CH], AF.Copy), r=[pb], w=[k96[h]])
                    pb = proj(lambda kc: wkr.ap[:, kc, :], 96, 8, xrhs, [wkr, xTc])
                    pbr = proj(lambda kc: wkrr.ap[:, kc, :], 96, 8, xrhs, [wkrr, xTc])
                    S.op('dve', lambda e, pb=pb: e.tensor_tensor(t1.ap[R, :], pb.ap[R, 0:CH], cosb.ap[R, :], ALU.mult), r=[pb, cosb], w=[t1])
                    S.op('dve', lambda e, pbr=pbr: e.tensor_tensor(t2.ap[R, :], pbr.ap[R, 0:CH], sinb.ap[R, :], ALU.mult), r=[pbr, sinb], w=[t2])
                    S.op('dve', lambda e: e.tensor_tensor(t1.ap[R, :], t1.ap[R, :], t2.ap[R, :], ALU.add), r=[t1, t2], w=[t1])
                    for h in range(4):
                        S.op('pool' if h % 2 else 'dve', lambda e, h=h: e.tensor_copy(k96[h].ap[R, cs], t1.ap[R, :]), r=[t1], w=[k96[h]])
                    if MIXDBG[0] == 16:
                        continue
                    for j in range(CH // 128):
                        tq = c * (CH // 128) + j
                        pb = nextps()
                        for kc in range(0 if MIXDBG[0] == 18 else 8):
                            S.op('pe', lambda e, kc=kc, j=j, pb=pb: e.matmul(pb.ap[:, 0:256], xTc.ap[:, kc, j * 128:(j + 1) * 128], win.ap[:, kc, C_SCB:C_SCB + 256],
                                                                             start=(kc == 0), stop=(kc == 7)), r=[xTc, win], w=[pb])
                        for kc in range(0 if MIXDBG[0] == 18 else 8):
                            S.op('pe', lambda e, kc=kc, j=j, pb=pb: e.matmul(pb.ap[:, 256:384], xTc.ap[:, kc, j * 128:(j + 1) * 128], win.ap[:, kc, C_SWV:C_SWV + 128],
                                                                             start=(kc == 0), stop=(kc == 7)), r=[xTc, win], w=[pb])
                        S.op('act', lambda e, pb=pb, tq=tq: e.activation(scb.ap[:, tq, :], pb.ap[:, 0:256], AF.Copy), r=[pb], w=[scb])
                        S.op('dve', lambda e, pb=pb, tq=tq: e.tensor_copy(swv.ap[:, tq, :, 0:64], pb.ap[:, 256:384].rearrange("p (a b) -> p a b", a=2)), r=[pb], w=[swv])
                        pb2 = nextps()
                        if MIXDBG[0] == 17:
                            continue
                        S.op('pe', lambda e, j=j, pb2=pb2: e.matmul(pb2.ap[:, 0:256], ckvn.ap[:, j * 128:(j + 1) * 128], wuv.ap[:], start=True, stop=True),
                             r=[ckvn, wuv], w=[pb2])
                        S.op('dve', lambda e, pb2=pb2, tq=tq: e.tensor_copy(mlav.ap[:, tq, :, 0:64], pb2.ap[:, 0:256].rearrange("p (a b) -> p a b", a=4)), r=[pb2], w=[mlav])
                S.barrier(dmab)
            if MIXDBG[0] == 1:
                continue
            with ExitStack() as es2:
                S.es = es2
                wout = S.sb("wout", [128, 8, 1024], BF16)
                wst2 = S.sb("wst2", [128, 1024], F32)
                scw = S.sb("scw", [128, 2, 3], F32)
                cfw = S.sb("cfw", [128, 2, 31], F32)
                dsc = S.sb("dsc", [128, 2, 3, 128], BF16)
                dcf = S.sb("dcf", [128, 2, 31, 128], BF16)
                ln1g = S.sb("ln1g", [128, 1024], F32)
                ln1b = S.sb("ln1b", [128, 1024], F32)
                mixg = S.sb("mixg", [128, 1024], F32)
                cfb = S.sb("cfb", [128, 256], F32)
                cfg_ = S.sb("cfg_", [128, 256], F32)
                cfbe = S.sb("cfbe", [128, 256], F32)
                alibi = S.sb("alibi", [128, 4, 2, 128], F32)
                trif = S.sb("trif", [128, 128], F32)
                trib = S.sb("trib", [128, 128], BF16)
                esink = S.sb("esink", [128, 4], F32)
                xs2 = [S.sb("xs2_%d" % i, [128, 1024], F32) for i in range(2)]
                mix = S.sb("mix", [128, 1024], F32)
                mix2 = S.sb("mix2", [128, 1024], F32)
                sqt = S.sb("sqt", [128, 1024], F32)
                mixb = S.sb("mixb", [128, 1024], BF16)
                mixT = S.sb("mixT", [128, 8, 128], BF16)
                yb = [S.sb("y%d" % i, [128, 1024], F32) for i in range(2)]
                PT = [S.sb("PT%d" % i, [128, 512], BF16) for i in range(3)]
                tS = [S.sb("tS%d" % i, [128, 256], F32) for i in range(2)]
                cft = S.sb("cft", [128, 256], F32)
                stats = S.sb("stats", [128, 2, 6], F32)
                aggr = S.sb("aggr", [128, 2], F32)
                rstd = S.sb("rstd", [128, 1], F32)
                st1 = S.sb("st1", [128, 6], F32)
                ag1 = S.sb("ag1", [128, 2], F32)
                rs1 = S.sb("rs1", [128, 1], F32)
                ssq = S.sb("ssq", [128, 16], F32)
                rec4 = S.sb("rec4", [128, 4], F32)
                dmab = [wst2, scw, cfw, ln1g, ln1b, mixg, cfb, cfg_, cfbe, alibi, trif, esink] + xs2 + yb
                for kc in range(8):
                    S.dma('sp', wst2, wst2.ap[:], D['w_out'][l, kc * 128:(kc + 1) * 128, :])
                    S.op('act', lambda e, kc=kc: e.activation(wout.ap[:, kc, :], wst2.ap[:], AF.Copy), r=[wst2], w=[wout])
                for cc in range(2):
                    S.dma('sp', scw, scw.ap[:, cc, :], D['sc_conv_w'][l][:, cc * 128:(cc + 1) * 128].rearrange("k p -> p k"), part=True, allow_slow_non_contiguous=True)
                    S.dma('sp', cfw, cfw.ap[:, cc, :], D['cf_dw_w'][l][:, cc * 128:(cc + 1) * 128].rearrange("k p -> p k"), part=True, allow_slow_non_contiguous=True)
                for cc in range(2):
                    for k in range(3):
                        S.op('dve', lambda e, cc=cc, k=k: e.tensor_scalar(dsc.ap[:, cc, k, :], identf.ap[:], scw.ap[:, cc, k:k + 1], None, ALU.mult), r=[identf, scw], w=[dsc])
                    for k in range(31):
                        S.op('dve', lambda e, cc=cc, k=k: e.tensor_scalar(dcf.ap[:, cc, k, :], identf.ap[:], cfw.ap[:, cc, k:k + 1], None, ALU.mult), r=[identf, cfw], w=[dcf])
                for (bf, key) in ((ln1g, 'ln1_g'), (ln1b, 'ln1_b'), (mixg, 'mix_norm_g'), (cfb, 'cf_dw_b'), (cfg_, 'cf_ln_g'), (cfbe, 'cf_ln_b'), (esink, 'swa_sinks')):
                    S.dma('sp', bf, bf.ap[:], D[key][l].partition_broadcast(128))
                S.op('act', lambda e: e.activation(esink.ap[:], esink.ap[:], AF.Exp), r=[esink], w=[esink])
                S.dma('sp', alibi, alibi.ap[:], D['c_alibi'][:, :, :, :])
                S.dma('sp', trif, trif.ap[:], D['c_tri'][:, :])
                S.op('dve', lambda e: e.tensor_copy(trib.ap[:], trif.ap[:]), r=[trif], w=[trib])
                ptcs = [0]
                mixes = [mix, mix2]

                def stageA(qb):
                    mix = mixes[qb % 2]
                    ptc = ptcs[0]
                    t = s * 16 + qb
                    qs = slice(qb * 128, (qb + 1) * 128)
                    if 'conv' in MIXSKIP:
                        S.op('dve', lambda e: e.memset(mix.ap[:], 0.0), w=[mix])
                    for cc in range(0 if 'conv' in MIXSKIP else 2):
                        for k in range(3):
                            S.op('pe', lambda e, cc=cc, k=k: e.matmul(P.X.ap[:, cc * 128:(cc + 1) * 128], usc.ap[:, cc, qb * 128 + k:qb * 128 + k + 128], dsc.ap[:, cc, k, :],
                                                                      start=(k == 0), stop=(k == 2)), r=[usc, dsc], w=[P.X])
                    for cc in range(0 if 'conv' in MIXSKIP else 2):
                        for k in range(31):
                            S.op('pe', lambda e, cc=cc, k=k: e.matmul(P.X.ap[:, 256 + cc * 128:256 + (cc + 1) * 128], ucf.ap[:, cc, qb * 128 + k:qb * 128 + k + 128], dcf.ap[:, cc, k, :],
                                                                      start=(k == 0), stop=(k == 30)), r=[ucf, dcf], w=[P.X])
                    if 'conv' not in MIXSKIP:
                      S.op('dve', lambda e: e.tensor_tensor(mix.ap[:, 0:256], P.X.ap[:, 0:256], scb.ap[:, qb, :], ALU.mult), r=[P.X, scb], w=[mix])
                    S.op('dve', lambda e: e.tensor_tensor(cft.ap[:], P.X.ap[:, 256:512], cfb.ap[:], ALU.add), r=[P.X, cfb], w=[cft])
                    S.op('dve', lambda e: e.bn_stats(st1.ap[:], cft.ap[:]), r=[cft], w=[st1])
                    S.op('dve', lambda e: e.bn_aggr(ag1.ap[:], st1.ap[:]), r=[st1], w=[ag1])
                    S.op('act', lambda e: e.activation(rs1.ap[:], ag1.ap[:, 1:2], AF.Sqrt, bias=EPS_AP[0].ap[:, 0:1], scale=1.0), r=[ag1], w=[rs1])
                    S.op('dve', lambda e: e.reciprocal(rs1.ap[:], rs1.ap[:]), r=[rs1], w=[rs1])
                    S.op('dve', lambda e: e.tensor_scalar(cft.ap[:], cft.ap[:], ag1.ap[:, 0:1], rs1.ap[:, 0:1], ALU.subtract, ALU.mult), r=[cft, ag1, rs1], w=[cft])
                    S.op('dve', lambda e: e.tensor_tensor(cft.ap[:], cft.ap[:], cfg_.ap[:], ALU.mult), r=[cft, cfg_], w=[cft])
                    S.op('dve', lambda e: e.tensor_tensor(cft.ap[:], cft.ap[:], cfbe.ap[:], ALU.add), r=[cft, cfbe], w=[cft])
                    S.op('act', lambda e: e.activation(mix.ap[:, 512:768], cft.ap[:], AF.Silu), r=[cft], w=[mix])
                    oP = P.G[0]
                    nk = qb + 1
                    for h in range(0 if 'mla' in MIXSKIP else 4):
                        for g0 in range(0, nk, 4):
                            kts = list(range(g0, min(g0 + 4, nk)))
                            pb = P.A[ptc % 2]
                            pt = PT[ptc % 3]
                            ptc += 1
                            for gi, kt in enumerate(kts):
                                S.op('pe', lambda e, gi=gi, kt=kt, pb=pb, h=h: e.matmul(pb.ap[:, gi * 128:(gi + 1) * 128], k96[h].ap[:, kt * 128:(kt + 1) * 128], q96[h].ap[:, qs],
                                                                                       start=True, stop=True), r=[k96[h], q96[h]], w=[pb])
                            n = len(kts) * 128
                            S.op('act', lambda e, pb=pb, pt=pt, n=n: e.activation(pt.ap[:, 0:n], pb.ap[:, 0:n], AF.Exp), r=[pb], w=[pt])
                            if kts[-1] == qb:
                                gi = len(kts) - 1
                                S.op('pool', lambda e, pt=pt, gi=gi: e.tensor_tensor(pt.ap[:, gi * 128:(gi + 1) * 128], pt.ap[:, gi * 128:(gi + 1) * 128], trib.ap[:], ALU.mult),
                                     r=[pt, trib], w=[pt])
                            for gi, kt in enumerate(kts):
                                S.op('pe', lambda e, gi=gi, kt=kt, pt=pt, h=h: e.matmul(oP.ap[:, h * 65:(h + 1) * 65], pt.ap[:, gi * 128:(gi + 1) * 128], mlav.ap[:, kt, h, 0:65],
                                                                                       start=(kt == 0), stop=(kt == nk - 1)), r=[pt, mlav], w=[oP])
                    o4 = oP.ap[:, 0:260].rearrange("p (h d) -> p h d", h=4)
                    if 'mla' not in MIXSKIP:
                      S.op('dve', lambda e: e.reciprocal(rec4.ap[:], o4[:, :, 64]), r=[oP], w=[rec4])
                    if 'mla' not in MIXSKIP:
                      S.op('dve', lambda e: e.tensor_tensor(mix.ap[:, 256:512].rearrange("p (h d) -> p h d", h=4), o4[:, :, 0:64],
                                                          rec4.ap[:].unsqueeze(2).broadcast_to([128, 4, 64]), ALU.mult), r=[oP, rec4], w=[mix])
                    oS = P.G[1]
                    kts = [kt for kt in (qb - 1, qb) if kt >= 0]
                    for h in range(0 if 'swa' in MIXSKIP else 4):
                        j, s_ = h // 2, h % 2
                        pr = slice(64 * s_, 64 * s_ + 64)
                        pb = P.A[ptc % 2]
                        pt = PT[ptc % 3]
                        ts_ = tS[ptc % 2]
                        ptc += 1
                        for gi, kt in enumerate(kts):
                            S.op('pe', lambda e, gi=gi, kt=kt, pb=pb, j=j, pr=pr: e.matmul(pb.ap[:, gi * 128:(gi + 1) * 128], swk.ap[pr, j, kt * 128:(kt + 1) * 128], swq.ap[pr, j, qs],
                                                                                          start=True, stop=True), r=[swk, swq], w=[pb])
                        n = len(kts) * 128
                        if len(kts) == 2:
                            bias_ap = alibi.ap[:, h, :, :].rearrange("p a b -> p (a b)")
                        else:
                            bias_ap = alibi.ap[:, h, 1, :]
                        S.op('dve', lambda e, pb=pb, ts_=ts_, n=n, bias_ap=bias_ap: e.tensor_tensor(ts_.ap[:, 0:n], pb.ap[:, 0:n], bias_ap, ALU.add), r=[pb, alibi], w=[ts_])
                        S.op('act', lambda e, pt=pt, ts_=ts_, n=n: e.activation(pt.ap[:, 0:n], ts_.ap[:, 0:n], AF.Exp), r=[ts_], w=[pt])
                        for gi, kt in enumerate(kts):
                            S.op('pe', lambda e, gi=gi, kt=kt, pt=pt, h=h, j=j: e.matmul(oS.ap[:, h * 65:(h + 1) * 65], pt.ap[:, gi * 128:(gi + 1) * 128], swv.ap[:, kt, j, 0:65],
                                                                                        start=(gi == 0), stop=(gi == len(kts) - 1)), r=[pt, swv], w=[oS])
                    s4 = oS.ap[:, 0:260].rearrange("p (h d) -> p h d", h=4)
                    if 'swa' not in MIXSKIP:
                      S.op('dve', lambda e: e.tensor_tensor(rec4.ap[:], s4[:, :, 64], esink.ap[:], ALU.add), r=[oS, esink], w=[rec4])
                      S.op('dve', lambda e: e.reciprocal(rec4.ap[:], rec4.ap[:]), r=[rec4], w=[rec4])
                    if 'swa' not in MIXSKIP:
                      S.op('dve', lambda e: e.tensor_tensor(mix.ap[:, 768:1024].rearrange("p (h d) -> p h d", h=4), s4[:, :, 0:64],
                                                          rec4.ap[:].unsqueeze(2).broadcast_to([128, 4, 64]), ALU.mult), r=[oS, rec4], w=[mix])
                    ptcs[0] = ptc

                def stageB(qb):
                    mix = mixes[qb % 2]
                    t = s * 16 + qb
                    S.op('act', lambda e: e.activation(sqt.ap[:], mix.ap[:], AF.Square), r=[mix], w=[sqt])
                    S.op('dve', lambda e: e.tensor_reduce(ssq.ap[:], sqt.ap[:].rearrange("p (a b) -> p a b", a=16), axis=AX.X, op=ALU.add), r=[sqt], w=[ssq])
                    S.op('act', lambda e: e.activation(ssq.ap[:], ssq.ap[:], AF.Sqrt, bias=EPS_AP[0].ap[:, 0:1], scale=1.0 / 64), r=[ssq], w=[ssq])
                    S.op('dve', lambda e: e.reciprocal(ssq.ap[:], ssq.ap[:]), r=[ssq], w=[ssq])
                    S.op('dve', lambda e: e.tensor_tensor(sqt.ap[:].rearrange("p (a b) -> p a b", a=16), mix.ap[:].rearrange("p (a b) -> p a b", a=16),
                                                          ssq.ap[:].unsqueeze(2).broadcast_to([128, 16, 64]), ALU.mult), r=[mix, ssq], w=[sqt])
                    S.op('dve', lambda e: e.tensor_tensor(mixb.ap[:], sqt.ap[:], mixg.ap[:], ALU.mult), r=[sqt, mixg], w=[mixb])
                    for kc in range(8):
                        S.op('pe', lambda e, kc=kc: e.transpose(P.T.ap[:, kc * 128:(kc + 1) * 128], mixb.ap[:, kc * 128:(kc + 1) * 128], identb.ap[:]),
                             r=[mixb, identb], w=[P.T])
                    S.op('act', lambda e: e.activation(mixT.ap[:].rearrange("p a b -> p (a b)"), P.T.ap[:], AF.Copy), r=[P.T], w=[mixT])
                    for half in range(2):
                        for kc in range(8):
                            S.op('pe', lambda e, half=half, kc=kc: e.matmul(P.O.ap[:, half * 512:(half + 1) * 512], mixT.ap[:, kc, :], wout.ap[:, kc, half * 512:(half + 1) * 512],
                                                                           start=(kc == 0), stop=(kc == 7)), r=[mixT, wout], w=[P.O])
                    xs_, y = xs2[qb % 2], yb[qb % 2]
                    S.dma('sp', xs_, xs_.ap[:], x_dram[t * 128:(t + 1) * 128, :], reads=[x_res])
                    S.op('dve', lambda e, xs_=xs_, y=y: e.scalar_tensor_tensor(y.ap[:], xs_.ap[:], float(ALPHA), P.O.ap[:], ALU.mult, ALU.add), r=[xs_, P.O], w=[y])
                    layer_norm_tile(S, y, stats, aggr, rstd, ln1g, ln1b, y)
                    S.dma('sp', y, h_dram[t * 128:(t + 1) * 128, :], y.ap[:], store=True, writes=[h_res])

                stageA(0)
                for qb in range(16):
                    if qb + 1 < 16:
                        stageA(qb + 1)
                    stageB(qb)
                S.barrier(dmab)
        S.barrier([])


def host_consts():
    c = {}
    c['c_ident'] = np.eye(128, dtype=np.float32)
    c['c_iota'] = np.tile(np.arange(128, dtype=np.float32)[None, :], (128, 1))
    k = np.arange(128)[:, None]
    q = np.arange(128)[None, :]
    c['c_tri'] = (k <= q).astype(np.float32)
    slopes = 2.0 ** (-8.0 * np.arange(1, 5, dtype=np.float64) / 4)
    al = np.zeros((128, 4, 2, 128), dtype=np.float32)
    NEG = -30000.0
    for h in range(4):
        dprev = (q + 128 - k).astype(np.float64)
        al[:, h, 0, :] = np.where(dprev < 128, -slopes[h] * dprev, NEG)
        dcur = (q - k).astype(np.float64)
        al[:, h, 1, :] = np.where(dcur >= 0, -slopes[h] * dcur, NEG)
    c['c_alibi'] = al
    inv = (10000.0 ** (-np.arange(16, dtype=np.float32) / np.float32(16))).astype(np.float32)
    f = np.zeros((96, 1), dtype=np.float32)
    f[64:96, 0] = np.concatenate([inv, inv])
    c['c_invf'] = f
    return c


CONST_SHAPES = {'c_ident': [128, 128], 'c_iota': [128, 128], 'c_tri': [128, 128], 'c_alibi': [128, 4, 2, 128], 'c_invf': [96, 1]}


def build_mixer_test(nseq=1):
    nc = bass.Bass("TRN2", target_bir_lowering=False)
    D = {}
    ntok = nseq * SEQ
    D['x'] = nc.dram_tensor("x", [ntok, 1024], F32, kind="ExternalInput").ap()
    D['positions'] = nc.dram_tensor("positions", [nseq, SEQ], I32, kind="ExternalInput").ap()
    for k, shp in CONST_SHAPES.items():
        D[k] = nc.dram_tensor(k, shp, F32, kind="ExternalInput").ap()
    for k in WSHAPES:
        if k.startswith("peer") or k.startswith("ln2"):
            continue
        D[k] = nc.dram_tensor(k, list(WSHAPES[k]), F32, kind="ExternalInput").ap()
    out = nc.dram_tensor("out", [ntok, 1024], F32, kind="ExternalOutput").ap()
    with ExitStack() as es:
        S = Sched(nc, es)
        P = PS(S)
        C = make_consts(S, nc, D)
        mixer_phase(S, nc, P, C, 0, D['x'], S.dram("xr"), out, S.dram("outr"), D, nseq=nseq)
        S.barrier([])
        print("instr counts", S.ninstr, "sems", S.nsem)
    return nc


def build_full(depth=DEPTH):
    nc = bass.Bass("TRN2", target_bir_lowering=False)
    D = {}
    D['x'] = nc.dram_tensor("x", [NTOK, 1024], F32, kind="ExternalInput").ap()
    D['positions'] = nc.dram_tensor("positions", [NSEQ, SEQ], I32, kind="ExternalInput").ap()
    for k, shp in CONST_SHAPES.items():
        D[k] = nc.dram_tensor(k, shp, F32, kind="ExternalInput").ap()
    for k in WSHAPES:
        D[k] = nc.dram_tensor(k, list(WSHAPES[k]), F32, kind="ExternalInput").ap()
    out = nc.dram_tensor("out", [NTOK, 1024], F32, kind="ExternalOutput").ap()
    hbuf = nc.dram_tensor("s_h", [NTOK, 1024], F32).ap()
    xbuf = nc.dram_tensor("s_x", [NTOK, 1024], F32).ap()
    D['scr'] = declare_scratch(nc, NTOK)
    with ExitStack() as es:
        S = Sched(nc, es)
        P = PS(S)
        C = make_consts(S, nc, D)
        x_res, h_res, xb_res, o_res = S.dram("x_res"), S.dram("h_res"), S.dram("xb_res"), S.dram("o_res")
        for l in range(depth):
            xin, xin_r = (D['x'], x_res) if l == 0 else (xbuf, xb_res)
            last = (l == depth - 1)
            xo, xo_r = (out, o_res) if last else (xbuf, xb_res)
            mixer_phase(S, nc, P, C, l, xin, xin_r, hbuf, h_res, D)
            peer_prepass(S, nc, P, C, l, D)
            PEER_FN[0](S, nc, P, C, l, hbuf, h_res, xo, xo_r, D, last)
        S.finish()
        print("instr counts", S.ninstr, "sems", S.nsem, flush=True)
    return nc


_NC_CACHE = {}


def kernel(**inputs):
    n = 8
    x = np.ascontiguousarray(np.asarray(inputs['x'], dtype=np.float32))
    pos = np.ascontiguousarray(np.asarray(inputs['positions'], dtype=np.int32))
    consts = host_consts()
    wts = {k: np.ascontiguousarray(np.asarray(inputs[k], dtype=np.float32)) for k in WSHAPES}
    if 'nc' not in _NC_CACHE:
        _NC_CACHE['nc'] = build_full()
    nc = _NC_CACHE['nc']
    in_maps = []
    for c in range(n):
        m = {'x': x[c * NSEQ:(c + 1) * NSEQ].reshape(NTOK, 1024), 'positions': pos[c * NSEQ:(c + 1) * NSEQ]}
        m.update(consts)
        m.update(wts)
        in_maps.append(m)
    res = run_bass_kernel_spmd(nc, in_maps, core_ids=list(range(n)))
    out = np.concatenate([r['out'].reshape(NSEQ, SEQ, 1024) for r in res.results], axis=0)
    return out.astype(np.float32)


U32 = mybir.dt.uint32


def peer_phase2(S, nc, P, C, l, h_dram, h_res, out_dram, out_res, D, final, ntiles=NT):
    identb = C['identb']
    iota = C['iota']
    iotab = C['iotab']
    scr = D['scr']
    hT_d, G_d = scr['hT'], scr['G']
    hT_r, G_r = S.dram("hT_r"), S.dram("G_r")
    with ExitStack() as es1:
        S.es = es1
        wq = S.sb("wq", [128, 8, 2048], BF16)
        wqst = [S.sb("wqst%d" % i, [128, 1024], F32) for i in range(2)]
        KT = S.sb("KT", [128, 16, 128], BF16)
        kst = [S.sb("kst%d" % i, [128, 128], F32) for i in range(2)]
        kb = [S.sb("kb%d" % i, [128, 128], BF16) for i in range(2)]
        hst = [S.sb("hst%d" % i, [128, 1024], F32) for i in range(2)]
        hb = S.sb("hb", [128, 1024], BF16)
        hTt = [S.sb("hTt%d" % i, [128, 8, 128], BF16) for i in range(2)]
        qT = [S.sb("qT%d" % i, [128, 16, 128], BF16) for i in range(2)]
        scs = [S.sb("sc%d" % i, [128, 16, 128], F32) for i in range(2)]
        mscr = S.sb("mscr", [128, 16, 128], F32)
        mscr2 = S.sb("mscr2", [128, 8, 256], F32)
        top16 = S.sb("top16", [128, 16, 16], F32)
        iu = S.sb("iu", [128, 16, 16], U32)
        idxf = S.sb("idxf", [128, 16, 16], F32)
        cand = S.sb("cand", [128, 8, 256], F32)
        best = S.sb("best", [128, 8, 16], F32)
        pu = S.sb("pu", [128, 8, 16], U32)
        posf = S.sb("posf", [128, 8, 16], F32)
        tmpf = S.sb("tmpf", [128, 8, 16], F32)
        ti = S.sb("ti", [128, 8, 16], I32)
        rif = S.sb("rif", [128, 8, 16], F32)
        rjf = S.sb("rjf", [128, 8, 16], F32)
        oh = [S.sb("oh%d" % i, [128, 8, 16, 16], BF16) for i in range(2)]
        sm = S.sb("sm", [128, 8, 8], F32)
        e16 = S.sb("e16", [128, 8, 16], F32)
        K3 = S.sb("K3", [128, 3, 128], F32)
        K3b = S.sb("K3b", [128, 3, 128], BF16)
        T3 = [S.sb("T3_%d" % i, [128, 3, 128], BF16) for i in range(2)]
        E0 = [S.sb("E0_%d" % i, [128, 8, 128], BF16) for i in range(2)]
        E1 = [S.sb("E1_%d" % i, [128, 8, 128], BF16) for i in range(2)]
        E2 = [S.sb("E2_%d" % i, [128, 8, 128], BF16) for i in range(2)]
        Gs = [S.sb("Gs%d" % i, [128, 128, 128], BF16) for i in range(2)]
        for kc2 in range(16):
            kc, hf = kc2 // 2, kc2 % 2
            st = wqst[kc2 % 2]
            S.dma('sp', st, st.ap[:], D['peer_w_q'][l, kc * 128:(kc + 1) * 128, hf * 1024:(hf + 1) * 1024])
            if kc2 % 2:
                S.op('pool', lambda e, kc=kc, hf=hf, st=st: e.tensor_copy(wq.ap[:, kc, hf * 1024:(hf + 1) * 1024], st.ap[:]), r=[st], w=[wq])
            else:
                S.op('act', lambda e, kc=kc, hf=hf, st=st: e.activation(wq.ap[:, kc, hf * 1024:(hf + 1) * 1024], st.ap[:], AF.Copy), r=[st], w=[wq])
        for hp in range(16):
            st = kst[hp % 2]
            S.dma('sp', st, st.ap[:], D['peer_sub_keys'][l, hp // 2, hp % 2, :, :])
            S.op('dve', lambda e, st=st, hp=hp: e.tensor_copy(kb[hp % 2].ap[:], st.ap[:]), r=[st], w=[kb[hp % 2]])
            S.op('pe', lambda e, hp=hp: e.transpose(P.T.ap[:, 0:128], kb[hp % 2].ap[:], identb.ap[:]), r=[kb[hp % 2], identb], w=[P.T])
            S.op('dve', lambda e, hp=hp: e.tensor_copy(KT.ap[:, hp, :], P.T.ap[:, 0:128]), r=[P.T], w=[KT])
        cnts = {'e': 0, 'p': 0}

        def front(t):
            hs, ht, qt, sc = hst[t % 2], hTt[t % 2], qT[t % 2], scs[t % 2]
            S.dma('sp', hs, hs.ap[:], h_dram[t * 128:(t + 1) * 128, :], reads=[h_res])
            S.op('act', lambda e: e.activation(hb.ap[:], hs.ap[:], AF.Copy), r=[hs], w=[hb])
            for kc in range(8):
                S.op('pe', lambda e, kc=kc: e.transpose(P.T.ap[:, kc * 128:(kc + 1) * 128], hb.ap[:, kc * 128:(kc + 1) * 128], identb.ap[:]),
                     r=[hb, identb], w=[P.T])
            S.op('act', lambda e: e.activation(ht.ap[:].rearrange("p a b -> p (a b)"), P.T.ap[:], AF.Copy), r=[P.T], w=[ht])
            S.dma('sp', ht, hT_d[:, :, t * 128:(t + 1) * 128], ht.ap[:], store=True, writes=[hT_r])
            banks = [P.G[0], P.G[1], P.X, P.G[0]]
            for g in range(4):
                pb = banks[g]
                for j in range(4):
                    hp = g * 4 + j
                    for kc in range(8):
                        S.op('pe', lambda e, pb=pb, j=j, hp=hp, kc=kc: e.matmul(
                            pb.ap[:, j * 128:(j + 1) * 128], wq.ap[:, kc, hp * 128:(hp + 1) * 128], ht.ap[:, kc, :],
                            start=(kc == 0), stop=(kc == 7)), r=[wq, ht], w=[pb])
                S.op('act', lambda e, pb=pb, g=g: e.activation(qt.ap[:, g * 4:(g + 1) * 4, :].rearrange("p a b -> p (a b)"), pb.ap[:], AF.Copy),
                     r=[pb], w=[(qt, g)])
            for g in range(4):
                pb = (P.G[1], P.X)[g % 2]
                for j in range(4):
                    hp = g * 4 + j
                    S.op('pe', lambda e, j=j, hp=hp, pb=pb: e.matmul(pb.ap[:, j * 128:(j + 1) * 128], qt.ap[:, hp, :], KT.ap[:, hp, :],
                                                                     start=True, stop=True), r=[(qt, hp // 4), KT], w=[pb])
                S.op('act', lambda e, g=g, pb=pb: e.activation(sc.ap[:, g * 4:(g + 1) * 4, :].rearrange("p a b -> p (a b)"), pb.ap[:], AF.Copy),
                     r=[pb], w=[(sc, g)])

        def topk(t):
            sc, t3 = scs[t % 2], T3[t % 2]
            for hp in range(16):
                S.op('dve', lambda e, hp=hp: e.max(top16.ap[:, hp, 0:8], sc.ap[:, hp, :]), r=[(sc, hp // 4)], w=[(top16, (hp, 0))])
            for hp in range(16):
                S.op('dve', lambda e, hp=hp: e.max_index(iu.ap[:, hp, 0:8], top16.ap[:, hp, 0:8], sc.ap[:, hp, :]), r=[(sc, hp // 4), (top16, (hp, 0))], w=[(iu, (hp, 0))])
            for hp in range(16):
                S.op('dve', lambda e, hp=hp: e.match_replace(mscr.ap[:, hp, :], top16.ap[:, hp, 0:8], sc.ap[:, hp, :], -1e30),
                     r=[(sc, hp // 4), (top16, (hp, 0))], w=[(mscr, hp)])
            for hp in range(16):
                S.op('dve', lambda e, hp=hp: e.max(top16.ap[:, hp, 8:16], mscr.ap[:, hp, :]), r=[(mscr, hp)], w=[(top16, (hp, 1))])
            for hp in range(16):
                S.op('dve', lambda e, hp=hp: e.max_index(iu.ap[:, hp, 8:16], top16.ap[:, hp, 8:16], mscr.ap[:, hp, :]), r=[(mscr, hp), (top16, (hp, 1))], w=[(iu, (hp, 1))])
            S.op('dve', lambda e: e.tensor_copy(idxf.ap[:], iu.ap[:]), r=[iu], w=[idxf])
            t4 = top16.ap[:].rearrange("p (h q) k -> p h q k", q=2)
            i4 = idxf.ap[:].rearrange("p (h q) k -> p h q k", q=2)
            S.op('dve', lambda e: e.tensor_tensor(cand.ap[:].rearrange("p h (a b) -> p h a b", a=16),
                                                  t4[:, :, 0, :].unsqueeze(3).broadcast_to([128, 8, 16, 16]),
                                                  t4[:, :, 1, :].unsqueeze(2).broadcast_to([128, 8, 16, 16]), ALU.add),
                 r=[top16], w=[cand])
            for h in range(8):
                S.op('dve', lambda e, h=h: e.max(best.ap[:, h, 0:8], cand.ap[:, h, :]), r=[cand], w=[(best, (h, 0))])
            for h in range(8):
                S.op('dve', lambda e, h=h: e.max_index(pu.ap[:, h, 0:8], best.ap[:, h, 0:8], cand.ap[:, h, :]), r=[cand, (best, (h, 0))], w=[(pu, (h, 0))])
            for h in range(8):
                S.op('dve', lambda e, h=h: e.match_replace(mscr2.ap[:, h, :], best.ap[:, h, 0:8], cand.ap[:, h, :], -1e30),
                     r=[cand, (best, (h, 0))], w=[(mscr2, h)])
            for h in range(8):
                S.op('dve', lambda e, h=h: e.max(best.ap[:, h, 8:16], mscr2.ap[:, h, :]), r=[(mscr2, h)], w=[(best, (h, 1))])
            for h in range(8):
                S.op('dve', lambda e, h=h: e.max_index(pu.ap[:, h, 8:16], best.ap[:, h, 8:16], mscr2.ap[:, h, :]), r=[(mscr2, h), (best, (h, 1))], w=[(pu, (h, 1))])
            S.op('dve', lambda e: e.tensor_copy(posf.ap[:], pu.ap[:]), r=[pu], w=[posf])
            S.op('dve', lambda e: e.tensor_tensor(e16.ap[:], best.ap[:], best.ap[:, :, 0:1].broadcast_to([128, 8, 16]), ALU.subtract),
                 r=[best], w=[e16])
            S.op('dve', lambda e: e.tensor_scalar(tmpf.ap[:], posf.ap[:], -7.5, 1.0 / 16, ALU.add, ALU.mult), r=[posf], w=[tmpf])
            S.op('act', lambda e: e.activation(e16.ap[:], e16.ap[:], AF.Exp), r=[e16], w=[e16])
            S.op('dve', lambda e: e.tensor_copy(ti.ap[:], tmpf.ap[:]), r=[tmpf], w=[ti])
            S.op('dve', lambda e: e.tensor_copy(rif.ap[:], ti.ap[:]), r=[ti], w=[rif])
            S.op('dve', lambda e: e.scalar_tensor_tensor(rjf.ap[:], rif.ap[:], -16.0, posf.ap[:], ALU.mult, ALU.add), r=[rif, posf], w=[rjf])
            io16 = iota.ap[:, 0:16].unsqueeze(1).unsqueeze(1).broadcast_to([128, 8, 16, 16])
            k3 = K3.ap[:].rearrange("p c (h k) -> p c h k", h=8)
            for q, rr in ((0, rif), (1, rjf)):
                S.op('dve', lambda e, rr=rr, q=q: e.tensor_tensor(oh[q].ap[:], rr.ap[:].unsqueeze(3).broadcast_to([128, 8, 16, 16]), io16, ALU.is_equal),
                     r=[rr, iota], w=[oh[q]])
            S.op('dve', lambda e: e.reduce_sum(sm.ap[:, 1, :], e16.ap[:], axis=AX.X), r=[e16], w=[(sm, 1)])
            for q in range(2):
                S.op('dve', lambda e, q=q: e.tensor_tensor(oh[q].ap[:], oh[q].ap[:], i4[:, :, q, :].unsqueeze(2).broadcast_to([128, 8, 16, 16]), ALU.mult),
                     r=[oh[q], idxf], w=[oh[q]])
            S.op('dve', lambda e: e.reciprocal(sm.ap[:, 2, :], sm.ap[:, 1, :]), r=[(sm, 1)], w=[(sm, 2)])
            for q in range(2):
                S.op('dve', lambda e, q=q: e.tensor_reduce(k3[:, q, :, :], oh[q].ap[:], axis=AX.X, op=ALU.add), r=[oh[q]], w=[(K3, q)])
            S.op('dve', lambda e: e.tensor_tensor(k3[:, 2, :, :], e16.ap[:], sm.ap[:, 2, :].unsqueeze(2).broadcast_to([128, 8, 16]), ALU.mult),
                 r=[e16, (sm, 2)], w=[(K3, 2)])
            S.op('dve', lambda e: e.tensor_copy(K3b.ap[:], K3.ap[:]), r=[K3], w=[K3b])
            for q in range(3):
                S.op('pe', lambda e, q=q: e.transpose(P.T.ap[:, q * 128:(q + 1) * 128], K3b.ap[:, q, :], identb.ap[:]), r=[K3b, identb], w=[P.T])
            S.op('act', lambda e: e.activation(t3.ap[:].rearrange("p a b -> p (a b)"), P.T.ap[:, 0:384], AF.Copy), r=[P.T], w=[t3])

        def onehot(t):
            t3 = T3[t % 2]
            gs = Gs[t % 2]
            NB = 8
            io_b = iotab.ap[:].unsqueeze(1).broadcast_to([128, NB, 128])
            for t0 in range(0, 128, NB):
                e0, e1, e2 = E0[cnts['e'] % 2], E1[cnts['e'] % 2], E2[cnts['e'] % 2]
                cnts['e'] += 1
                S.op('dve', lambda e, e0=e0, t0=t0: e.tensor_tensor(e0.ap[:], io_b, t3.ap[:, 0, t0:t0 + NB].unsqueeze(2).broadcast_to([128, NB, 128]), ALU.is_equal),
                     r=[iotab, t3], w=[e0])
                S.op('dve', lambda e, e1=e1, t0=t0: e.tensor_tensor(e1.ap[:], io_b, t3.ap[:, 1, t0:t0 + NB].unsqueeze(2).broadcast_to([128, NB, 128]), ALU.is_equal),
                     r=[iotab, t3], w=[e1])
                S.op('pool', lambda e, e1=e1, e2=e2, t0=t0: e.tensor_tensor(e2.ap[:], e1.ap[:], t3.ap[:, 2, t0:t0 + NB].unsqueeze(2).broadcast_to([128, NB, 128]), ALU.mult),
                     r=[e1, t3], w=[e2])
                pb = (P.O, P.O2)[cnts['p'] % 2]
                cnts['p'] += 1
                for tk in range(NB):
                    S.op('pe', lambda e, e0=e0, e2=e2, pb=pb, tk=tk: e.matmul(pb.ap[:, tk * 128:(tk + 1) * 128], e2.ap[:, tk, :], e0.ap[:, tk, :], start=True, stop=True),
                         r=[e0, e2], w=[pb])
                S.op('act', lambda e, pb=pb, gs=gs, t0=t0: e.activation(gs.ap[:, :, t0:t0 + NB], pb.ap[:].rearrange("p (t i) -> p i t", t=NB), AF.Copy),
                     r=[pb], w=[gs])
            S.dma('sp', gs, G_d[:, :, t * 128:(t + 1) * 128].rearrange("i j t -> j i t"), gs.ap[:], store=True, writes=[G_r])

        front(0)
        for t in range(ntiles):
            topk(t)
            if t + 1 < ntiles:
                front(t + 1)
            onehot(t)
        S.barrier()
    UT_d, Vb_d = scr['UT'], scr['Vb']
    UT_r, Vb_r = PRE['UT_r'], PRE['Vb_r']
    with ExitStack() as es2:
        S.es = es2
        TB = TBT * 128
        hTB = S.sb("hTB", [128, 8, TB], BF16)
        acc = S.sb("acc", [128, TBT, 1024], F32)
        Gb = [S.sb("Gb%d" % i, [128, TB], BF16) for i in range(4)]
        UT = [S.sb("UT%d" % i, [128, 8, 128], BF16) for i in range(4)]
        Vb = [S.sb("Vb%d" % i, [128, 1024], BF16) for i in range(12)]
        WT = [S.sb("WT%d" % i, [128, 4, TB], BF16) for i in range(3)]
        GA = [S.sb("GA%d" % i, [128, TB], BF16) for i in range(3)]
        hs2 = [S.sb("hs2_%d" % i, [128, 1024], F32) for i in range(2)]
        yb = [S.sb("yb%d" % i, [128, 1024], F32) for i in range(2)]
        gt = S.sb("gt", [128, 1024], F32)
        bt = S.sb("bt", [128, 1024], F32)
        stats = S.sb("stats", [128, 2, 6], F32)
        aggr = S.sb("aggr", [128, 2], F32)
        rstd = S.sb("rstd", [128, 1], F32)
        S.dma('sp', gt, gt.ap[:], D['ln2_g'][l].partition_broadcast(128))
        S.dma('sp', bt, bt.ap[:], D['ln2_b'][l].partition_broadcast(128))
        nblk = ntiles // TBT
        cnt = 0
        ocnt = 0
        NG = NE // 4

        def vstage(eg, first):
            nonlocal ocnt
            wt = WT[eg % 3]
            for tt in range(TBT):
                ob = ocnt % 2
                ocnt += 1
                for half in range(2):
                    po = (P.O if ob == 0 else P.G[half])
                    osl = slice(half * 512, (half + 1) * 512) if ob == 0 else slice(0, 512)
                    for ei in range(4):
                        vb = Vb[(eg % 3) * 4 + ei]
                        S.op('pe', lambda e, tt=tt, half=half, ei=ei, vb=vb, wt=wt, po=po, osl=osl: e.matmul(
                            po.ap[:, osl], wt.ap[:, ei, tt * 128:(tt + 1) * 128], vb.ap[:, half * 512:(half + 1) * 512],
                            start=(ei == 0), stop=(ei == 3)), r=[wt, vb], w=[po])
                if ob == 0:
                    if first:
                        S.op('dve', lambda e, tt=tt: e.tensor_copy(acc.ap[:, tt, :], P.O.ap[:]), r=[P.O], w=[acc])
                    else:
                        S.op('dve', lambda e, tt=tt: e.tensor_tensor(acc.ap[:, tt, :], acc.ap[:, tt, :], P.O.ap[:], ALU.add), r=[P.O, acc], w=[acc])
                else:
                    for half in range(2):
                        hsl = slice(half * 512, (half + 1) * 512)
                        if first:
                            S.op('dve', lambda e, tt=tt, half=half, hsl=hsl: e.tensor_copy(acc.ap[:, tt, hsl], P.G[half].ap[:]), r=[P.G[half]], w=[acc])
                        else:
                            S.op('dve', lambda e, tt=tt, half=half, hsl=hsl: e.tensor_tensor(acc.ap[:, tt, hsl], acc.ap[:, tt, hsl], P.G[half].ap[:], ALU.add),
                                 r=[P.G[half], acc], w=[acc])

        for blk in range(nblk):
            t0 = blk * TBT
            S.dma('sp', hTB, hTB.ap[:], hT_d[:, :, t0 * 128:(t0 + TBT) * 128], reads=[hT_r])
            for eg in range(NG):
                wt = WT[eg % 3]
                for ei in range(4):
                    e_ = eg * 4 + ei
                    gb = Gb[cnt % 4]
                    ut = UT[cnt % 4]
                    vb = Vb[(eg % 3) * 4 + ei]
                    pA = P.A[cnt % 2]
                    ga = GA[cnt % 3]
                    cnt += 1
                    S.dma('sp', ut, ut.ap[:], UT_d[e_], reads=[UT_r])
                    S.dma('sp', vb, vb.ap[:], Vb_d[e_], reads=[Vb_r])
                    S.dma('sp', gb, gb.ap[:], G_d[e_, :, t0 * 128:(t0 + TBT) * 128], reads=[G_r])
                    for kc in range(8):
                        S.op('pe', lambda e, kc=kc, ut=ut, pA=pA: e.matmul(pA.ap[:], ut.ap[:, kc, :], hTB.ap[:, kc, :], start=(kc == 0), stop=(kc == 7)),
                             r=[ut, hTB], w=[pA])
                    S.op('act', lambda e, ga=ga, pA=pA: e.activation(ga.ap[:], pA.ap[:], AF.Gelu), r=[pA], w=[ga])
                    S.op('dve', lambda e, ga=ga, gb=gb, wt=wt, ei=ei: e.tensor_tensor(wt.ap[:, ei, :], gb.ap[:], ga.ap[:], ALU.mult), r=[gb, ga], w=[wt])
                if eg >= 1:
                    vstage(eg - 1, eg - 1 == 0)
            vstage(NG - 1, False)
            for tt in range(TBT):
                t = t0 + tt
                hs, y = hs2[tt % 2], yb[tt % 2]
                S.dma('sp', hs, hs.ap[:], h_dram[t * 128:(t + 1) * 128, :], reads=[h_res])
                S.op('dve', lambda e, hs=hs, y=y, tt=tt: e.scalar_tensor_tensor(y.ap[:], hs.ap[:], float(ALPHA), acc.ap[:, tt, :], ALU.mult, ALU.add),
                     r=[hs, acc], w=[y])
                layer_norm_tile(S, y, stats, aggr, rstd, gt, bt, y)
                S.dma('sp', y, out_dram[t * 128:(t + 1) * 128, :], y.ap[:], store=True, writes=[out_res], final=final)
        S.barrier()


PRE = {}


def peer_prepass(S, nc, P, C, l, D):
    identb = C['identb']
    scr = D['scr']
    UT_d, Vb_d = scr['UT'], scr['Vb']
    PRE['UT_r'], PRE['Vb_r'] = S.dram("UT_r"), S.dram("Vb_r")
    with ExitStack() as es0:
        S.es = es0
        Ust = [S.sb("Ust%d" % i, [128, 1024], F32) for i in range(3)]
        Vst = [S.sb("Vst%d" % i, [128, 1024], F32) for i in range(3)]
        Ub = [S.sb("Ub%d" % i, [128, 1024], BF16) for i in range(2)]
        UTs = [S.sb("UTs%d" % i, [128, 8, 128], BF16) for i in range(3)]
        Vbs = [S.sb("Vbs%d" % i, [128, 1024], BF16) for i in range(3)]
        for e_ in range(NE):
            us, vs = Ust[e_ % 3], Vst[e_ % 3]
            ub, ut, vb = Ub[e_ % 2], UTs[e_ % 3], Vbs[e_ % 3]
            S.dma('sp', us, us.ap[:], D['peer_u'][l, e_ * 128:(e_ + 1) * 128, :])
            S.dma('sp', vs, vs.ap[:], D['peer_v'][l, e_ * 128:(e_ + 1) * 128, :])
            S.op('act', lambda e, ub=ub, us=us: e.activation(ub.ap[:], us.ap[:], AF.Copy), r=[us], w=[ub])
            S.op('dve', lambda e, vb=vb, vs=vs: e.tensor_copy(vb.ap[:], vs.ap[:]), r=[vs], w=[vb])
            S.dma('sp', vb, Vb_d[e_], vb.ap[:], store=True, writes=[PRE['Vb_r']])
            for kc in range(8):
                S.op('pe', lambda e, kc=kc, ub=ub: e.transpose(P.T.ap[:, kc * 128:(kc + 1) * 128], ub.ap[:, kc * 128:(kc + 1) * 128], identb.ap[:]),
                     r=[ub, identb], w=[P.T])
            S.op('dve' if e_ % 2 else 'act', (lambda e, ut=ut: e.tensor_copy(ut.ap[:].rearrange("p a b -> p (a b)"), P.T.ap[:])) if e_ % 2 else
                 (lambda e, ut=ut: e.activation(ut.ap[:].rearrange("p a b -> p (a b)"), P.T.ap[:], AF.Copy)), r=[P.T], w=[ut])
            S.dma('sp', ut, UT_d[e_], ut.ap[:], store=True, writes=[PRE['UT_r']])
        S.barrier()


PEER_FN[0] = peer_phase2
```

```python
import numpy as np
from contextlib import ExitStack
import concourse.bass as bass
import concourse.mybir as mybir
from concourse.bass_utils import run_bass_kernel_spmd

F32 = mybir.dt.float32
BF16 = mybir.dt.bfloat16
I32 = mybir.dt.int32
ALU = mybir.AluOpType
AF = mybir.ActivationFunctionType
AX = mybir.AxisListType

SEM_EPOCH = 1 << 30


class Buf:
    def __init__(self, name, handle, kind):
        self.name = name
        self.ap = handle
        self.kind = kind
        self.w = {}
        self.r = {}
        self.dsem = None
        self.dcnt = 0
        self.dlast = None


class Sched:
    def __init__(self, nc, es):
        self.nc = nc
        self.es = es
        self.es0 = es
        self.eng = {'pe': nc.tensor, 'act': nc.scalar, 'dve': nc.vector, 'pool': nc.gpsimd, 'sp': nc.sync}
        self.sem = {}
        self.cnt = {}
        self.known = {k: {} for k in self.eng}
        self.nsem = 0
        for k in ('pe', 'act', 'dve', 'pool'):
            self.sem[k] = self._newsem("e_" + k)
            self.cnt[k] = 0
        self.out_tickets = []
        self.dpool = []
        self.dbufs = []
        self.ninstr = {k: 0 for k in self.eng}

    def _newsem(self, name):
        self.nsem += 1
        return self.es0.enter_context(self.nc.semaphore("%s_%d" % (name, self.nsem)))

    def sb(self, name, shape, dtype):
        self.ntens = getattr(self, 'ntens', 0) + 1
        name = "%s_%d" % (name, self.ntens)
        h = self.es.enter_context(self.nc.sbuf_tensor(name, shape, dtype))
        return Buf(name, h, 'sb')

    def ps(self, name, shape, dtype):
        h = self.es.enter_context(self.nc.psum_tensor(name, shape, dtype))
        return Buf(name, h, 'ps')

    def dram(self, name):
        return Buf(name, None, 'dram')

    def _wait(self, ek, deps):
        e = self.eng[ek]
        kn = self.known[ek]
        for sem, val in deps.items():
            if kn.get(sem, 0) >= val:
                continue
            e.wait_ge(sem, val)
            self.ninstr[ek] += 1
            kn[sem] = val

    @staticmethod
    def _merge(d, sem, val):
        if d.get(sem, 0) < val:
            d[sem] = val

    @staticmethod
    def _bk(x):
        return x if isinstance(x, tuple) else (x, None)

    def _rdeps(self, deps, b, k):
        if k is None:
            for d in b.w.values():
                for s_, v in d.items():
                    self._merge(deps, s_, v)
        else:
            for kk in (k, None):
                for s_, v in b.w.get(kk, {}).items():
                    self._merge(deps, s_, v)

    def _wdeps(self, deps, b, k):
        self._rdeps(deps, b, k)
        if k is None:
            for d in b.r.values():
                for s_, v in d.items():
                    self._merge(deps, s_, v)
        else:
            for kk in (k, None):
                for s_, v in b.r.get(kk, {}).items():
                    self._merge(deps, s_, v)

    def _rmark(self, b, k, tk):
        self._merge(b.r.setdefault(k, {}), tk[0], tk[1])

    def _wmark(self, b, k, tk):
        if k is None:
            b.w = {None: {tk[0]: tk[1]}}
            b.r = {}
        else:
            b.w[k] = {tk[0]: tk[1]}
            b.r[k] = {}

    def op(self, ek, fn, r=(), w=()):
        r = [self._bk(x) for x in r]
        w = [self._bk(x) for x in w]
        if ek != 'pe':
            w = w + [x for x in r if x[0].kind == 'ps']
            r = [x for x in r if x[0].kind != 'ps']
        deps = {}
        own = self.sem[ek]
        for b, k in r:
            self._rdeps(deps, b, k)
        for b, k in w:
            self._wdeps(deps, b, k)
        if ek == 'pe':
            deps.pop(own, None)
        self._wait(ek, deps)
        ins = fn(self.eng[ek])
        self.ninstr[ek] += 1
        self.cnt[ek] += 1
        ins.then_inc(own, 1)
        tk = (own, self.cnt[ek])
        for b, k in r:
            self._rmark(b, k, tk)
        for b, k in w:
            self._wmark(b, k, tk)
        return ins

    def dma(self, qk, sbuf, out_ap, in_ap, reads=(), writes=(), store=False, part=False, final=False, **kw):
        deps = {}
        if store:
            self._rdeps(deps, sbuf, None)
        else:
            self._wdeps(deps, sbuf, None)
            if part and sbuf.dsem is not None:
                deps.pop(sbuf.dsem, None)
        for b in reads:
            self._rdeps(deps, b, None)
        for b in writes:
            self._wdeps(deps, b, None)
        if sbuf.dsem is None:
            if self.dpool:
                sbuf.dsem, sbuf.dcnt = self.dpool.pop()
            else:
                sbuf.dsem = self._newsem("dma")
                sbuf.dcnt = 0
            sbuf.dlast = None
            self.dbufs.append(sbuf)
        elif not part and sbuf.dlast is not None:
            self._merge(deps, sbuf.dlast[0], sbuf.dlast[1])
        self._wait(qk, deps)
        ins = self.eng[qk].dma_start(out=out_ap, in_=in_ap, **kw)
        self.ninstr[qk] += 1
        sbuf.dcnt += 16
        ins.then_inc(sbuf.dsem, 16)
        tk = (sbuf.dsem, sbuf.dcnt)
        sbuf.dlast = tk
        if store:
            self._rmark(sbuf, None, tk)
        else:
            if part:
                self._merge(sbuf.w.setdefault(None, {}), tk[0], tk[1])
                sbuf.r = {}
            else:
                self._wmark(sbuf, None, tk)
        for b in reads:
            self._rmark(b, None, tk)
        for b in writes:
            self._wmark(b, None, tk)
        if final:
            self.out_tickets.append(tk)
        return ins

    def finish(self):
        deps = {}
        for s, v in self.out_tickets:
            self._merge(deps, s, v)
        self._wait('sp', deps)

    def barrier(self, bufs=()):
        deps = {}
        for k in ('pe', 'act', 'dve', 'pool'):
            if self.cnt[k] > 0:
                self._merge(deps, self.sem[k], self.cnt[k])
        for b in self.dbufs:
            if b.dlast is not None:
                self._merge(deps, b.dlast[0], b.dlast[1])
        for k in ('pe', 'act', 'dve', 'pool', 'sp'):
            self._wait(k, dict(deps))
        for b in self.dbufs:
            self.dpool.append((b.dsem, b.dcnt))
            b.dsem = None
            b.dlast = None
        self.dbufs = []


D_MODEL = 1024
SEQ = 2048
NSEQ = 2
NTOK = NSEQ * SEQ
NT = NTOK // 128
DEPTH = 4
ALPHA = (2 * DEPTH) ** 0.25
EPS = 1e-5
IN_COLS = 2208
TBT = 4
NE = 128


def bcast_row(ap1d, n):
    return ap1d.partition_broadcast(128)


class HView:
    def __init__(self, h, lo, n):
        self.h, self.lo, self.n = h, lo, n

    def __getitem__(self, key):
        if not isinstance(key, tuple):
            key = (key, slice(None))
        pk, fk = key
        a = 0 if fk.start is None else fk.start
        b = self.n if fk.stop is None else fk.stop
        return self.h[pk, self.lo + a:self.lo + b]


class PS:
    def __init__(self, S):
        self.O2 = S.ps("psO2", [128, 1024], F32)
        self.A = [Buf("psA0", HView(self.O2.ap, 0, 512), 'ps'), Buf("psA1", HView(self.O2.ap, 512, 512), 'ps')]
        self.G = [S.ps("psG0", [128, 512], F32), S.ps("psG1", [128, 512], F32)]
        self.O = S.ps("psO", [128, 1024], F32)
        self.T = S.ps("psT", [128, 1024], BF16)
        self.X = S.ps("psX", [128, 512], F32)


def layer_norm_tile(S, y, stats, aggr, rstd, gt, bt, out):
    for c in range(2):
        S.op('dve', lambda e, c=c: e.bn_stats(stats.ap[:, c, :], y.ap[:, c * 512:(c + 1) * 512]), r=[y], w=[stats])
    S.op('dve', lambda e: e.bn_aggr(aggr.ap[:], stats.ap[:]), r=[stats], w=[aggr])
    S.op('act', lambda e: e.activation(rstd.ap[:], aggr.ap[:, 1:2], AF.Sqrt, bias=EPS_AP[0].ap[:, 0:1], scale=1.0), r=[aggr], w=[rstd])
    S.op('dve', lambda e: e.reciprocal(rstd.ap[:], rstd.ap[:]), r=[rstd], w=[rstd])
    S.op('dve', lambda e: e.tensor_scalar(y.ap[:], y.ap[:], aggr.ap[:, 0:1], rstd.ap[:, 0:1], ALU.subtract, ALU.mult), r=[y, aggr, rstd], w=[y])
    S.op('dve', lambda e: e.tensor_tensor(y.ap[:], y.ap[:], gt.ap[:], ALU.mult), r=[y, gt], w=[y])
    S.op('dve', lambda e: e.tensor_tensor(out.ap[:], y.ap[:], bt.ap[:], ALU.add), r=[y, bt], w=[out])


EPS_AP = [None]


def peer_phase(S, nc, P, C, l, h_dram, h_res, out_dram, out_res, D, final, ntiles=NT):
    identb = C['identb']
    scr = D['scr']
    hT_d, a_d, b_d, dg_d = scr['hT'], scr['a'], scr['b'], scr['dg']
    hT_r, a_r, b_r, dg_r = S.dram("hT_r"), S.dram("a_r"), S.dram("b_r"), S.dram("dg_r")
    with ExitStack() as es1:
        S.es = es1
        wq = S.sb("wq", [128, 8, 2048], BF16)
        wqst = [S.sb("wqst%d" % i, [128, 2048], F32) for i in range(2)]
        KT = S.sb("KT", [128, 16, 128], BF16)
        kst = [S.sb("kst%d" % i, [128, 128], F32) for i in range(2)]
        kb = [S.sb("kb%d" % i, [128, 128], BF16) for i in range(2)]
        hst = [S.sb("hst%d" % i, [128, 1024], F32) for i in range(2)]
        hb = S.sb("hb", [128, 1024], BF16)
        hTt = [S.sb("hTt%d" % i, [128, 8, 128], BF16) for i in range(2)]
        qT = S.sb("qT", [128, 16, 128], BF16)
        sc = S.sb("sc", [128, 16, 128], F32)
        mscr = S.sb("mscr", [128, 256], F32)
        top16 = S.sb("top16", [128, 16, 16], F32)
        cand = S.sb("cand", [128, 8, 256], F32)
        best = S.sb("best", [128, 8, 16], F32)
        sm = S.sb("sm", [128, 8, 8], F32)
        e16 = S.sb("e16", [128, 8, 16], F32)
        at = [S.sb("at%d" % i, [128, 8, 128], BF16) for i in range(2)]
        bt_ = [S.sb("bt%d" % i, [128, 8, 128], BF16) for i in range(2)]
        dgt = [S.sb("dgt%d" % i, [128, 8, 128], BF16) for i in range(2)]
        allb = wqst + kst + hst + hTt + at + bt_ + dgt
        for kc in range(8):
            st = wqst[kc % 2]
            S.dma('sp', st, st.ap[:], D['peer_w_q'][l, kc * 128:(kc + 1) * 128, :])
            S.op('pool' if kc % 2 else 'act', (lambda e, kc=kc, st=st: e.tensor_copy(wq.ap[:, kc, :], st.ap[:])) if kc % 2 else
                 (lambda e, kc=kc, st=st: e.activation(wq.ap[:, kc, :], st.ap[:], AF.Copy)), r=[st], w=[wq])
        for hp in range(16):
            st = kst[hp % 2]
            S.dma('sp', st, st.ap[:], D['peer_sub_keys'][l, hp // 2, hp % 2, :, :])
            S.op('dve', lambda e, st=st, hp=hp: e.tensor_copy(kb[hp % 2].ap[:], st.ap[:]), r=[st], w=[kb[hp % 2]])
            S.op('pe', lambda e, hp=hp: e.transpose(P.T.ap[:, 0:128], kb[hp % 2].ap[:], identb.ap[:]), r=[kb[hp % 2], identb], w=[P.T])
            S.op('dve', lambda e, hp=hp: e.tensor_copy(KT.ap[:, hp, :], P.T.ap[:, 0:128]), r=[P.T], w=[KT])
        for t in range(ntiles):
            hs = hst[t % 2]
            S.dma('sp', hs, hs.ap[:], h_dram[t * 128:(t + 1) * 128, :], reads=[h_res])
            S.op('act', lambda e, hs=hs: e.activation(hb.ap[:], hs.ap[:], AF.Copy), r=[hs], w=[hb])
            for kc in range(8):
                S.op('pe', lambda e, kc=kc: e.transpose(P.T.ap[:, kc * 128:(kc + 1) * 128], hb.ap[:, kc * 128:(kc + 1) * 128], identb.ap[:]),
                     r=[hb, identb], w=[P.T])
            ht = hTt[t % 2]
            S.op('dve', lambda e, ht=ht: e.tensor_copy(ht.ap[:].rearrange("p a b -> p (a b)"), P.T.ap[:]), r=[P.T], w=[ht])
            S.dma('sp', ht, hT_d[:, :, t * 128:(t + 1) * 128], ht.ap[:], store=True, writes=[hT_r])
            banks = [P.A[0], P.A[1], P.G[0], P.G[1]]
            for g in range(4):
                pb = banks[g]
                for j in range(4):
                    hp = g * 4 + j
                    for kc in range(8):
                        S.op('pe', lambda e, pb=pb, j=j, hp=hp, kc=kc: e.matmul(
                            pb.ap[:, j * 128:(j + 1) * 128], wq.ap[:, kc, hp * 128:(hp + 1) * 128], ht.ap[:, kc, :],
                            start=(kc == 0), stop=(kc == 7)), r=[wq, ht], w=[pb])
                S.op('act', lambda e, pb=pb, g=g: e.activation(qT.ap[:, g * 4:(g + 1) * 4, :].rearrange("p a b -> p (a b)"), pb.ap[:], AF.Copy),
                     r=[pb], w=[qT])
            for half in range(2):
                for j in range(8):
                    hp = half * 8 + j
                    S.op('pe', lambda e, j=j, hp=hp: e.matmul(P.O.ap[:, j * 128:(j + 1) * 128], qT.ap[:, hp, :], KT.ap[:, hp, :],
                                                              start=True, stop=True), r=[qT, KT], w=[P.O])
                S.op('dve', lambda e, half=half: e.tensor_copy(sc.ap[:, half * 8:(half + 1) * 8, :].rearrange("p a b -> p (a b)"), P.O.ap[:]),
                     r=[P.O], w=[sc])
            for hp in range(16):
                S.op('dve', lambda e, hp=hp: e.max(top16.ap[:, hp, 0:8], sc.ap[:, hp, :]), r=[sc], w=[top16])
                S.op('dve', lambda e, hp=hp: e.match_replace(mscr.ap[:, 0:128], top16.ap[:, hp, 0:8], sc.ap[:, hp, :], -1e30),
                     r=[sc, top16], w=[mscr])
                S.op('dve', lambda e, hp=hp: e.max(top16.ap[:, hp, 8:16], mscr.ap[:, 0:128]), r=[mscr], w=[top16])
            t4 = top16.ap[:].rearrange("p (h q) k -> p h q k", q=2)
            S.op('dve', lambda e: e.tensor_tensor(cand.ap[:].rearrange("p h (a b) -> p h a b", a=16),
                                                  t4[:, :, 0, :].unsqueeze(3).broadcast_to([128, 8, 16, 16]),
                                                  t4[:, :, 1, :].unsqueeze(2).broadcast_to([128, 8, 16, 16]), ALU.add),
                 r=[top16], w=[cand])
            for h in range(8):
                S.op('dve', lambda e, h=h: e.max(best.ap[:, h, 0:8], cand.ap[:, h, :]), r=[cand], w=[best])
                S.op('dve', lambda e, h=h: e.match_replace(mscr.ap[:], best.ap[:, h, 0:8], cand.ap[:, h, :], -1e30),
                     r=[cand, best], w=[mscr])
                S.op('dve', lambda e, h=h: e.max(best.ap[:, h, 8:16], mscr.ap[:]), r=[mscr], w=[best])
            S.op('dve', lambda e: e.tensor_tensor(e16.ap[:], best.ap[:], best.ap[:, :, 0:1].broadcast_to([128, 8, 16]), ALU.subtract),
                 r=[best], w=[e16])
            S.op('act', lambda e: e.activation(e16.ap[:], e16.ap[:], AF.Exp), r=[e16], w=[e16])
            S.op('dve', lambda e: e.reduce_sum(sm.ap[:, 1, :], e16.ap[:], axis=AX.X), r=[e16], w=[sm])
            S.op('dve', lambda e: e.reciprocal(sm.ap[:, 2, :], sm.ap[:, 1, :]), r=[sm], w=[sm])
            S.op('dve', lambda e: e.tensor_tensor(sm.ap[:, 3, :], e16.ap[:, :, 15], sm.ap[:, 2, :], ALU.mult), r=[e16, sm], w=[sm])
            S.op('dve', lambda e: e.tensor_tensor(sm.ap[:, 4, :], t4[:, :, 1, 0], best.ap[:, :, 15], ALU.subtract), r=[top16, best], w=[sm])
            S.op('dve', lambda e: e.tensor_scalar(sm.ap[:, 5, :], t4[:, :, 1, 0], -1.0, None, ALU.mult), r=[top16], w=[sm])
            a_t, b_t, d_t = at[t % 2], bt_[t % 2], dgt[t % 2]
            for h in range(8):
                S.op('act', lambda e, h=h: e.activation(a_t.ap[:, h, :], sc.ap[:, 2 * h, :], AF.Exp, bias=sm.ap[:, 4, h:h + 1], scale=1.0),
                     r=[sc, sm], w=[a_t])
                S.op('act', lambda e, h=h: e.activation(b_t.ap[:, h, :], sc.ap[:, 2 * h + 1, :], AF.Exp, bias=sm.ap[:, 5, h:h + 1], scale=1.0),
                     r=[sc, sm], w=[b_t])
                S.op('dve', lambda e, h=h: e.tensor_scalar(d_t.ap[:, h, :], identb.ap[:], sm.ap[:, 3, h:h + 1], None, ALU.mult),
                     r=[identb, sm], w=[d_t])
            S.dma('sp', a_t, a_d[t], a_t.ap[:], store=True, writes=[a_r])
            S.dma('sp', b_t, b_d[t], b_t.ap[:], store=True, writes=[b_r])
            S.dma('sp', d_t, dg_d[t], d_t.ap[:], store=True, writes=[dg_r])
        S.barrier(allb)
    with ExitStack() as es2:
        S.es = es2
        aB = S.sb("aB", [128, TBT, 8, 128], BF16)
        bB = S.sb("bB", [128, TBT, 8, 128], BF16)
        dB = S.sb("dB", [128, TBT, 8, 128], BF16)
        hTB = S.sb("hTB", [128, 8, TBT * 128], BF16)
        acc = S.sb("acc", [128, TBT, 1024], F32)
        Ust = [S.sb("Ust%d" % i, [128, 1024], F32) for i in range(3)]
        Vst = [S.sb("Vst%d" % i, [128, 1024], F32) for i in range(3)]
        Ub = [S.sb("Ub%d" % i, [128, 1024], BF16) for i in range(2)]
        UT = [S.sb("UT%d" % i, [128, 8, 128], BF16) for i in range(2)]
        Vb = [S.sb("Vb%d" % i, [128, 1024], BF16) for i in range(8)]
        WT = [S.sb("WT%d" % i, [128, 4, TBT * 128], BF16) for i in range(2)]
        Pm = [S.sb("Pm%d" % i, [128, 8, 128], BF16) for i in range(3)]
        Wh = [S.sb("Wh%d" % i, [128, 8, 128], BF16) for i in range(3)]
        GA = [S.sb("GA%d" % i, [128, TBT * 128], F32) for i in range(2)]
        hs2 = [S.sb("hs2_%d" % i, [128, 1024], F32) for i in range(2)]
        yb = [S.sb("yb%d" % i, [128, 1024], F32) for i in range(2)]
        gt = S.sb("gt", [128, 1024], F32)
        bt = S.sb("bt", [128, 1024], F32)
        stats = S.sb("stats", [128, 2, 6], F32)
        aggr = S.sb("aggr", [128, 2], F32)
        rstd = S.sb("rstd", [128, 1], F32)
        allb = [aB, bB, dB, hTB, gt, bt] + Ust + Vst + hs2 + yb
        S.dma('sp', gt, gt.ap[:], D['ln2_g'][l].partition_broadcast(128))
        S.dma('sp', bt, bt.ap[:], D['ln2_b'][l].partition_broadcast(128))
        nblk = ntiles // TBT
        cnt = 0
        for blk in range(nblk):
            t0 = blk * TBT
            for (bf, dd, rr) in ((aB, a_d, a_r), (bB, b_d, b_r), (dB, dg_d, dg_r)):
                S.dma('sp', bf, bf.ap[:], dd[t0:t0 + TBT].rearrange("t p h j -> p t h j"), reads=[rr])
            S.dma('sp', hTB, hTB.ap[:], hT_d[:, :, t0 * 128:(t0 + TBT) * 128], reads=[hT_r])
            for eg in range(NE // 4):
                wt = WT[eg % 2]
                for ei in range(4):
                    e_ = eg * 4 + ei
                    us, vs = Ust[cnt % 3], Vst[cnt % 3]
                    ub, ut = Ub[cnt % 2], UT[cnt % 2]
                    vb = Vb[(eg % 2) * 4 + ei]
                    pA, pG = P.A[cnt % 2], P.G[cnt % 2]
                    ga = GA[cnt % 2]
                    cnt += 1
                    S.dma('sp', us, us.ap[:], D['peer_u'][l, e_ * 128:(e_ + 1) * 128, :])
                    S.dma('sp', vs, vs.ap[:], D['peer_v'][l, e_ * 128:(e_ + 1) * 128, :])
                    S.op('act', lambda e, ub=ub, us=us: e.activation(ub.ap[:], us.ap[:], AF.Copy), r=[us], w=[ub])
                    S.op('act', lambda e, vb=vb, vs=vs: e.activation(vb.ap[:], vs.ap[:], AF.Copy), r=[vs], w=[vb])
                    for kc in range(8):
                        S.op('pe', lambda e, kc=kc, ub=ub: e.transpose(P.T.ap[:, kc * 128:(kc + 1) * 128], ub.ap[:, kc * 128:(kc + 1) * 128], identb.ap[:]),
                             r=[ub, identb], w=[P.T])
                    S.op('act', lambda e, ut=ut: e.activation(ut.ap[:].rearrange("p a b -> p (a b)"), P.T.ap[:], AF.Copy), r=[P.T], w=[ut])
                    for kc in range(8):
                        S.op('pe', lambda e, kc=kc, ut=ut, pA=pA: e.matmul(pA.ap[:], ut.ap[:, kc, :], hTB.ap[:, kc, :], start=(kc == 0), stop=(kc == 7)),
                             r=[ut, hTB], w=[pA])
                    for tt in range(TBT):
                        pm, wh = Pm[(cnt * TBT + tt) % 3], Wh[(cnt * TBT + tt) % 3]
                        S.op('pool', lambda e, pm=pm, tt=tt, e_=e_: e.tensor_tensor(
                            pm.ap[:], aB.ap[:, tt, :, e_:e_ + 1].broadcast_to([128, 8, 128]), bB.ap[:, tt, :, :], ALU.mult), r=[aB, bB], w=[pm])
                        S.op('dve', lambda e, pm=pm, wh=wh: e.scalar_tensor_tensor(wh.ap[:], pm.ap[:], 1.0, pm.ap[:], ALU.is_ge, ALU.mult), r=[pm], w=[wh])
                        for h in range(8):
                            S.op('pe', lambda e, h=h, tt=tt, wh=wh, pG=pG: e.matmul(pG.ap[:, tt * 128:(tt + 1) * 128], wh.ap[:, h, :], dB.ap[:, tt, h, :],
                                                                                    start=(h == 0), stop=(h == 7)), r=[wh, dB], w=[pG])
                    S.op('act', lambda e, ga=ga, pA=pA: e.activation(ga.ap[:], pA.ap[:], AF.Gelu), r=[pA], w=[ga])
                    S.op('dve', lambda e, ga=ga, pG=pG, wt=wt, ei=ei: e.tensor_tensor(wt.ap[:, ei, :], pG.ap[:], ga.ap[:], ALU.mult), r=[pG, ga], w=[wt])
                for tt in range(TBT):
                    for half in range(2):
                        for ei in range(4):
                            vb = Vb[(eg % 2) * 4 + ei]
                            S.op('pe', lambda e, tt=tt, half=half, ei=ei, vb=vb, wt=wt: e.matmul(
                                P.O.ap[:, half * 512:(half + 1) * 512], wt.ap[:, ei, tt * 128:(tt + 1) * 128], vb.ap[:, half * 512:(half + 1) * 512],
                                start=(ei == 0), stop=(ei == 3)), r=[wt, vb], w=[P.O])
                    if eg == 0:
                        S.op('dve', lambda e, tt=tt: e.tensor_copy(acc.ap[:, tt, :], P.O.ap[:]), r=[P.O], w=[acc])
                    else:
                        S.op('dve', lambda e, tt=tt: e.tensor_tensor(acc.ap[:, tt, :], acc.ap[:, tt, :], P.O.ap[:], ALU.add), r=[P.O, acc], w=[acc])
            for tt in range(TBT):
                t = t0 + tt
                hs, y = hs2[tt % 2], yb[tt % 2]
                S.dma('sp', hs, hs.ap[:], h_dram[t * 128:(t + 1) * 128, :], reads=[h_res])
                S.op('dve', lambda e, hs=hs, y=y, tt=tt: e.scalar_tensor_tensor(y.ap[:], hs.ap[:], float(ALPHA), acc.ap[:, tt, :], ALU.mult, ALU.add),
                     r=[hs, acc], w=[y])
                layer_norm_tile(S, y, stats, aggr, rstd, gt, bt, y)
                S.dma('sp', y, out_dram[t * 128:(t + 1) * 128, :], y.ap[:], store=True, writes=[out_res], final=final)
        S.barrier(allb)


def make_consts(S, nc, D):
    C = {}
    idf = S.sb("idf", [128, 128], F32)
    C['identb'] = S.sb("identb", [128, 128], BF16)
    S.dma('sp', idf, idf.ap[:], D['c_ident'][:, :])
    S.op('dve', lambda e: e.tensor_copy(C['identb'].ap[:], idf.ap[:]), r=[idf], w=[C['identb']])
    C['identf'] = idf
    C['iota'] = S.sb("iota", [128, 128], F32)
    S.dma('sp', C['iota'], C['iota'].ap[:], D['c_iota'][:, :])
    C['iotab'] = S.sb("iotab", [128, 128], BF16)
    S.op('dve', lambda e: e.tensor_copy(C['iotab'].ap[:], C['iota'].ap[:]), r=[C['iota']], w=[C['iotab']])
    eps = S.sb("eps", [128, 1], F32)
    S.op('dve', lambda e: e.memset(eps.ap[:], EPS), w=[eps])
    EPS_AP[0] = eps
    return C


def declare_scratch(nc, ntok=NTOK):
    nt = ntok // 128
    scr = {}
    scr['hT'] = nc.dram_tensor("s_hT", [128, 8, ntok], BF16).ap()
    scr['a'] = nc.dram_tensor("s_a", [nt, 128, 8, 128], BF16).ap()
    scr['b'] = nc.dram_tensor("s_b", [nt, 128, 8, 128], BF16).ap()
    scr['dg'] = nc.dram_tensor("s_dg", [nt, 128, 8, 128], BF16).ap()
    scr['G'] = nc.dram_tensor("s_G", [128, 128, ntok], BF16).ap()
    scr['UT'] = nc.dram_tensor("s_UT", [2, 128, 128, 8, 128], BF16).ap()
    scr['Vb'] = nc.dram_tensor("s_Vb", [2, 128, 128, 1024], BF16).ap()
    return scr


WSHAPES = {
    "w_in": (4, 1024, 2208), "sc_conv_w": (4, 3, 256), "mla_q_norm": (4, 256), "mla_kv_norm": (4, 128),
    "mla_w_uq": (4, 256, 384), "mla_w_uk": (4, 128, 256), "mla_w_uv": (4, 128, 256), "cf_dw_w": (4, 31, 256),
    "cf_dw_b": (4, 256), "cf_ln_g": (4, 256), "cf_ln_b": (4, 256), "swa_sinks": (4, 4), "mix_norm_g": (4, 1024),
    "w_out": (4, 1024, 1024), "ln1_g": (4, 1024), "ln1_b": (4, 1024), "peer_w_q": (4, 1024, 2048),
    "peer_sub_keys": (4, 8, 2, 128, 128), "peer_u": (4, 16384, 1024), "peer_v": (4, 16384, 1024),
    "ln2_g": (4, 1024), "ln2_b": (4, 1024),
}


PEER_FN = [None]


def build_peer_test(ntiles=4):
    nc = bass.Bass("TRN2", target_bir_lowering=False)
    D = {}
    D['x'] = nc.dram_tensor("x", [ntiles * 128, 1024], F32, kind="ExternalInput").ap()
    D['c_ident'] = nc.dram_tensor("c_ident", [128, 128], F32, kind="ExternalInput").ap()
    D['c_iota'] = nc.dram_tensor("c_iota", [128, 128], F32, kind="ExternalInput").ap()
    for k in ("peer_w_q", "peer_sub_keys", "peer_u", "peer_v", "ln2_g", "ln2_b"):
        D[k] = nc.dram_tensor(k, list(WSHAPES[k]), F32, kind="ExternalInput").ap()
    out = nc.dram_tensor("out", [ntiles * 128, 1024], F32, kind="ExternalOutput").ap()
    D['scr'] = declare_scratch(nc, ntiles * 128)
    with ExitStack() as es:
        S = Sched(nc, es)
        P = PS(S)
        C = make_consts(S, nc, D)
        peer_prepass(S, nc, P, C, 0, D)
        PEER_FN[0](S, nc, P, C, 0, D['x'], S.dram("xr"), out, S.dram("outr"), D, True, ntiles=ntiles)
        S.finish()
        print("instr counts", S.ninstr, "sems", S.nsem)
    return nc


CH = 256
MIXDBG = [0]
MIXSKIP = set()
C_SCB, C_SCC, C_SCH, C_CQ, C_CKV, C_KR, C_CFA, C_CFG, C_SWQ, C_SWK, C_SWV = 0, 256, 512, 768, 1024, 1152, 1184, 1440, 1696, 1952, 2080
QSCALE = 96.0 ** -0.5
TWO_PI = 2.0 * np.pi
CW1 = 6.28125
CW2 = float(TWO_PI - CW1)


def mixer_phase(S, nc, P, C, l, x_dram, x_res, h_dram, h_res, D, nseq=NSEQ):
    identb, identf = C['identb'], C['identf']
    psr = [P.A[0], P.A[1], P.G[0], P.G[1], P.X]
    ctr = [0]

    def nextps():
        ctr[0] += 1
        return psr[ctr[0] % 5]

    with ExitStack() as esm:
        S.es = esm
        usc = S.sb("usc", [128, 2, 2 + SEQ], BF16)
        ucf = S.sb("ucf", [128, 2, 30 + SEQ], BF16)
        scb = S.sb("scb", [128, 16, 256], BF16)
        swq = S.sb("swq", [128, 2, SEQ], BF16)
        swk = S.sb("swk", [128, 2, SEQ], BF16)
        swv = S.sb("swv", [128, 16, 2, 66], BF16)
        mlav = S.sb("mlav", [128, 16, 4, 66], BF16)
        q96 = [S.sb("q96_%d" % h, [96, SEQ], BF16) for h in range(4)]
        k96 = [S.sb("k96_%d" % h, [96, SEQ], BF16) for h in range(4)]
        onesb = S.sb("onesb", [128, 128], BF16)
        S.op('dve', lambda e: e.memset(onesb.ap[:], 1.0), w=[onesb])
        for s in range(nseq):
            with ExitStack() as es1:
                S.es = es1
                win = S.sb("win", [128, 8, IN_COLS], BF16)
                wst = S.sb("wst", [128, IN_COLS], F32)
                wswk = S.sb("wswk", [128, 8, 256], BF16)
                wkr = S.sb("wkr", [128, 8, 96], BF16)
                wkrr = S.sb("wkrr", [128, 8, 96], BF16)
                wuq = S.sb("wuq", [128, 2, 384], BF16)
                wuqr = S.sb("wuqr", [128, 2, 384], BF16)
                wuk = S.sb("wuk", [128, 256], BF16)
                wuv = S.sb("wuv", [128, 256], BF16)
                sst = S.sb("sst", [128, 2, 384], F32)
                gq = S.sb("gq", [128, 2], F32)
                gkv = S.sb("gkv", [128, 1], F32)
                invf = S.sb("invf", [96, 1], F32)
                xs = S.sb("xs", [128, 1024], F32)
                xb = S.sb("xb", [128, 1024], BF16)
                xTc = S.sb("xTc", [128, 8, CH], BF16)
                tA = [S.sb("tA%d" % i, [128, CH], BF16) for i in range(2)]
                cqf = S.sb("cqf", [128, 2, CH], F32)
                cqs = S.sb("cqs", [128, 2, CH], BF16)
                rq = S.sb("rq", [128, CH], F32)
                cqn = S.sb("cqn", [128, 2, CH], BF16)
                ckvf = S.sb("ckvf", [128, CH], F32)
                ckvs = S.sb("ckvs", [128, CH], BF16)
                rkv = S.sb("rkv", [128, CH], F32)
                ckvn = S.sb("ckvn", [128, CH], BF16)
                posi = S.sb("posi", [96, CH], I32)
                ang = S.sb("ang", [96, CH], F32)
                ang2 = S.sb("ang2", [96, CH], F32)
                tm = S.sb("tm", [96, CH], F32)
                ki = S.sb("ki", [96, CH], I32)
                kf = S.sb("kf", [96, CH], F32)
                cosb = S.sb("cosb", [96, CH], F32)
                sinb = S.sb("sinb", [96, CH], F32)
                t1 = S.sb("t1", [96, CH], F32)
                t2 = S.sb("t2", [96, CH], F32)
                dmab = [wst, sst, gq, gkv, invf, xs, posi]
                R = slice(64, 96)
                for kc in range(8):
                    S.dma('sp', wst, wst.ap[:], D['w_in'][l, kc * 128:(kc + 1) * 128, :])
                    if kc % 2:
                        S.op('pool', lambda e, kc=kc: e.tensor_copy(win.ap[:, kc, :], wst.ap[:]), r=[wst], w=[win])
                    else:
                        S.op('act', lambda e, kc=kc: e.activation(win.ap[:, kc, :], wst.ap[:], AF.Copy), r=[wst], w=[win])
                for j in range(2):
                    S.op('dve', lambda e, j=j: e.tensor_copy(
                        wswk.ap[:, :, j * 128:(j + 1) * 128].rearrange("p k (a b) -> p k a b", a=2),
                        win.ap[:, :, C_SWK + j * 64:C_SWK + (j + 1) * 64].unsqueeze(2).broadcast_to([128, 8, 2, 64])), r=[win], w=[wswk])
                S.op('dve', lambda e: e.memset(wkr.ap[:], 0.0), w=[wkr])
                S.op('dve', lambda e: e.memset(wkrr.ap[:], 0.0), w=[wkrr])
                S.op('dve', lambda e: e.tensor_copy(wkr.ap[:, :, 64:96], win.ap[:, :, C_KR:C_KR + 32]), r=[win], w=[wkr])
                S.op('dve', lambda e: e.tensor_scalar(wkrr.ap[:, :, 64:80], win.ap[:, :, C_KR + 16:C_KR + 32], -1.0, None, ALU.mult), r=[win], w=[wkrr])
                S.op('dve', lambda e: e.tensor_copy(wkrr.ap[:, :, 80:96], win.ap[:, :, C_KR:C_KR + 16]), r=[win], w=[wkrr])
                S.dma('sp', sst, sst.ap[:], D['mla_w_uq'][l].rearrange("(c p) n -> p c n", p=128))
                S.op('dve', lambda e: e.tensor_copy(wuq.ap[:], sst.ap[:]), r=[sst], w=[wuq])
                S.op('dve', lambda e: e.memset(wuqr.ap[:], 0.0), w=[wuqr])
                wq4 = wuq.ap[:].rearrange("p c (h d) -> p c h d", h=4)
                wr4 = wuqr.ap[:].rearrange("p c (h d) -> p c h d", h=4)
                S.op('dve', lambda e: e.tensor_scalar(wr4[:, :, :, 64:80], wq4[:, :, :, 80:96], -1.0, None, ALU.mult), r=[wuq], w=[wuqr])
                S.op('dve', lambda e: e.tensor_copy(wr4[:, :, :, 80:96], wq4[:, :, :, 64:80]), r=[wuq], w=[wuqr])
                S.dma('sp', sst, sst.ap[:, 0, 0:256], D['mla_w_uk'][l])
                S.op('dve', lambda e: e.tensor_copy(wuk.ap[:], sst.ap[:, 0, 0:256]), r=[sst], w=[wuk])
                S.dma('sp', sst, sst.ap[:, 0, 0:256], D['mla_w_uv'][l])
                S.op('dve', lambda e: e.tensor_copy(wuv.ap[:], sst.ap[:, 0, 0:256]), r=[sst], w=[wuv])
                S.dma('sp', gq, gq.ap[:], D['mla_q_norm'][l].rearrange("(c p) -> p c", p=128), allow_slow_non_contiguous=True)
                S.dma('sp', gkv, gkv.ap[:], D['mla_kv_norm'][l].rearrange("(p o) -> p o", o=1))
                S.dma('sp', invf, invf.ap[:], D['c_invf'][:, :])
                S.op('dve', lambda e: e.memset(usc.ap[:, :, 0:2], 0.0), w=[usc])
                S.op('dve', lambda e: e.memset(ucf.ap[:, :, 0:30], 0.0), w=[ucf])
                S.op('dve', lambda e: e.memset(swv.ap[:, :, :, 64:65], 1.0), w=[swv])
                S.op('dve', lambda e: e.memset(mlav.ap[:, :, :, 64:65], 1.0), w=[mlav])

                def proj(lhs, M, nk, rhs, rbufs):
                    pb = nextps()
                    for kc in range(nk):
                        S.op('pe', lambda e, kc=kc: e.matmul(pb.ap[0:M, 0:CH], lhs(kc), rhs(kc), start=(kc == 0), stop=(kc == nk - 1)),
                             r=rbufs, w=[pb])
                    return pb

                def range_sin(src, dst):
                    S.op('dve', lambda e: e.tensor_scalar(tm.ap[R, :], src.ap[R, :], 1.0 / TWO_PI, None, ALU.mult), r=[src], w=[tm])
                    S.op('dve', lambda e: e.tensor_copy(ki.ap[R, :], tm.ap[R, :]), r=[tm], w=[ki])
                    S.op('dve', lambda e: e.tensor_copy(kf.ap[R, :], ki.ap[R, :]), r=[ki], w=[kf])
                    S.op('dve', lambda e: e.scalar_tensor_tensor(tm.ap[R, :], kf.ap[R, :], -CW1, src.ap[R, :], ALU.mult, ALU.add), r=[kf, src], w=[tm])
                    S.op('dve', lambda e: e.scalar_tensor_tensor(tm.ap[R, :], kf.ap[R, :], -CW2, tm.ap[R, :], ALU.mult, ALU.add), r=[kf, tm], w=[tm])
                    S.op('dve', lambda e: e.tensor_scalar(kf.ap[R, :], tm.ap[R, :], float(np.pi), float(TWO_PI), ALU.is_gt, ALU.mult), r=[tm], w=[kf])
                    S.op('dve', lambda e: e.tensor_tensor(tm.ap[R, :], tm.ap[R, :], kf.ap[R, :], ALU.subtract), r=[tm, kf], w=[tm])
                    S.op('dve', lambda e: e.tensor_scalar(kf.ap[R, :], tm.ap[R, :], float(-np.pi), float(TWO_PI), ALU.is_lt, ALU.mult), r=[tm], w=[kf])
                    S.op('dve', lambda e: e.tensor_tensor(tm.ap[R, :], tm.ap[R, :], kf.ap[R, :], ALU.add), r=[tm, kf], w=[tm])
                    S.op('act', lambda e: e.activation(dst.ap[R, :], tm.ap[R, :], AF.Sin), r=[tm], w=[dst])

                xrhs = lambda kc: xTc.ap[:, kc, :]
                for c in range(0 if MIXDBG[0] == 10 else SEQ // CH):
                    cs = slice(c * CH, (c + 1) * CH)
                    for j in range(CH // 128):
                        t = s * 16 + c * (CH // 128) + j
                        S.dma('sp', xs, xs.ap[:], x_dram[t * 128:(t + 1) * 128, :], reads=[x_res])
                        S.op('act', lambda e: e.activation(xb.ap[:], xs.ap[:], AF.Copy), r=[xs], w=[xb])
                        for kc in range(8):
                            S.op('pe', lambda e, kc=kc: e.transpose(P.T.ap[:, kc * 128:(kc + 1) * 128], xb.ap[:, kc * 128:(kc + 1) * 128], identb.ap[:]),
                                 r=[xb, identb], w=[P.T])
                        S.op('dve', lambda e, j=j: e.tensor_copy(xTc.ap[:, :, j * 128:(j + 1) * 128], P.T.ap[:].rearrange("p (a b) -> p a b", a=8)),
                             r=[P.T], w=[xTc])
                    S.dma('sp', posi, posi.ap[R, :], D['positions'][s, c * CH:(c + 1) * CH].partition_broadcast(32))
                    S.op('dve', lambda e: e.tensor_copy(ang.ap[R, :], posi.ap[R, :]), r=[posi], w=[ang])
                    S.op('dve', lambda e: e.tensor_scalar(ang.ap[R, :], ang.ap[R, :], invf.ap[R, 0:1], None, ALU.mult), r=[ang, invf], w=[ang])
                    S.op('dve', lambda e: e.tensor_scalar(ang2.ap[R, :], ang.ap[R, :], float(np.pi / 2), None, ALU.add), r=[ang], w=[ang2])
                    range_sin(ang, sinb)
                    range_sin(ang2, cosb)
                    if MIXDBG[0] == 11:
                        continue
                    for cc in range(2):
                        pb = proj(lambda kc, cc=cc: win.ap[:, kc, C_SCC + cc * 128:C_SCC + (cc + 1) * 128], 128, 8, xrhs, [win, xTc])
                        S.op('act', lambda e, pb=pb, cc=cc: e.activation(tA[cc].ap[:], pb.ap[:, 0:CH], AF.Copy), r=[pb], w=[tA[cc]])
                        pb2 = proj(lambda kc, cc=cc: win.ap[:, kc, C_SCH + cc * 128:C_SCH + (cc + 1) * 128], 128, 8, xrhs, [win, xTc])
                        S.op('dve', lambda e, pb2=pb2, cc=cc: e.tensor_tensor(usc.ap[:, cc, 2 + c * CH:2 + (c + 1) * CH], pb2.ap[:, 0:CH], tA[cc].ap[:], ALU.mult),
                             r=[pb2, tA[cc]], w=[usc])
                    for cc in range(2):
                        pb = proj(lambda kc, cc=cc: win.ap[:, kc, C_CFG + cc * 128:C_CFG + (cc + 1) * 128], 128, 8, xrhs, [win, xTc])
                        S.op('act', lambda e, pb=pb, cc=cc: e.activation(tA[cc].ap[:], pb.ap[:, 0:CH], AF.Sigmoid), r=[pb], w=[tA[cc]])
                        pb2 = proj(lambda kc, cc=cc: win.ap[:, kc, C_CFA + cc * 128:C_CFA + (cc + 1) * 128], 128, 8, xrhs, [win, xTc])
                        S.op('dve', lambda e, pb2=pb2, cc=cc: e.tensor_tensor(ucf.ap[:, cc, 30 + c * CH:30 + (c + 1) * CH], pb2.ap[:, 0:CH], tA[cc].ap[:], ALU.mult),
                             r=[pb2, tA[cc]], w=[ucf])
                    if MIXDBG[0] == 12:
                        continue
                    for j in range(2):
                        pb = proj(lambda kc, j=j: win.ap[:, kc, C_SWQ + j * 128:C_SWQ + (j + 1) * 128], 128, 8, xrhs, [win, xTc])
                        S.op('act', lambda e, pb=pb, j=j: e.activation(swq.ap[:, j, cs], pb.ap[:, 0:CH], AF.Copy, scale=0.125), r=[pb], w=[swq])
                        pb2 = proj(lambda kc, j=j: wswk.ap[:, kc, j * 128:(j + 1) * 128], 128, 8, xrhs, [wswk, xTc])
                        S.op('dve', lambda e, pb2=pb2, j=j: e.tensor_copy(swk.ap[:, j, cs], pb2.ap[:, 0:CH]), r=[pb2], w=[swk])
                    if MIXDBG[0] == 13:
                        continue
                    for cc in range(2):
                        pb = proj(lambda kc, cc=cc: win.ap[:, kc, C_CQ + cc * 128:C_CQ + (cc + 1) * 128], 128, 8, xrhs, [win, xTc])
                        S.op('act', lambda e, pb=pb, cc=cc: e.activation(cqf.ap[:, cc, :], pb.ap[:, 0:CH], AF.Copy), r=[pb], w=[cqf])
                        S.op('dve', lambda e, pb=pb, cc=cc: e.tensor_tensor(cqs.ap[:, cc, :], pb.ap[:, 0:CH], cqf.ap[:, cc, :], ALU.mult), r=[pb, cqf], w=[cqs])
                    pb = proj(lambda cc: onesb.ap[:], 128, 2, lambda cc: cqs.ap[:, cc, :], [onesb, cqs])
                    S.op('act', lambda e, pb=pb: e.activation(rq.ap[:], pb.ap[:, 0:CH], AF.Sqrt, bias=EPS_AP[0].ap[:, 0:1], scale=1.0 / 256), r=[pb], w=[rq])
                    S.op('dve', lambda e: e.reciprocal(rq.ap[:], rq.ap[:]), r=[rq], w=[rq])
                    for cc in range(2):
                        S.op('dve', lambda e, cc=cc: e.scalar_tensor_tensor(cqn.ap[:, cc, :], cqf.ap[:, cc, :], gq.ap[:, cc:cc + 1], rq.ap[:], ALU.mult, ALU.mult),
                             r=[cqf, gq, rq], w=[cqn])
                    pb = proj(lambda kc: win.ap[:, kc, C_CKV:C_CKV + 128], 128, 8, xrhs, [win, xTc])
                    S.op('act', lambda e, pb=pb: e.activation(ckvf.ap[:], pb.ap[:, 0:CH], AF.Copy), r=[pb], w=[ckvf])
                    S.op('dve', lambda e, pb=pb: e.tensor_tensor(ckvs.ap[:], pb.ap[:, 0:CH], ckvf.ap[:], ALU.mult), r=[pb, ckvf], w=[ckvs])
                    pb = proj(lambda cc: onesb.ap[:], 128, 1, lambda cc: ckvs.ap[:], [onesb, ckvs])
                    S.op('act', lambda e, pb=pb: e.activation(rkv.ap[:], pb.ap[:, 0:CH], AF.Sqrt, bias=EPS_AP[0].ap[:, 0:1], scale=1.0 / 128), r=[pb], w=[rkv])
                    S.op('dve', lambda e: e.reciprocal(rkv.ap[:], rkv.ap[:]), r=[rkv], w=[rkv])
                    S.op('dve', lambda e: e.scalar_tensor_tensor(ckvn.ap[:], ckvf.ap[:], gkv.ap[:, 0:1], rkv.ap[:], ALU.mult, ALU.mult), r=[ckvf, gkv, rkv], w=[ckvn])
                    if MIXDBG[0] == 14:
                        continue
                    for h in range(4):
                        pb = proj(lambda cc, h=h: wuq.ap[:, cc, h * 96:(h + 1) * 96], 96, 2, lambda cc: cqn.ap[:, cc, :], [wuq, cqn])
                        pbr = proj(lambda cc, h=h: wuqr.ap[:, cc, h * 96:(h + 1) * 96], 96, 2, lambda cc: cqn.ap[:, cc, :], [wuqr, cqn])
                        S.op('act', lambda e, pb=pb, h=h: e.activation(q96[h].ap[0:64, cs], pb.ap[0:64, 0:CH], AF.Copy, scale=QSCALE), r=[pb], w=[q96[h]])
                        S.op('dve', lambda e, pb=pb: e.scalar_tensor_tensor(t1.ap[R, :], pb.ap[R, 0:CH], QSCALE, cosb.ap[R, :], ALU.mult, ALU.mult), r=[pb, cosb], w=[t1])
                        S.op('dve', lambda e, pbr=pbr: e.scalar_tensor_tensor(t2.ap[R, :], pbr.ap[R, 0:CH], QSCALE, sinb.ap[R, :], ALU.mult, ALU.mult), r=[pbr, sinb], w=[t2])
                        S.op('dve', lambda e, h=h: e.tensor_tensor(q96[h].ap[R, cs], t1.ap[R, :], t2.ap[R, :], ALU.add), r=[t1, t2], w=[q96[h]])
                    if MIXDBG[0] == 15:
                        continue
                    for h in range(4):
                        pb = proj(lambda kc, h=h: wuk.ap[:, h * 64:(h + 1) * 64], 64, 1, lambda kc: ckvn.ap[:], [wuk, ckvn])
                        S.op('act', lambda e, pb=pb, h=h: e.activation(k96[h].ap[0:64, cs], pb.ap[0:64, 0:CH], AF.Copy), r=[pb], w=[k96[h]])
                    pb = proj(lambda kc: wkr.ap[:, kc, :], 96, 8, xrhs, [wkr, xTc])
                    pbr = proj(lambda kc: wkrr.ap[:, kc, :], 96, 8, xrhs, [wkrr, xTc])
                    S.op('dve', lambda e, pb=pb: e.tensor_tensor(t1.ap[R, :], pb.ap[R, 0:CH], cosb.ap[R, :], ALU.mult), r=[pb, cosb], w=[t1])
                    S.op('dve', lambda e, pbr=pbr: e.tensor_tensor(t2.ap[R, :], pbr.ap[R, 0:CH], sinb.ap[R, :], ALU.mult), r=[pbr, sinb], w=[t2])
                    S.op('dve', lambda e: e.tensor_tensor(t1.ap[R, :], t1.ap[R, :], t2.ap[R, :], ALU.add), r=[t1, t2], w=[t1])
                    for h in range(4):
                        S.op('pool' if h % 2 else 'dve', lambda e, h=h: e.tensor_copy(k96[h].ap[R, cs], t1.ap[R, :]), r=[t1], w=[k96[h]])
                    if MIXDBG[0] == 16:
                        continue
                    for j in range(CH // 128):
                        tq = c * (CH // 128) + j
                        pb = nextps()
                        for kc in range(0 if MIXDBG[0] == 18 else 8):
                            S.op('pe', lambda e, kc=kc, j=j, pb=pb: e.matmul(pb.ap[:, 0:256], xTc.ap[:, kc, j * 128:(j + 1) * 128], win.ap[:, kc, C_SCB:C_SCB + 256],
                                                                             start=(kc == 0), stop=(kc == 7)), r=[xTc, win], w=[pb])
                        for kc in range(0 if MIXDBG[0] == 18 else 8):
                            S.op('pe', lambda e, kc=kc, j=j, pb=pb: e.matmul(pb.ap[:, 256:384], xTc.ap[:, kc, j * 128:(j + 1) * 128], win.ap[:, kc, C_SWV:C_SWV + 128],
                                                                             start=(kc == 0), stop=(kc == 7)), r=[xTc, win], w=[pb])
                        S.op('act', lambda e, pb=pb, tq=tq: e.activation(scb.ap[:, tq, :], pb.ap[:, 0:256], AF.Copy), r=[pb], w=[scb])
                        S.op('dve', lambda e, pb=pb, tq=tq: e.tensor_copy(swv.ap[:, tq, :, 0:64], pb.ap[:, 256:384].rearrange("p (a b) -> p a b", a=2)), r=[pb], w=[swv])
                        pb2 = nextps()
                        if MIXDBG[0] == 17:
                            continue
                        S.op('pe', lambda e, j=j, pb2=pb2: e.matmul(pb2.ap[:, 0:256], ckvn.ap[:, j * 128:(j + 1) * 128], wuv.ap[:], start=True, stop=True),
                             r=[ckvn, wuv], w=[pb2])
                        S.op('dve', lambda e, pb2=pb2, tq=tq: e.tensor_copy(mlav.ap[:, tq, :, 0:64], pb2.ap[:, 0:256].rearrange("p (a b) -> p a b", a=4)), r=[pb2], w=[mlav])
                S.barrier(dmab)
            if MIXDBG[0] == 1:
                continue
            with ExitStack() as es2:
                S.es = es2
                wout = S.sb("wout", [128, 8, 1024], BF16)
                wst2 = S.sb("wst2", [128, 1024], F32)
                scw = S.sb("scw", [128, 2, 3], F32)
                cfw = S.sb("cfw", [128, 2, 31], F32)
                dsc = S.sb("dsc", [128, 2, 3, 128], BF16)
                dcf = S.sb("dcf", [128, 2, 31, 128], BF16)
                ln1g = S.sb("ln1g", [128, 1024], F32)
                ln1b = S.sb("ln1b", [128, 1024], F32)
                mixg = S.sb("mixg", [128, 1024], F32)
                cfb = S.sb("cfb", [128, 256], F32)
                cfg_ = S.sb("cfg_", [128, 256], F32)
                cfbe = S.sb("cfbe", [128, 256], F32)
                alibi = S.sb("alibi", [128, 4, 2, 128], F32)
                trif = S.sb("trif", [128, 128], F32)
                trib = S.sb("trib", [128, 128], BF16)
                esink = S.sb("esink", [128, 4], F32)
                xs2 = [S.sb("xs2_%d" % i, [128, 1024], F32) for i in range(2)]
                mix = S.sb("mix", [128, 1024], F32)
                mix2 = S.sb("mix2", [128, 1024], F32)
                sqt = S.sb("sqt", [128, 1024], F32)
                mixb = S.sb("mixb", [128, 1024], BF16)
                mixT = S.sb("mixT", [128, 8, 128], BF16)
                yb = [S.sb("y%d" % i, [128, 1024], F32) for i in range(2)]
                PT = [S.sb("PT%d" % i, [128, 512], BF16) for i in range(3)]
                tS = [S.sb("tS%d" % i, [128, 256], F32) for i in range(2)]
                cft = S.sb("cft", [128, 256], F32)
                stats = S.sb("stats", [128, 2, 6], F32)
                aggr = S.sb("aggr", [128, 2], F32)
                rstd = S.sb("rstd", [128, 1], F32)
                st1 = S.sb("st1", [128, 6], F32)
                ag1 = S.sb("ag1", [128, 2], F32)
                rs1 = S.sb("rs1", [128, 1], F32)
                ssq = S.sb("ssq", [128, 16], F32)
                rec4 = S.sb("rec4", [128, 4], F32)
                rec4b = S.sb("rec4b", [128, 4], F32)
                dmab = [wst2, scw, cfw, ln1g, ln1b, mixg, cfb, cfg_, cfbe, alibi, trif, esink] + xs2 + yb
                for kc in range(8):
                    S.dma('sp', wst2, wst2.ap[:], D['w_out'][l, kc * 128:(kc + 1) * 128, :])
                    S.op('act', lambda e, kc=kc: e.activation(wout.ap[:, kc, :], wst2.ap[:], AF.Copy), r=[wst2], w=[wout])
                for cc in range(2):
                    S.dma('sp', scw, scw.ap[:, cc, :], D['sc_conv_w'][l][:, cc * 128:(cc + 1) * 128].rearrange("k p -> p k"), part=True, allow_slow_non_contiguous=True)
                    S.dma('sp', cfw, cfw.ap[:, cc, :], D['cf_dw_w'][l][:, cc * 128:(cc + 1) * 128].rearrange("k p -> p k"), part=True, allow_slow_non_contiguous=True)
                for cc in range(2):
                    for k in range(3):
                        S.op('dve', lambda e, cc=cc, k=k: e.tensor_scalar(dsc.ap[:, cc, k, :], identf.ap[:], scw.ap[:, cc, k:k + 1], None, ALU.mult), r=[identf, scw], w=[dsc])
                    for k in range(31):
                        S.op('dve', lambda e, cc=cc, k=k: e.tensor_scalar(dcf.ap[:, cc, k, :], identf.ap[:], cfw.ap[:, cc, k:k + 1], None, ALU.mult), r=[identf, cfw], w=[dcf])
                for (bf, key) in ((ln1g, 'ln1_g'), (ln1b, 'ln1_b'), (mixg, 'mix_norm_g'), (cfb, 'cf_dw_b'), (cfg_, 'cf_ln_g'), (cfbe, 'cf_ln_b'), (esink, 'swa_sinks')):
                    S.dma('sp', bf, bf.ap[:], D[key][l].partition_broadcast(128))
                S.op('act', lambda e: e.activation(esink.ap[:], esink.ap[:], AF.Exp), r=[esink], w=[esink])
                S.dma('sp', alibi, alibi.ap[:], D['c_alibi'][:, :, :, :])
                S.dma('sp', trif, trif.ap[:], D['c_tri'][:, :])
                S.op('dve', lambda e: e.tensor_copy(trib.ap[:], trif.ap[:]), r=[trif], w=[trib])
                ptcs = [0]
                mixes = [mix, mix2]

                def stageA(qb):
                    mix = mixes[qb % 2]
                    ptc = ptcs[0]
                    t = s * 16 + qb
                    qs = slice(qb * 128, (qb + 1) * 128)
                    if 'conv' in MIXSKIP:
                        S.op('dve', lambda e: e.memset(mix.ap[:], 0.0), w=[mix])
                    for cc in range(0 if 'conv' in MIXSKIP else 2):
                        for k in range(3):
                            S.op('pe', lambda e, cc=cc, k=k: e.matmul(P.X.ap[:, cc * 128:(cc + 1) * 128], usc.ap[:, cc, qb * 128 + k:qb * 128 + k + 128], dsc.ap[:, cc, k, :],
                                                                      start=(k == 0), stop=(k == 2)), r=[usc, dsc], w=[P.X])
                    for cc in range(0 if 'conv' in MIXSKIP else 2):
                        for k in range(31):
                            S.op('pe', lambda e, cc=cc, k=k: e.matmul(P.X.ap[:, 256 + cc * 128:256 + (cc + 1) * 128], ucf.ap[:, cc, qb * 128 + k:qb * 128 + k + 128], dcf.ap[:, cc, k, :],
                                                                      start=(k == 0), stop=(k == 30)), r=[ucf, dcf], w=[P.X])
                    if 'conv' not in MIXSKIP:
                      S.op('dve', lambda e: e.tensor_tensor(mix.ap[:, 0:256], P.X.ap[:, 0:256], scb.ap[:, qb, :], ALU.mult), r=[P.X, scb], w=[mix])
                    S.op('dve', lambda e: e.tensor_tensor(cft.ap[:], P.X.ap[:, 256:512], cfb.ap[:], ALU.add), r=[P.X, cfb], w=[cft])
                    S.op('dve', lambda e: e.bn_stats(st1.ap[:], cft.ap[:]), r=[cft], w=[st1])
                    S.op('dve', lambda e: e.bn_aggr(ag1.ap[:], st1.ap[:]), r=[st1], w=[ag1])
                    S.op('act', lambda e: e.activation(rs1.ap[:], ag1.ap[:, 1:2], AF.Sqrt, bias=EPS_AP[0].ap[:, 0:1], scale=1.0), r=[ag1], w=[rs1])
                    S.op('dve', lambda e: e.reciprocal(rs1.ap[:], rs1.ap[:]), r=[rs1], w=[rs1])
                    S.op('dve', lambda e: e.tensor_scalar(cft.ap[:], cft.ap[:], ag1.ap[:, 0:1], rs1.ap[:, 0:1], ALU.subtract, ALU.mult), r=[cft, ag1, rs1], w=[cft])
                    S.op('dve', lambda e: e.tensor_tensor(cft.ap[:], cft.ap[:], cfg_.ap[:], ALU.mult), r=[cft, cfg_], w=[cft])
                    S.op('dve', lambda e: e.tensor_tensor(cft.ap[:], cft.ap[:], cfbe.ap[:], ALU.add), r=[cft, cfbe], w=[cft])
                    S.op('act', lambda e: e.activation(mix.ap[:, 512:768], cft.ap[:], AF.Silu), r=[cft], w=[mix])
                    oP, oS = P.G[0], P.G[1]
                    nk = qb + 1
                    items = []
                    if 'mla' not in MIXSKIP:
                        for h in range(4):
                            for g0 in range(0, nk, 4):
                                items.append(('mla', h, list(range(g0, min(g0 + 4, nk)))))
                    if 'swa' not in MIXSKIP:
                        for h in range(4):
                            items.append(('swa', h, [kt for kt in (qb - 1, qb) if kt >= 0]))

                    def emit_front(it, k):
                        kind, h, kts = it
                        pb, pt = P.A[k % 2], PT[k % 3]
                        n = len(kts) * 128
                        if kind == 'mla':
                            for gi, kt in enumerate(kts):
                                S.op('pe', lambda e, gi=gi, kt=kt: e.matmul(pb.ap[:, gi * 128:(gi + 1) * 128], k96[h].ap[:, kt * 128:(kt + 1) * 128], q96[h].ap[:, qs],
                                                                           start=True, stop=True), r=[k96[h], q96[h]], w=[pb])
                            S.op('act', lambda e: e.activation(pt.ap[:, 0:n], pb.ap[:, 0:n], AF.Exp), r=[pb], w=[pt])
                            if kts[-1] == qb:
                                gi = len(kts) - 1
                                S.op('pool', lambda e: e.tensor_tensor(pt.ap[:, gi * 128:(gi + 1) * 128], pt.ap[:, gi * 128:(gi + 1) * 128], trib.ap[:], ALU.mult),
                                     r=[pt, trib], w=[pt])
                        else:
                            j, s_ = h // 2, h % 2
                            pr = slice(64 * s_, 64 * s_ + 64)
                            ts_ = tS[k % 2]
                            for gi, kt in enumerate(kts):
                                S.op('pe', lambda e, gi=gi, kt=kt: e.matmul(pb.ap[:, gi * 128:(gi + 1) * 128], swk.ap[pr, j, kt * 128:(kt + 1) * 128], swq.ap[pr, j, qs],
                                                                           start=True, stop=True), r=[swk, swq], w=[pb])
                            bias_ap = alibi.ap[:, h, :, :].rearrange("p a b -> p (a b)") if len(kts) == 2 else alibi.ap[:, h, 1, :]
                            S.op('dve', lambda e: e.tensor_tensor(ts_.ap[:, 0:n], pb.ap[:, 0:n], bias_ap, ALU.add), r=[pb, alibi], w=[ts_])
                            S.op('act', lambda e: e.activation(pt.ap[:, 0:n], ts_.ap[:, 0:n], AF.Exp), r=[ts_], w=[pt])

                    def emit_pv(it, k):
                        kind, h, kts = it
                        pt = PT[k % 3]
                        if kind == 'mla':
                            for gi, kt in enumerate(kts):
                                S.op('pe', lambda e, gi=gi, kt=kt: e.matmul(oP.ap[:, h * 65:(h + 1) * 65], pt.ap[:, gi * 128:(gi + 1) * 128], mlav.ap[:, kt, h, 0:65],
                                                                           start=(kt == 0), stop=(kt == nk - 1)), r=[pt, mlav], w=[oP])
                        else:
                            j = h // 2
                            for gi, kt in enumerate(kts):
                                S.op('pe', lambda e, gi=gi, kt=kt: e.matmul(oS.ap[:, h * 65:(h + 1) * 65], pt.ap[:, gi * 128:(gi + 1) * 128], swv.ap[:, kt, j, 0:65],
                                                                           start=(gi == 0), stop=(gi == len(kts) - 1)), r=[pt, swv], w=[oS])

                    base = ptc
                    for k, it in enumerate(items):
                        emit_front(it, base + k)
                        if k >= 1:
                            emit_pv(items[k - 1], base + k - 1)
                    if items:
                        emit_pv(items[-1], base + len(items) - 1)
                    ptc = base + len(items)
                    o4 = oP.ap[:, 0:260].rearrange("p (h d) -> p h d", h=4)
                    if 'mla' not in MIXSKIP:
                        S.op('dve', lambda e: e.reciprocal(rec4.ap[:], o4[:, :, 64]), r=[oP], w=[rec4])
                        S.op('dve', lambda e: e.tensor_tensor(mix.ap[:, 256:512].rearrange("p (h d) -> p h d", h=4), o4[:, :, 0:64],
                                                              rec4.ap[:].unsqueeze(2).broadcast_to([128, 4, 64]), ALU.mult), r=[oP, rec4], w=[mix])
                    s4 = oS.ap[:, 0:260].rearrange("p (h d) -> p h d", h=4)
                    if 'swa' not in MIXSKIP:
                        S.op('dve', lambda e: e.tensor_tensor(rec4b.ap[:], s4[:, :, 64], esink.ap[:], ALU.add), r=[oS, esink], w=[rec4b])
                        S.op('dve', lambda e: e.reciprocal(rec4b.ap[:], rec4b.ap[:]), r=[rec4b], w=[rec4b])
                        S.op('dve', lambda e: e.tensor_tensor(mix.ap[:, 768:1024].rearrange("p (h d) -> p h d", h=4), s4[:, :, 0:64],
                                                              rec4b.ap[:].unsqueeze(2).broadcast_to([128, 4, 64]), ALU.mult), r=[oS, rec4b], w=[mix])
                    ptcs[0] = ptc

                def stageB(qb):
                    mix = mixes[qb % 2]
                    t = s * 16 + qb
                    S.op('act', lambda e: e.activation(sqt.ap[:], mix.ap[:], AF.Square), r=[mix], w=[sqt])
                    S.op('dve', lambda e: e.tensor_reduce(ssq.ap[:], sqt.ap[:].rearrange("p (a b) -> p a b", a=16), axis=AX.X, op=ALU.add), r=[sqt], w=[ssq])
                    S.op('act', lambda e: e.activation(ssq.ap[:], ssq.ap[:], AF.Sqrt, bias=EPS_AP[0].ap[:, 0:1], scale=1.0 / 64), r=[ssq], w=[ssq])
                    S.op('dve', lambda e: e.reciprocal(ssq.ap[:], ssq.ap[:]), r=[ssq], w=[ssq])
                    S.op('dve', lambda e: e.tensor_tensor(sqt.ap[:].rearrange("p (a b) -> p a b", a=16), mix.ap[:].rearrange("p (a b) -> p a b", a=16),
                                                          ssq.ap[:].unsqueeze(2).broadcast_to([128, 16, 64]), ALU.mult), r=[mix, ssq], w=[sqt])
                    S.op('dve', lambda e: e.tensor_tensor(mixb.ap[:], sqt.ap[:], mixg.ap[:], ALU.mult), r=[sqt, mixg], w=[mixb])
                    for kc in range(8):
                        S.op('pe', lambda e, kc=kc: e.transpose(P.T.ap[:, kc * 128:(kc + 1) * 128], mixb.ap[:, kc * 128:(kc + 1) * 128], identb.ap[:]),
                             r=[mixb, identb], w=[P.T])
                    S.op('act', lambda e: e.activation(mixT.ap[:].rearrange("p a b -> p (a b)"), P.T.ap[:], AF.Copy), r=[P.T], w=[mixT])
                    for half in range(2):
                        for kc in range(8):
                            S.op('pe', lambda e, half=half, kc=kc: e.matmul(P.O.ap[:, half * 512:(half + 1) * 512], mixT.ap[:, kc, :], wout.ap[:, kc, half * 512:(half + 1) * 512],
                                                                           start=(kc == 0), stop=(kc == 7)), r=[mixT, wout], w=[P.O])
                    xs_, y = xs2[qb % 2], yb[qb % 2]
                    S.dma('sp', xs_, xs_.ap[:], x_dram[t * 128:(t + 1) * 128, :], reads=[x_res])
                    S.op('dve', lambda e, xs_=xs_, y=y: e.scalar_tensor_tensor(y.ap[:], xs_.ap[:], float(ALPHA), P.O.ap[:], ALU.mult, ALU.add), r=[xs_, P.O], w=[y])
                    layer_norm_tile(S, y, stats, aggr, rstd, ln1g, ln1b, y)
                    S.dma('sp', y, h_dram[t * 128:(t + 1) * 128, :], y.ap[:], store=True, writes=[h_res])

                stageA(0)
                for qb in range(16):
                    if qb + 1 < 16:
                        stageA(qb + 1)
                    stageB(qb)
                S.barrier(dmab)
        S.barrier([])


def host_consts():
    c = {}
    c['c_ident'] = np.eye(128, dtype=np.float32)
    c['c_iota'] = np.tile(np.arange(128, dtype=np.float32)[None, :], (128, 1))
    k = np.arange(128)[:, None]
    q = np.arange(128)[None, :]
    c['c_tri'] = (k <= q).astype(np.float32)
    slopes = 2.0 ** (-8.0 * np.arange(1, 5, dtype=np.float64) / 4)
    al = np.zeros((128, 4, 2, 128), dtype=np.float32)
    NEG = -30000.0
    for h in range(4):
        dprev = (q + 128 - k).astype(np.float64)
        al[:, h, 0, :] = np.where(dprev < 128, -slopes[h] * dprev, NEG)
        dcur = (q - k).astype(np.float64)
        al[:, h, 1, :] = np.where(dcur >= 0, -slopes[h] * dcur, NEG)
    c['c_alibi'] = al
    inv = (10000.0 ** (-np.arange(16, dtype=np.float32) / np.float32(16))).astype(np.float32)
    f = np.zeros((96, 1), dtype=np.float32)
    f[64:96, 0] = np.concatenate([inv, inv])
    c['c_invf'] = f
    return c


CONST_SHAPES = {'c_ident': [128, 128], 'c_iota': [128, 128], 'c_tri': [128, 128], 'c_alibi': [128, 4, 2, 128], 'c_invf': [96, 1]}


def build_mixer_test(nseq=1):
    nc = bass.Bass("TRN2", target_bir_lowering=False)
    D = {}
    ntok = nseq * SEQ
    D['x'] = nc.dram_tensor("x", [ntok, 1024], F32, kind="ExternalInput").ap()
    D['positions'] = nc.dram_tensor("positions", [nseq, SEQ], I32, kind="ExternalInput").ap()
    for k, shp in CONST_SHAPES.items():
        D[k] = nc.dram_tensor(k, shp, F32, kind="ExternalInput").ap()
    for k in WSHAPES:
        if k.startswith("peer") or k.startswith("ln2"):
            continue
        D[k] = nc.dram_tensor(k, list(WSHAPES[k]), F32, kind="ExternalInput").ap()
    out = nc.dram_tensor("out", [ntok, 1024], F32, kind="ExternalOutput").ap()
    with ExitStack() as es:
        S = Sched(nc, es)
        P = PS(S)
        C = make_consts(S, nc, D)
        mixer_phase(S, nc, P, C, 0, D['x'], S.dram("xr"), out, S.dram("outr"), D, nseq=nseq)
        S.barrier([])
        print("instr counts", S.ninstr, "sems", S.nsem)
    return nc


def build_full(depth=DEPTH):
    nc = bass.Bass("TRN2", target_bir_lowering=False)
    D = {}
    D['x'] = nc.dram_tensor("x", [NTOK, 1024], F32, kind="ExternalInput").ap()
    D['positions'] = nc.dram_tensor("positions", [NSEQ, SEQ], I32, kind="ExternalInput").ap()
    for k, shp in CONST_SHAPES.items():
        D[k] = nc.dram_tensor(k, shp, F32, kind="ExternalInput").ap()
    for k in WSHAPES:
        D[k] = nc.dram_tensor(k, list(WSHAPES[k]), F32, kind="ExternalInput").ap()
    out = nc.dram_tensor("out", [NTOK, 1024], F32, kind="ExternalOutput").ap()
    hbuf = nc.dram_tensor("s_h", [NTOK, 1024], F32).ap()
    xbuf = nc.dram_tensor("s_x", [NTOK, 1024], F32).ap()
    D['scr'] = declare_scratch(nc, NTOK)
    with ExitStack() as es:
        S = Sched(nc, es)
        P = PS(S)
        C = make_consts(S, nc, D)
        x_res, h_res, xb_res, o_res = S.dram("x_res"), S.dram("h_res"), S.dram("xb_res"), S.dram("o_res")
        for l in range(depth):
            xin, xin_r = (D['x'], x_res) if l == 0 else (xbuf, xb_res)
            last = (l == depth - 1)
            xo, xo_r = (out, o_res) if last else (xbuf, xb_res)
            mixer_phase(S, nc, P, C, l, xin, xin_r, hbuf, h_res, D)
            if l == 0:
                peer_prepass(S, nc, P, C, l, D)
            PEER_FN[0](S, nc, P, C, l, hbuf, h_res, xo, xo_r, D, last, next_pre=(not last))
        S.finish()
        print("instr counts", S.ninstr, "sems", S.nsem, flush=True)
    return nc


_NC_CACHE = {}


def kernel(**inputs):
    n = 8
    x = np.ascontiguousarray(np.asarray(inputs['x'], dtype=np.float32))
    pos = np.ascontiguousarray(np.asarray(inputs['positions'], dtype=np.int32))
    consts = host_consts()
    wts = {k: np.ascontiguousarray(np.asarray(inputs[k], dtype=np.float32)) for k in WSHAPES}
    if 'nc' not in _NC_CACHE:
        _NC_CACHE['nc'] = build_full()
    nc = _NC_CACHE['nc']
    in_maps = []
    for c in range(n):
        m = {'x': x[c * NSEQ:(c + 1) * NSEQ].reshape(NTOK, 1024), 'positions': pos[c * NSEQ:(c + 1) * NSEQ]}
        m.update(consts)
        m.update(wts)
        in_maps.append(m)
    res = run_bass_kernel_spmd(nc, in_maps, core_ids=list(range(n)))
    out = np.concatenate([r['out'].reshape(NSEQ, SEQ, 1024) for r in res.results], axis=0)
    return out.astype(np.float32)


U32 = mybir.dt.uint32


def peer_phase2(S, nc, P, C, l, h_dram, h_res, out_dram, out_res, D, final, ntiles=NT, next_pre=False):
    identb = C['identb']
    iota = C['iota']
    iotab = C['iotab']
    scr = D['scr']
    hT_d, G_d = scr['hT'], scr['G']
    hT_r, G_r = S.dram("hT_r"), S.dram("G_r")
    with ExitStack() as es1:
        S.es = es1
        wq = S.sb("wq", [128, 8, 2048], BF16)
        wqst = [S.sb("wqst%d" % i, [128, 1024], F32) for i in range(2)]
        KT = S.sb("KT", [128, 16, 128], BF16)
        kst = [S.sb("kst%d" % i, [128, 128], F32) for i in range(2)]
        kb = [S.sb("kb%d" % i, [128, 128], BF16) for i in range(2)]
        hst = [S.sb("hst%d" % i, [128, 1024], F32) for i in range(2)]
        hb = S.sb("hb", [128, 1024], BF16)
        hTt = [S.sb("hTt%d" % i, [128, 8, 128], BF16) for i in range(2)]
        qT = [S.sb("qT%d" % i, [128, 16, 128], BF16) for i in range(2)]
        scs = [S.sb("sc%d" % i, [128, 16, 128], F32) for i in range(2)]
        mscr = S.sb("mscr", [128, 16, 128], F32)
        mscr2 = S.sb("mscr2", [128, 8, 256], F32)
        top16 = S.sb("top16", [128, 16, 16], F32)
        iu = S.sb("iu", [128, 16, 16], U32)
        idxf = S.sb("idxf", [128, 16, 16], F32)
        cand = S.sb("cand", [128, 8, 256], F32)
        best = S.sb("best", [128, 8, 16], F32)
        pu = S.sb("pu", [128, 8, 16], U32)
        posf = S.sb("posf", [128, 8, 16], F32)
        tmpf = S.sb("tmpf", [128, 8, 16], F32)
        ti = S.sb("ti", [128, 8, 16], I32)
        rif = S.sb("rif", [128, 8, 16], F32)
        rjf = S.sb("rjf", [128, 8, 16], F32)
        oh = [S.sb("oh%d" % i, [128, 8, 16, 16], BF16) for i in range(2)]
        sm = S.sb("sm", [128, 8, 8], F32)
        e16 = S.sb("e16", [128, 8, 16], F32)
        K3 = S.sb("K3", [128, 3, 128], F32)
        K3b = S.sb("K3b", [128, 3, 128], BF16)
        T3 = [S.sb("T3_%d" % i, [128, 3, 128], BF16) for i in range(2)]
        E0 = [S.sb("E0_%d" % i, [128, 8, 128], BF16) for i in range(2)]
        E1 = [S.sb("E1_%d" % i, [128, 8, 128], BF16) for i in range(2)]
        E2 = [S.sb("E2_%d" % i, [128, 8, 128], BF16) for i in range(2)]
        Gs = [S.sb("Gs%d" % i, [128, 128, 128], BF16) for i in range(2)]
        for kc2 in range(16):
            kc, hf = kc2 // 2, kc2 % 2
            st = wqst[kc2 % 2]
            S.dma('sp', st, st.ap[:], D['peer_w_q'][l, kc * 128:(kc + 1) * 128, hf * 1024:(hf + 1) * 1024])
            if kc2 % 2:
                S.op('pool', lambda e, kc=kc, hf=hf, st=st: e.tensor_copy(wq.ap[:, kc, hf * 1024:(hf + 1) * 1024], st.ap[:]), r=[st], w=[wq])
            else:
                S.op('act', lambda e, kc=kc, hf=hf, st=st: e.activation(wq.ap[:, kc, hf * 1024:(hf + 1) * 1024], st.ap[:], AF.Copy), r=[st], w=[wq])
        for hp in range(16):
            st = kst[hp % 2]
            S.dma('sp', st, st.ap[:], D['peer_sub_keys'][l, hp // 2, hp % 2, :, :])
            S.op('dve', lambda e, st=st, hp=hp: e.tensor_copy(kb[hp % 2].ap[:], st.ap[:]), r=[st], w=[kb[hp % 2]])
            S.op('pe', lambda e, hp=hp: e.transpose(P.T.ap[:, 0:128], kb[hp % 2].ap[:], identb.ap[:]), r=[kb[hp % 2], identb], w=[P.T])
            S.op('dve', lambda e, hp=hp: e.tensor_copy(KT.ap[:, hp, :], P.T.ap[:, 0:128]), r=[P.T], w=[KT])
        cnts = {'e': 0, 'p': 0}

        def front(t):
            hs, ht, qt, sc = hst[t % 2], hTt[t % 2], qT[t % 2], scs[t % 2]
            S.dma('sp', hs, hs.ap[:], h_dram[t * 128:(t + 1) * 128, :], reads=[h_res])
            S.op('act', lambda e: e.activation(hb.ap[:], hs.ap[:], AF.Copy), r=[hs], w=[hb])
            for kc in range(8):
                S.op('pe', lambda e, kc=kc: e.transpose(P.T.ap[:, kc * 128:(kc + 1) * 128], hb.ap[:, kc * 128:(kc + 1) * 128], identb.ap[:]),
                     r=[hb, identb], w=[P.T])
            S.op('act', lambda e: e.activation(ht.ap[:].rearrange("p a b -> p (a b)"), P.T.ap[:], AF.Copy), r=[P.T], w=[ht])
            S.dma('sp', ht, hT_d[:, :, t * 128:(t + 1) * 128], ht.ap[:], store=True, writes=[hT_r])
            banks = [P.G[0], P.G[1], P.X, P.G[0]]
            for g in range(4):
                pb = banks[g]
                for j in range(4):
                    hp = g * 4 + j
                    for kc in range(8):
                        S.op('pe', lambda e, pb=pb, j=j, hp=hp, kc=kc: e.matmul(
                            pb.ap[:, j * 128:(j + 1) * 128], wq.ap[:, kc, hp * 128:(hp + 1) * 128], ht.ap[:, kc, :],
                            start=(kc == 0), stop=(kc == 7)), r=[wq, ht], w=[pb])
                S.op('act', lambda e, pb=pb, g=g: e.activation(qt.ap[:, g * 4:(g + 1) * 4, :].rearrange("p a b -> p (a b)"), pb.ap[:], AF.Copy),
                     r=[pb], w=[(qt, g)])
            for g in range(4):
                pb = (P.G[1], P.X)[g % 2]
                for j in range(4):
                    hp = g * 4 + j
                    S.op('pe', lambda e, j=j, hp=hp, pb=pb: e.matmul(pb.ap[:, j * 128:(j + 1) * 128], qt.ap[:, hp, :], KT.ap[:, hp, :],
                                                                     start=True, stop=True), r=[(qt, hp // 4), KT], w=[pb])
                S.op('act', lambda e, g=g, pb=pb: e.activation(sc.ap[:, g * 4:(g + 1) * 4, :].rearrange("p a b -> p (a b)"), pb.ap[:], AF.Copy),
                     r=[pb], w=[(sc, g)])

        def topk(t):
            sc, t3 = scs[t % 2], T3[t % 2]
            for hp in range(16):
                S.op('dve', lambda e, hp=hp: e.max(top16.ap[:, hp, 0:8], sc.ap[:, hp, :]), r=[(sc, hp // 4)], w=[(top16, (hp, 0))])
            for hp in range(16):
                S.op('dve', lambda e, hp=hp: e.max_index(iu.ap[:, hp, 0:8], top16.ap[:, hp, 0:8], sc.ap[:, hp, :]), r=[(sc, hp // 4), (top16, (hp, 0))], w=[(iu, (hp, 0))])
            for hp in range(16):
                S.op('dve', lambda e, hp=hp: e.match_replace(mscr.ap[:, hp, :], top16.ap[:, hp, 0:8], sc.ap[:, hp, :], -1e30),
                     r=[(sc, hp // 4), (top16, (hp, 0))], w=[(mscr, hp)])
            for hp in range(16):
                S.op('dve', lambda e, hp=hp: e.max(top16.ap[:, hp, 8:16], mscr.ap[:, hp, :]), r=[(mscr, hp)], w=[(top16, (hp, 1))])
            for hp in range(16):
                S.op('dve', lambda e, hp=hp: e.max_index(iu.ap[:, hp, 8:16], top16.ap[:, hp, 8:16], mscr.ap[:, hp, :]), r=[(mscr, hp), (top16, (hp, 1))], w=[(iu, (hp, 1))])
            S.op('dve', lambda e: e.tensor_copy(idxf.ap[:], iu.ap[:]), r=[iu], w=[idxf])
            t4 = top16.ap[:].rearrange("p (h q) k -> p h q k", q=2)
            i4 = idxf.ap[:].rearrange("p (h q) k -> p h q k", q=2)
            S.op('dve', lambda e: e.tensor_tensor(cand.ap[:].rearrange("p h (a b) -> p h a b", a=16),
                                                  t4[:, :, 0, :].unsqueeze(3).broadcast_to([128, 8, 16, 16]),
                                                  t4[:, :, 1, :].unsqueeze(2).broadcast_to([128, 8, 16, 16]), ALU.add),
                 r=[top16], w=[cand])
            for h in range(8):
                S.op('dve', lambda e, h=h: e.max(best.ap[:, h, 0:8], cand.ap[:, h, :]), r=[cand], w=[(best, (h, 0))])
            for h in range(8):
                S.op('dve', lambda e, h=h: e.max_index(pu.ap[:, h, 0:8], best.ap[:, h, 0:8], cand.ap[:, h, :]), r=[cand, (best, (h, 0))], w=[(pu, (h, 0))])
            for h in range(8):
                S.op('dve', lambda e, h=h: e.match_replace(mscr2.ap[:, h, :], best.ap[:, h, 0:8], cand.ap[:, h, :], -1e30),
                     r=[cand, (best, (h, 0))], w=[(mscr2, h)])
            for h in range(8):
                S.op('dve', lambda e, h=h: e.max(best.ap[:, h, 8:16], mscr2.ap[:, h, :]), r=[(mscr2, h)], w=[(best, (h, 1))])
            for h in range(8):
                S.op('dve', lambda e, h=h: e.max_index(pu.ap[:, h, 8:16], best.ap[:, h, 8:16], mscr2.ap[:, h, :]), r=[(mscr2, h), (best, (h, 1))], w=[(pu, (h, 1))])
            S.op('dve', lambda e: e.tensor_copy(posf.ap[:], pu.ap[:]), r=[pu], w=[posf])
            S.op('dve', lambda e: e.tensor_tensor(e16.ap[:], best.ap[:], best.ap[:, :, 0:1].broadcast_to([128, 8, 16]), ALU.subtract),
                 r=[best], w=[e16])
            S.op('dve', lambda e: e.tensor_scalar(tmpf.ap[:], posf.ap[:], -7.5, 1.0 / 16, ALU.add, ALU.mult), r=[posf], w=[tmpf])
            S.op('act', lambda e: e.activation(e16.ap[:], e16.ap[:], AF.Exp), r=[e16], w=[e16])
            S.op('dve', lambda e: e.tensor_copy(ti.ap[:], tmpf.ap[:]), r=[tmpf], w=[ti])
            S.op('dve', lambda e: e.tensor_copy(rif.ap[:], ti.ap[:]), r=[ti], w=[rif])
            S.op('dve', lambda e: e.scalar_tensor_tensor(rjf.ap[:], rif.ap[:], -16.0, posf.ap[:], ALU.mult, ALU.add), r=[rif, posf], w=[rjf])
            io16 = iota.ap[:, 0:16].unsqueeze(1).unsqueeze(1).broadcast_to([128, 8, 16, 16])
            k3 = K3.ap[:].rearrange("p c (h k) -> p c h k", h=8)
            for q, rr in ((0, rif), (1, rjf)):
                S.op('dve', lambda e, rr=rr, q=q: e.tensor_tensor(oh[q].ap[:], rr.ap[:].unsqueeze(3).broadcast_to([128, 8, 16, 16]), io16, ALU.is_equal),
                     r=[rr, iota], w=[oh[q]])
            S.op('dve', lambda e: e.reduce_sum(sm.ap[:, 1, :], e16.ap[:], axis=AX.X), r=[e16], w=[(sm, 1)])
            for q in range(2):
                S.op('dve', lambda e, q=q: e.tensor_tensor(oh[q].ap[:], oh[q].ap[:], i4[:, :, q, :].unsqueeze(2).broadcast_to([128, 8, 16, 16]), ALU.mult),
                     r=[oh[q], idxf], w=[oh[q]])
            S.op('dve', lambda e: e.reciprocal(sm.ap[:, 2, :], sm.ap[:, 1, :]), r=[(sm, 1)], w=[(sm, 2)])
            for q in range(2):
                S.op('dve', lambda e, q=q: e.tensor_reduce(k3[:, q, :, :], oh[q].ap[:], axis=AX.X, op=ALU.add), r=[oh[q]], w=[(K3, q)])
            S.op('dve', lambda e: e.tensor_tensor(k3[:, 2, :, :], e16.ap[:], sm.ap[:, 2, :].unsqueeze(2).broadcast_to([128, 8, 16]), ALU.mult),
                 r=[e16, (sm, 2)], w=[(K3, 2)])
            S.op('dve', lambda e: e.tensor_copy(K3b.ap[:], K3.ap[:]), r=[K3], w=[K3b])
            for q in range(3):
                S.op('pe', lambda e, q=q: e.transpose(P.T.ap[:, q * 128:(q + 1) * 128], K3b.ap[:, q, :], identb.ap[:]), r=[K3b, identb], w=[P.T])
            S.op('act', lambda e: e.activation(t3.ap[:].rearrange("p a b -> p (a b)"), P.T.ap[:, 0:384], AF.Copy), r=[P.T], w=[t3])

        def onehot(t):
            t3 = T3[t % 2]
            gs = Gs[t % 2]
            NB = 8
            io_b = iotab.ap[:].unsqueeze(1).broadcast_to([128, NB, 128])
            for t0 in range(0, 128, NB):
                e0, e1, e2 = E0[cnts['e'] % 2], E1[cnts['e'] % 2], E2[cnts['e'] % 2]
                cnts['e'] += 1
                S.op('dve', lambda e, e0=e0, t0=t0: e.tensor_tensor(e0.ap[:], io_b, t3.ap[:, 0, t0:t0 + NB].unsqueeze(2).broadcast_to([128, NB, 128]), ALU.is_equal),
                     r=[iotab, t3], w=[e0])
                S.op('dve', lambda e, e1=e1, t0=t0: e.tensor_tensor(e1.ap[:], io_b, t3.ap[:, 1, t0:t0 + NB].unsqueeze(2).broadcast_to([128, NB, 128]), ALU.is_equal),
                     r=[iotab, t3], w=[e1])
                S.op('pool', lambda e, e1=e1, e2=e2, t0=t0: e.tensor_tensor(e2.ap[:], e1.ap[:], t3.ap[:, 2, t0:t0 + NB].unsqueeze(2).broadcast_to([128, NB, 128]), ALU.mult),
                     r=[e1, t3], w=[e2])
                pb = (P.O, P.O2)[cnts['p'] % 2]
                cnts['p'] += 1
                for tk in range(NB):
                    S.op('pe', lambda e, e0=e0, e2=e2, pb=pb, tk=tk: e.matmul(pb.ap[:, tk * 128:(tk + 1) * 128], e2.ap[:, tk, :], e0.ap[:, tk, :], start=True, stop=True),
                         r=[e0, e2], w=[pb])
                S.op('act', lambda e, pb=pb, gs=gs, t0=t0: e.activation(gs.ap[:, :, t0:t0 + NB], pb.ap[:].rearrange("p (t i) -> p i t", t=NB), AF.Copy),
                     r=[pb], w=[gs])
            S.dma('sp', gs, G_d[:, :, t * 128:(t + 1) * 128].rearrange("i j t -> j i t"), gs.ap[:], store=True, writes=[G_r])

        front(0)
        for t in range(ntiles):
            topk(t)
            if t + 1 < ntiles:
                front(t + 1)
            onehot(t)
        S.barrier()
    UT_d, Vb_d = scr['UT'][l % 2], scr['Vb'][l % 2]
    UT_r, Vb_r = PRE['UT_r'][l % 2], PRE['Vb_r'][l % 2]
    with ExitStack() as es2:
        S.es = es2
        TB = TBT * 128
        hTB = S.sb("hTB", [128, 8, TB], BF16)
        acc = S.sb("acc", [128, TBT, 1024], F32)
        Gb = [S.sb("Gb%d" % i, [128, TB], BF16) for i in range(4)]
        UT = [S.sb("UT%d" % i, [128, 8, 128], BF16) for i in range(4)]
        Vb = [S.sb("Vb%d" % i, [128, 1024], BF16) for i in range(12)]
        WT = [S.sb("WT%d" % i, [128, 4, TB], BF16) for i in range(3)]
        GA = [S.sb("GA%d" % i, [128, TB], BF16) for i in range(3)]
        hs2 = [S.sb("hs2_%d" % i, [128, 1024], F32) for i in range(2)]
        yb = [S.sb("yb%d" % i, [128, 1024], F32) for i in range(2)]
        gt = S.sb("gt", [128, 1024], F32)
        bt = S.sb("bt", [128, 1024], F32)
        stats = S.sb("stats", [128, 2, 6], F32)
        aggr = S.sb("aggr", [128, 2], F32)
        rstd = S.sb("rstd", [128, 1], F32)
        S.dma('sp', gt, gt.ap[:], D['ln2_g'][l].partition_broadcast(128))
        S.dma('sp', bt, bt.ap[:], D['ln2_b'][l].partition_broadcast(128))
        nblk = ntiles // TBT
        cnt = 0
        ocnt = 0
        NG = NE // 4
        pre_gen = None
        if next_pre:
            pre_gen = prepass_iter(S, P, C, l + 1, D, prepass_bufs(S), q='pool')
        pre_every = max(1, (nblk * NE) // NE)

        def vstage(eg, first):
            nonlocal ocnt
            wt = WT[eg % 3]
            for tt in range(TBT):
                ob = ocnt % 2
                ocnt += 1
                for half in range(2):
                    po = (P.O if ob == 0 else P.G[half])
                    osl = slice(half * 512, (half + 1) * 512) if ob == 0 else slice(0, 512)
                    for ei in range(4):
                        vb = Vb[(eg % 3) * 4 + ei]
                        S.op('pe', lambda e, tt=tt, half=half, ei=ei, vb=vb, wt=wt, po=po, osl=osl: e.matmul(
                            po.ap[:, osl], wt.ap[:, ei, tt * 128:(tt + 1) * 128], vb.ap[:, half * 512:(half + 1) * 512],
                            start=(ei == 0), stop=(ei == 3)), r=[wt, vb], w=[po])
                if ob == 0:
                    if first:
                        S.op('dve', lambda e, tt=tt: e.tensor_copy(acc.ap[:, tt, :], P.O.ap[:]), r=[P.O], w=[acc])
                    else:
                        S.op('dve', lambda e, tt=tt: e.tensor_tensor(acc.ap[:, tt, :], acc.ap[:, tt, :], P.O.ap[:], ALU.add), r=[P.O, acc], w=[acc])
                else:
                    for half in range(2):
                        hsl = slice(half * 512, (half + 1) * 512)
                        if first:
                            S.op('dve', lambda e, tt=tt, half=half, hsl=hsl: e.tensor_copy(acc.ap[:, tt, hsl], P.G[half].ap[:]), r=[P.G[half]], w=[acc])
                        else:
                            S.op('dve', lambda e, tt=tt, half=half, hsl=hsl: e.tensor_tensor(acc.ap[:, tt, hsl], acc.ap[:, tt, hsl], P.G[half].ap[:], ALU.add),
                                 r=[P.G[half], acc], w=[acc])

        for blk in range(nblk):
            t0 = blk * TBT
            S.dma('sp', hTB, hTB.ap[:], hT_d[:, :, t0 * 128:(t0 + TBT) * 128], reads=[hT_r])
            for eg in range(NG):
                wt = WT[eg % 3]
                for ei in range(4):
                    e_ = eg * 4 + ei
                    gb = Gb[cnt % 4]
                    ut = UT[cnt % 4]
                    vb = Vb[(eg % 3) * 4 + ei]
                    pA = P.A[cnt % 2]
                    ga = GA[cnt % 3]
                    cnt += 1
                    S.dma('sp', ut, ut.ap[:], UT_d[e_], reads=[UT_r])
                    S.dma('sp', vb, vb.ap[:], Vb_d[e_], reads=[Vb_r])
                    S.dma('sp', gb, gb.ap[:], G_d[e_, :, t0 * 128:(t0 + TBT) * 128], reads=[G_r])
                    for kc in range(8):
                        S.op('pe', lambda e, kc=kc, ut=ut, pA=pA: e.matmul(pA.ap[:], ut.ap[:, kc, :], hTB.ap[:, kc, :], start=(kc == 0), stop=(kc == 7)),
                             r=[ut, hTB], w=[pA])
                    S.op('act', lambda e, ga=ga, pA=pA: e.activation(ga.ap[:], pA.ap[:], AF.Gelu), r=[pA], w=[ga])
                    S.op('dve', lambda e, ga=ga, gb=gb, wt=wt, ei=ei: e.tensor_tensor(wt.ap[:, ei, :], gb.ap[:], ga.ap[:], ALU.mult), r=[gb, ga], w=[wt])
                    if pre_gen is not None and cnt % pre_every == 0:
                        next(pre_gen, None)
                if eg >= 1:
                    vstage(eg - 1, eg - 1 == 0)
            vstage(NG - 1, False)
            for tt in range(TBT):
                t = t0 + tt
                hs, y = hs2[tt % 2], yb[tt % 2]
                S.dma('sp', hs, hs.ap[:], h_dram[t * 128:(t + 1) * 128, :], reads=[h_res])
                S.op('dve', lambda e, hs=hs, y=y, tt=tt: e.scalar_tensor_tensor(y.ap[:], hs.ap[:], float(ALPHA), acc.ap[:, tt, :], ALU.mult, ALU.add),
                     r=[hs, acc], w=[y])
                layer_norm_tile(S, y, stats, aggr, rstd, gt, bt, y)
                S.dma('sp', y, out_dram[t * 128:(t + 1) * 128, :], y.ap[:], store=True, writes=[out_res], final=final)
        if pre_gen is not None:
            for _ in pre_gen:
                pass
        S.barrier()


PRE = {}


def prepass_bufs(S):
    return dict(
        Ust=[S.sb("Ust%d" % i, [128, 1024], F32) for i in range(2)],
        Vst=[S.sb("Vst%d" % i, [128, 1024], F32) for i in range(2)],
        Ub=[S.sb("Ub%d" % i, [128, 1024], BF16) for i in range(2)],
        UTs=[S.sb("UTs%d" % i, [128, 8, 128], BF16) for i in range(2)],
        Vbs=[S.sb("Vbs%d" % i, [128, 1024], BF16) for i in range(2)])


def prepass_iter(S, P, C, l, D, B, q='sp'):
    identb = C['identb']
    scr = D['scr']
    slot = l % 2
    UT_d, Vb_d = scr['UT'][slot], scr['Vb'][slot]
    if 'UT_r' not in PRE:
        PRE['UT_r'] = [S.dram("UT_r0"), S.dram("UT_r1")]
        PRE['Vb_r'] = [S.dram("Vb_r0"), S.dram("Vb_r1")]
    UT_r, Vb_r = PRE['UT_r'][slot], PRE['Vb_r'][slot]
    for e_ in range(NE):
        us, vs = B['Ust'][e_ % 2], B['Vst'][e_ % 2]
        ub, ut, vb = B['Ub'][e_ % 2], B['UTs'][e_ % 2], B['Vbs'][e_ % 2]
        S.dma(q, us, us.ap[:], D['peer_u'][l, e_ * 128:(e_ + 1) * 128, :])
        S.dma(q, vs, vs.ap[:], D['peer_v'][l, e_ * 128:(e_ + 1) * 128, :])
        S.op('act', lambda e, ub=ub, us=us: e.activation(ub.ap[:], us.ap[:], AF.Copy), r=[us], w=[ub])
        S.op('pool', lambda e, vb=vb, vs=vs: e.tensor_copy(vb.ap[:], vs.ap[:]), r=[vs], w=[vb])
        S.dma(q, vb, Vb_d[e_], vb.ap[:], store=True, writes=[Vb_r])
        for kc in range(8):
            S.op('pe', lambda e, kc=kc, ub=ub: e.transpose(P.T.ap[:, kc * 128:(kc + 1) * 128], ub.ap[:, kc * 128:(kc + 1) * 128], identb.ap[:]),
                 r=[ub, identb], w=[P.T])
        S.op('act', lambda e, ut=ut: e.activation(ut.ap[:].rearrange("p a b -> p (a b)"), P.T.ap[:], AF.Copy), r=[P.T], w=[ut])
        S.dma(q, ut, UT_d[e_], ut.ap[:], store=True, writes=[UT_r])
        yield e_


def peer_prepass(S, nc, P, C, l, D):
    with ExitStack() as es0:
        S.es = es0
        B = prepass_bufs(S)
        for _ in prepass_iter(S, P, C, l, D, B):
            pass
        S.barrier()


PEER_FN[0] = peer_phase2
```
